# Optimizing a Trainium2 kernel written in Bass

```python
import jax, jax.numpy as jnp
from jax import lax
import numpy as np

D_MODEL = 1024
BATCH = 8
SEQ = 2048
DEPTH = 4

CHUNK = 128
EPS = 1e-6
GM_HEADS = 4
GM_HEAD_DIM = 64
GM_WIDTH = GM_HEADS * GM_HEAD_DIM
HG_HEADS = 4
HG_HEAD_DIM = 64
HG_WIDTH = HG_HEADS * HG_HEAD_DIM
MLA_HEADS = 8
QK_NOPE = 64
QK_ROPE = 32
QK_DIM = QK_NOPE + QK_ROPE
V_DIM = 64
Q_LORA = 256
KV_LORA = 128
MLA_WIDTH = MLA_HEADS * V_DIM
ROPE_THETA = 10000.0
Q_BLOCK = 128
D_MIX = GM_WIDTH + HG_WIDTH + MLA_WIDTH
D_FF = 4 * D_MODEL
IN_SPLITS = (GM_WIDTH, GM_WIDTH, HG_WIDTH, HG_WIDTH, HG_WIDTH, HG_WIDTH, Q_LORA, KV_LORA, QK_ROPE)
D_IN = 2 * GM_WIDTH + 4 * HG_WIDTH + Q_LORA + KV_LORA + QK_ROPE

kernel_name = "hybrid_gmlp_hgrn2_mla_trunk"


def rms_norm(x, gain):
    xf = x.astype(jnp.float32)
    y = xf * lax.rsqrt(jnp.mean(xf * xf, axis=-1, keepdims=True) + EPS)
    return (y * gain.astype(jnp.float32)).astype(x.dtype)


def head_rms_norm(x, n_heads, gain):
    shp = x.shape
    xh = x.reshape(shp[:-1] + (n_heads, shp[-1] // n_heads))
    return rms_norm(xh, gain.reshape(n_heads, -1)).reshape(shp)


def rope(x, positions):
    half = QK_ROPE // 2
    inv_freq = ROPE_THETA ** (-jnp.arange(half, dtype=jnp.float32) / half)
    ang = positions.astype(jnp.float32)[:, :, None, None] * inv_freq
    cos, sin = jnp.cos(ang), jnp.sin(ang)
    xf = x.astype(jnp.float32)
    x1, x2 = xf[..., :half], xf[..., half:]
    return jnp.concatenate([x1 * cos - x2 * sin, x2 * cos + x1 * sin], axis=-1).astype(x.dtype)


def chunked_spatial_gating(u_raw, v_raw, v_gain, w_s, b_s, out_gain):
    B, S, _ = u_raw.shape
    nc = S // CHUNK
    u = jax.nn.gelu(u_raw)
    v = head_rms_norm(jax.nn.gelu(v_raw), GM_HEADS, v_gain)
    v = v.reshape(B, nc, CHUNK, GM_HEADS, GM_HEAD_DIM)
    causal = jnp.tril(jnp.ones((CHUNK, CHUNK), dtype=bool))
    w = jnp.where(causal, w_s, 0).astype(v.dtype)
    y = jnp.einsum('hts,bnshd->bnthd', w, v) + b_s.T[:, :, None].astype(v.dtype)
    out = u * y.reshape(B, S, GM_WIDTH)
    return head_rms_norm(out, GM_HEADS, out_gain)


def hgrn2(q_raw, f_raw, i_raw, g_raw, lower_bound, out_gain):
    B, S, _ = q_raw.shape
    nc = S // CHUNK
    f32 = jnp.float32
    q = jax.nn.silu(q_raw.astype(f32))
    lb = lower_bound.astype(f32)
    f = lb + (1.0 - lb) * jax.nn.sigmoid(f_raw.astype(f32))
    k = 1.0 - f
    log_f = jnp.log(f)

    def to_chunks(t):
        return t.reshape(B, nc, CHUNK, HG_HEADS, HG_HEAD_DIM).transpose(1, 0, 3, 2, 4)

    qc, kc, vc, lc = to_chunks(q), to_chunks(k), to_chunks(i_raw.astype(f32)), to_chunks(log_f)
    bc = jnp.cumsum(lc, axis=-2)
    causal = jnp.tril(jnp.ones((CHUNK, CHUNK), dtype=bool))[:, :, None]

    def step(state, inp):
        q_, k_, v_, b_ = inp
        inter = jnp.einsum('bhtk,bhkv->bhtv', q_ * jnp.exp(b_), state)
        diff = b_[:, :, :, None, :] - b_[:, :, None, :, :]
        decay = jnp.exp(jnp.where(causal, diff, -jnp.inf))
        scores = jnp.einsum('bhtk,bhtsk,bhsk->bhts', q_, decay, k_)
        intra = jnp.einsum('bhts,bhsv->bhtv', scores, v_)
        b_last = b_[:, :, -1:, :]
        new_state = (jnp.exp(b_last[:, :, 0, :, None]) * state
                     + jnp.einsum('bhsk,bhsv->bhkv', k_ * jnp.exp(b_last - b_), v_))
        return new_state, inter + intra

    s0 = jnp.zeros((B, HG_HEADS, HG_HEAD_DIM, HG_HEAD_DIM), f32)
    _, o = lax.scan(step, s0, (qc, kc, vc, bc))
    o = o.transpose(1, 0, 3, 2, 4).reshape(B, S, HG_WIDTH)
    o = head_rms_norm(o, HG_HEADS, out_gain) * jax.nn.silu(g_raw.astype(f32))
    return o.astype(q_raw.dtype)


def mla(cq_raw, ckv_raw, kpe_raw, positions, q_a_gain, w_uq, kv_a_gain, w_ukv,
        q_gain, k_gain, out_gain):
    B, S, _ = cq_raw.shape
    q = (rms_norm(cq_raw, q_a_gain) @ w_uq).reshape(B, S, MLA_HEADS, QK_DIM)
    kv = (rms_norm(ckv_raw, kv_a_gain) @ w_ukv).reshape(B, S, MLA_HEADS, QK_NOPE + V_DIM)
    k_nope, v = kv[..., :QK_NOPE], kv[..., QK_NOPE:]
    k_pe = jnp.broadcast_to(kpe_raw[:, :, None, :], (B, S, MLA_HEADS, QK_ROPE))
    k = jnp.concatenate([k_nope, k_pe], axis=-1)
    q = rms_norm(q, q_gain)
    k = rms_norm(k, k_gain)
    q = jnp.concatenate([q[..., :QK_NOPE], rope(q[..., QK_NOPE:], positions)], axis=-1)
    k = jnp.concatenate([k[..., :QK_NOPE], rope(k[..., QK_NOPE:], positions)], axis=-1)
    q = q.transpose(0, 2, 1, 3)
    k = k.transpose(0, 2, 1, 3)
    v = v.transpose(0, 2, 1, 3)
    scale = QK_DIM ** -0.5
    outs = []
    for blk in range(S // Q_BLOCK):
        lo, hi = blk * Q_BLOCK, (blk + 1) * Q_BLOCK
        s = jnp.einsum('bhqd,bhkd->bhqk', q[:, :, lo:hi], k[:, :, :hi]).astype(jnp.float32) * scale
        mask = (lo + jnp.arange(Q_BLOCK))[:, None] >= jnp.arange(hi)[None, :]
        p = jax.nn.softmax(jnp.where(mask, s, -jnp.inf), axis=-1).astype(v.dtype)
        outs.append(jnp.einsum('bhqk,bhkd->bhqd', p, v[:, :, :hi]))
    o = jnp.concatenate(outs, axis=2).transpose(0, 2, 1, 3).reshape(B, S, MLA_WIDTH)
    return head_rms_norm(o, MLA_HEADS, out_gain)


def setup_inputs(seed: int = 0) -> dict:
    key = jax.random.key(seed)
    ks = jax.random.split(key, 24)
    f32 = jnp.float32

    def nrm(k, shape, scale):
        return jax.random.normal(k, shape, f32) * scale

    def gain(k, shape):
        return 1.0 + 0.02 * jax.random.normal(k, shape, f32)

    x = jax.random.normal(ks[0], (BATCH, SEQ, D_MODEL), f32)
    offsets = jax.random.randint(ks[1], (BATCH, 1), 0, 1024, dtype=jnp.int32)
    positions = offsets + jnp.arange(SEQ, dtype=jnp.int32)[None, :]
    return {
        "x": x,
        "positions": positions,
        "norm1_gain": gain(ks[2], (DEPTH, D_MODEL)),
        "w_in": nrm(ks[3], (DEPTH, D_MODEL, D_IN), D_MODEL ** -0.5),
        "gm_v_gain": gain(ks[4], (DEPTH, GM_WIDTH)),
        "gm_w_s": nrm(ks[5], (DEPTH, GM_HEADS, CHUNK, CHUNK), CHUNK ** -0.5),
        "gm_b_s": gain(ks[6], (DEPTH, GM_HEADS, CHUNK)),
        "gm_out_gain": gain(ks[7], (DEPTH, GM_WIDTH)),
        "hg_lower_bound": nrm(ks[8], (DEPTH, HG_WIDTH), 0.1),
        "hg_out_gain": gain(ks[9], (DEPTH, HG_WIDTH)),
        "mla_q_a_gain": gain(ks[10], (DEPTH, Q_LORA)),
        "mla_w_uq": nrm(ks[11], (DEPTH, Q_LORA, MLA_HEADS * QK_DIM), Q_LORA ** -0.5),
        "mla_kv_a_gain": gain(ks[12], (DEPTH, KV_LORA)),
        "mla_w_ukv": nrm(ks[13], (DEPTH, KV_LORA, MLA_HEADS * (QK_NOPE + V_DIM)), KV_LORA ** -0.5),
        "mla_q_gain": gain(ks[14], (DEPTH, QK_DIM)),
        "mla_k_gain": gain(ks[15], (DEPTH, QK_DIM)),
        "mla_out_gain": gain(ks[16], (DEPTH, MLA_WIDTH)),
        "w_out": nrm(ks[17], (DEPTH, D_MIX, D_MODEL), (2 * D_MIX) ** -0.5),
        "norm2_gain": gain(ks[18], (DEPTH, D_MODEL)),
        "w_ff1": nrm(ks[19], (DEPTH, D_MODEL, D_FF), D_MODEL ** -0.5),
        "w_ff2": nrm(ks[20], (DEPTH, D_FF, D_MODEL), (2 * D_FF) ** -0.5),
    }


def reference(x, positions, norm1_gain, w_in, gm_v_gain, gm_w_s, gm_b_s, gm_out_gain,
              hg_lower_bound, hg_out_gain, mla_q_a_gain, mla_w_uq, mla_kv_a_gain, mla_w_ukv,
              mla_q_gain, mla_k_gain, mla_out_gain, w_out, norm2_gain, w_ff1, w_ff2):
    lb_soft = jax.nn.softmax(hg_lower_bound.astype(jnp.float32), axis=0)
    lower_bounds = jnp.cumsum(lb_soft, axis=0) - lb_soft[0]
    split_points = [int(s) for s in np.cumsum(IN_SPLITS)[:-1]]
    for l in range(DEPTH):
        h = rms_norm(x, norm1_gain[l])
        proj = h @ w_in[l]
        a_u, a_v, b_q, b_f, b_i, b_g, c_q, c_kv, c_pe = jnp.split(proj, split_points, axis=-1)
        y_a = chunked_spatial_gating(a_u, a_v, gm_v_gain[l], gm_w_s[l], gm_b_s[l], gm_out_gain[l])
        y_b = hgrn2(b_q, b_f, b_i, b_g, lower_bounds[l], hg_out_gain[l])
        y_c = mla(c_q, c_kv, c_pe, positions, mla_q_a_gain[l], mla_w_uq[l], mla_kv_a_gain[l],
                  mla_w_ukv[l], mla_q_gain[l], mla_k_gain[l], mla_out_gain[l])
        mix = jnp.concatenate([y_a, y_b, y_c], axis=-1)
        x = x + mix @ w_out[l]
        h2 = rms_norm(x, norm2_gain[l])
        x = x + jnp.square(jax.nn.relu(h2 @ w_ff1[l])) @ w_ff2[l]
    return x
```

```python
from contextlib import ExitStack
import math
import numpy as np
import concourse.bass as bass
import concourse.mybir as mybir
from concourse.bass_utils import run_bass_kernel_spmd

F32 = mybir.dt.float32
BF16 = mybir.dt.bfloat16
I32 = mybir.dt.int32
ALU = mybir.AluOpType
AF = mybir.ActivationFunctionType
AX = mybir.AxisListType
ENGS = ("pe", "act", "dve", "pool", "sp")

L = 4
S = 2048
D = 1024
NT = 16
NB = 4
EPS = 1e-6
DIN = 1952
DFF = 4096


class Op:
    __slots__ = ("eng", "fn", "deps", "signal", "semval", "slot", "dval")


class Prog:
    def __init__(self):
        self.ops = {e: [] for e in ENGS}
        self.slot_cnt = {}
        self.lastw = {}
        self.readers = {}
        self.ranges = {}
        self.touch = {}
        self._ov = {}

    @staticmethod
    def kbuf(k):
        if isinstance(k, tuple):
            n = k[0]
            if n in ("wg", "pT", "hT", "W1g", "W2g"):
                return n + str(k[1])
            if n == "Ktok":
                return "Kend_tok"
            return n
        if k == "Vx1":
            return "Vx"
        return k

    def overlaps(self, b):
        r = self._ov.get(b)
        if r is None:
            s0, e0 = self.ranges[b]
            r = [y for y, (s1, e1) in self.ranges.items() if y != b and s1 < e0 and s0 < e1]
            self._ov[b] = r
        return r

    def add(self, eng, fn, reads=(), writes=(), slot=None, war=()):
        o = Op()
        o.eng, o.fn, o.signal, o.semval, o.slot, o.dval = eng, fn, False, 0, slot, 0
        d = []
        wb = {self.kbuf(k) for k in writes}
        for b in wb:
            if b in self.ranges:
                for y in self.overlaps(b):
                    t = self.touch.get(y)
                    if t:
                        d.extend(t["last"].values())
                        d.extend(t["dma"])
        for k in list(reads) + list(writes):
            b = self.kbuf(k)
            if b in self.ranges:
                t = self.touch.setdefault(b, {"last": {}, "dma": []})
                if slot is None:
                    t["last"][eng] = o
                else:
                    t["dma"] = (t["dma"] + [o])[-16:]
        for k in war:
            w = self.lastw.get(k)
            if w is not None:
                d.append(w)
            d.extend(self.readers.get(k, ()))
        for k in reads:
            w = self.lastw.get(k)
            if w is not None:
                d.append(w)
        for k in writes:
            w = self.lastw.get(k)
            if w is not None:
                d.append(w)
            d.extend(self.readers.get(k, ()))
        for k in reads:
            self.readers.setdefault(k, []).append(o)
        for k in writes:
            self.lastw[k] = o
            self.readers[k] = []
        o.deps = d
        if slot is not None:
            self.slot_cnt[slot] = self.slot_cnt.get(slot, 0) + 1
            o.dval = 16 * self.slot_cnt[slot]
        self.ops[eng].append(o)
        return o

    def emit(self, nc):
        for e in ENGS:
            for o in self.ops[e]:
                for d in o.deps:
                    if d.slot is None and not (d.eng == "pe" and e == "pe"):
                        d.signal = True
        for e in ENGS:
            c = 0
            for o in self.ops[e]:
                if o.slot is None and o.signal:
                    c += 1
                    o.semval = c
        with ExitStack() as st:
            sems = {e: st.enter_context(nc.semaphore("s_" + e)) for e in ENGS}
            ssem = {s: st.enter_context(nc.semaphore("d_" + str(s))) for s in self.slot_cnt}
            block = st.enter_context(nc.Block())

            def run(e, eng):
                known = {}
                for o in self.ops[e]:
                    for d in o.deps:
                        if d.slot is not None:
                            key, sem, val = ("d", d.slot), ssem[d.slot], d.dval
                        else:
                            if d.eng == "pe" and e == "pe":
                                continue
                            key, sem, val = ("e", d.eng), sems[d.eng], d.semval
                        if known.get(key, 0) >= val:
                            continue
                        eng.wait_ge(sem, val)
                        known[key] = val
                    if o.fn is None:
                        continue
                    ins = o.fn(eng)
                    if o.slot is not None:
                        ins.then_inc(ssem[o.slot], 16)
                    elif o.signal:
                        ins.then_inc(sems[e], 1)

            block.tensor(lambda eng: run("pe", eng))
            block.scalar(lambda eng: run("act", eng))
            block.vector(lambda eng: run("dve", eng))
            block.gpsimd(lambda eng: run("pool", eng))
            block.sync(lambda eng: run("sp", eng))


def build(n_layers=L, debug=False, stop_at=None):
    nc = bass.Bass("TRN2", target_bir_lowering=False)
    P = Prog()

    def din(name, shape, dt=F32):
        return nc.dram_tensor(name, list(shape), dt, kind="ExternalInput").ap()

    xT_d = din("xT", [D, S])
    pos_d = din("pos", [128, NT], I32)
    invf_d = din("invf", [128, 16])
    ident_d = din("ident", [128, 128])
    mask_d = din("maskT", [128, 128])
    g1T_d = din("g1T", [L, 128, 8])
    g2T_d = din("g2T", [L, 128, 8])
    win_d = din("w_in", [L, D, DIN])
    gmvg_d = din("gmvg", [L, 128, 256])
    gmog_d = din("gmog", [L, 128, 256])
    hgog_d = din("hgog", [L, 128, 256])
    mlaog_d = din("mlaog", [L, 128, 512])
    gmwT_d = din("gmwT", [L, 128, 512])
    gmbT_d = din("gmbT", [L, 128, 4])
    hglb_d = din("hglbT", [128, 2 * L])
    qagT_d = din("qagT", [L, 128, 2])
    kvagT_d = din("kvagT", [L, 128, 1])
    wuq_d = din("wuq", [L, 256, 768])
    wukv_d = din("wukv", [L, 128, 1024])
    qg_d = din("qg", [L, 128, 96])
    kgn_d = din("kgn", [L, 128, 1])
    kgpe_d = din("kgpe", [L, 128, 32])
    wout_d = din("w_out", [L, D, D])
    wff1_d = din("w_ff1", [L, D, DFF])
    wff2_d = din("w_ff2", [L, DFF, D])
    outT_d = nc.dram_tensor("outT", [D, S], F32, kind="ExternalOutput").ap()
    if debug:
        dbg_d = nc.dram_tensor("dbg", [D, S], BF16, kind="ExternalOutput").ap()

    SB0 = 16512
    cur = [SB0]

    _mine = set()

    def sb_at(name, shape, dt, off):
        _mine.add(name)
        return nc.alloc_sbuf_tensor_at(name, list(shape), dt, offset=off)

    def nbytes(shape, dt):
        n = 1
        for s_ in shape[1:]:
            n *= s_
        return n * (4 if dt in (F32, I32) else 2)

    def bump(name, shape, dt):
        off = (cur[0] + 31) // 32 * 32
        t = sb_at(name, shape, dt, off)
        cur[0] = off + nbytes(shape, dt)
        return t

    xT = bump("xT", [128, 8, S], F32)
    actT = bump("actT", [128, 8, S], BF16)
    ident = bump("ident", [128, 128], BF16)
    ones_bf = bump("ones_bf", [128, 128], BF16)
    ones_f = bump("ones_f", [128, 128], F32)
    maskT = bump("maskT", [128, 128], F32)
    mask_bf = bump("mask_bf", [128, 128], BF16)
    cosT = bump("cosT", [128, NT, 16], F32)
    sinT = bump("sinT", [128, NT, 16], F32)
    lbv = bump("lbv", [128, 2 * L], F32)
    oml = bump("oml", [128, 2 * L], F32)
    g1T = bump("g1T", [128, 8], F32)
    g2T = bump("g2T", [128, 8], F32)
    gmvg = bump("gmvg", [128, 256], F32)
    gmog = bump("gmog", [128, 256], F32)
    hgog = bump("hgog", [128, 256], F32)
    mlaog = bump("mlaog", [128, 512], F32)
    wTm = bump("wTm", [128, 4, 128], BF16)
    gmbT = bump("gmbT", [128, 4], F32)
    qagT = bump("qagT", [128, 2], F32)
    kvagT = bump("kvagT", [128, 1], F32)
    qg = bump("qg", [128, 96], F32)
    kgn = bump("kgn", [128, 1], F32)
    kgpe = bump("kgpe", [128, 32], F32)
    t1 = bump("t1", [128, 512], F32)
    t2 = bump("t2", [128, 512], F32)
    uvg = bump("uvg", [128, 512], F32)
    sq4 = bump("sq4", [128, 4, 512], BF16)
    tmpA = bump("tmpA", [128, 512], F32)
    tmpB = bump("tmpB", [128, 512], F32)
    sm = bump("sm", [128, 16], F32)
    sm2 = bump("sm2", [128, 16], F32)
    vn = bump("vn", [128, 256], BF16)
    yb = bump("yb", [128, 256], BF16)
    sspe = bump("sspe", [128, NT], F32)
    ra = bump("ra", [128, 4, 16], F32)
    rb = bump("rb", [128, 4, 16], F32)
    rkt = bump("rkt", [128, 32], F32)
    rkb = bump("rkb", [128, 4, 32], BF16)
    krT_off = (cur[0] + 31) // 32 * 32
    krT = bump("krT", [128, S], BF16)
    ybh = sb_at("ybh", [128, 4, 128], BF16, krT_off)
    P.ranges["krT"] = (10 ** 7, 10 ** 7 + 4096)
    P.ranges["ybh"] = (10 ** 7, 10 ** 7 + 1024)
    ybh_b = sb_at("ybh_b", [128, 4, 128], BF16, krT_off + 1024)
    P.ranges["ybh_b"] = (10 ** 7 + 1024, 10 ** 7 + 2048)
    biasc = bump("biasc", [128, 8], F32)
    zeros_bf = bump("zeros_bf", [128, 272], BF16)
    AR = (cur[0] + 31) // 32 * 32
    AEND = 229376 - 256
    ASZ = (AEND - AR) // 32 * 32
    assert ASZ >= 82304 and ASZ - 16384 >= 65856, (AR, ASZ)

    def ar(name, shape, dt, off):
        assert off % 32 == 0 and off + nbytes(shape, dt) <= ASZ, (name, off, ASZ)
        P.ranges[name] = (off, off + nbytes(shape, dt))
        return sb_at(name, shape, dt, AR + off)

    gmT = ar("gmT", [128, 2, S], BF16, 0)
    posi = ar("posi", [128, NT], I32, 0)
    iscr = ar("iscr", [128, 256], I32, 1024)
    hgT = ar("hgT", [128, 2, S], BF16, 8192)
    wg = [ar("wg0", [128, 8, 512], BF16, ASZ - 16384), ar("wg1", [128, 8, 512], BF16, ASZ - 8192)]
    G_t1 = ar("G_t1", [128, 2048], F32, 16384)
    G_t2 = ar("G_t2", [128, 2048], F32, 24576)
    u4 = ar("u4", [128, 1024], F32, 32768)
    v4 = ar("v4", [128, 1024], F32, 36864)
    vn4 = ar("vn4", [128, 4, 256], BF16, 40960)
    yb4 = ar("yb4", [128, 4, 256], BF16, 43008)
    G_t3 = ar("G_t3", [128, 1024], F32, 45056)
    hgi = ar("hgi", [128, NT, 128], BF16, 16384)
    sg = ar("sg", [128, NT, 128], BF16, 20480)
    qT = ar("qT", [128, S], BF16, 24576)
    kT = ar("kT", [128, S], BF16, 28672)
    bT = ar("bT", [128, NT, 128], F32, 32768)
    dtmp = ar("dtmp", [128, 8, 128], F32, 40960)
    Etmp = ar("Etmp", [128, 8, 128], F32, 45056)
    KendT = ar("KendT", [128, S], BF16, 49152)
    Kend_tok = ar("Kend_tok", [128, NT, 128], BF16, 53248)
    QA = ar("QA", [128, 8, 128], BF16, 49152)
    QB = ar("QB", [128, 8, 64], BF16, 51200)
    KA = ar("KA", [128, 8, 128], BF16, 53248)
    KB = ar("KB", [128, 8, 128], BF16, 55296)
    Sall = ar("Sall", [128, NT + 1, 64], F32, 57344)
    Sbf = ar("Sbf", [128, NT, 64], BF16, 61696)
    eb127 = ar("eb127", [128, NT], F32, 63744)
    sT4 = ar("sT4", [128, 2, 4, 128], BF16, 63808)
    sT4_b = ar("sT4_b", [128, 2, 4, 128], BF16, 40960)
    cqnT = ar("cqnT", [128, 2, S], BF16, 16384)
    ckvnT = ar("ckvnT", [128, S], BF16, 24576)
    pT23 = [ar("pT2", [128, 512], BF16, 28672), ar("pT3", [128, 512], BF16, 29696)]
    wuq = ar("wuq", [128, 2, 768], BF16, 30720)
    wukv = ar("wukv", [128, 1024], BF16, 33792)
    rks = ar("rks", [128, NT, 8], F32, 35840)
    Vx = ar("Vx", [128, NT, 4, 65], BF16, 36352)
    QT = ar("QT", [128, 4, S], BF16, 44672)
    KT = ar("KT", [128, 4, S], BF16, 61056)
    pT = [ar("pT0", [128, 512], BF16, 77440), ar("pT1", [128, 512], BF16, 78464)] + pT23
    qtmp = ar("qtmp", [128, 4, 96], F32, 79488)
    xr = ar("xr", [128, 4, 32], F32, 81024)
    qf = ar("qf", [128, 4, 96], BF16, 81536)
    qtmp_b = ar("qtmp_b", [128, 4, 96], F32, 28672)
    xr_b = ar("xr_b", [128, 4, 32], F32, 30208)
    qf_b = ar("qf_b", [128, 4, 96], BF16, 77440)
    ra_b = ar("ra_b", [128, 4, 16], F32, 78464)
    rb_b = ar("rb_b", [128, 4, 16], F32, 78720)
    smb = ar("smb", [128, 16], F32, 78976)
    sm2b = ar("sm2b", [128, 16], F32, 79040)
    cqraw = ar("cqraw", [128, 3, 512], F32, 44672)
    Wout = ar("Wout", [128, 8, D], BF16, 16384)
    hT = [ar("hT0", [128, 4, S], BF16, 0), ar("hT1", [128, 4, S], BF16, 16384)]
    W1g = [ar("W1g0", [128, 8, 512], BF16, 32768), ar("W1g1", [128, 8, 512], BF16, 40960)]
    W2g = [ar("W2g0", [128, 4, D], BF16, 49152), ar("W2g1", [128, 4, D], BF16, 57344)]
    relu_a = t1

    psbig = nc.alloc_psum_tensor("psbig", [128, 4096], F32)
    ps = [psbig[:, i * 512:(i + 1) * 512] for i in range(8)]
    pctr = [0]

    def nb():
        i = pctr[0] % 6
        pctr[0] += 1
        return i

    lctr = [0]

    def nbl():
        i = 6 + lctr[0] % 2
        lctr[0] += 1
        return i

    def mm(out, lhsT, rhs, start, stop, reads, writes):
        P.add("pe", lambda e: e.matmul(out, lhsT=lhsT, rhs=rhs, start=start, stop=stop), reads, writes)

    def act(out, in_, func, reads, writes, scale=1.0, bias=0.0):
        if bias == EPS:
            bias, reads = biasc[0:out.shape[0], 0:1], list(reads) + ["biasc"]
        elif bias == 1.0:
            bias, reads = biasc[0:out.shape[0], 2:3], list(reads) + ["biasc"]
        elif bias != 0.0:
            assert abs(bias + 0.5 * math.log(96.0)) < 1e-9
            bias, reads = biasc[0:out.shape[0], 1:2], list(reads) + ["biasc"]
        P.add("act", lambda e: e.activation(out, in_, func, bias=bias, scale=scale), reads, writes)

    def tt(out, in0, in1, op, reads, writes, eng="dve"):
        P.add(eng, lambda e: e.tensor_tensor(out=out, in0=in0, in1=in1, op=op), reads, writes)

    def ts(out, in0, s1, s2, op0, op1, reads, writes, eng="dve"):
        if s2 is None:
            P.add(eng, lambda e: e.tensor_scalar(out=out, in0=in0, scalar1=s1, scalar2=None, op0=op0), reads, writes)
        else:
            P.add(eng, lambda e: e.tensor_scalar(out=out, in0=in0, scalar1=s1, scalar2=s2, op0=op0, op1=op1), reads, writes)

    def stt(out, in0, scalar, in1, op0, op1, reads, writes):
        P.add("dve", lambda e: e.scalar_tensor_tensor(out=out, in0=in0, scalar=scalar, in1=in1, op0=op0, op1=op1), reads, writes)

    def red(out, in_, reads, writes, op=ALU.add):
        P.add("dve", lambda e: e.tensor_reduce(out=out, in_=in_, axis=AX.X, op=op), reads, writes)

    def recip(out, in_, reads, writes):
        P.add("dve", lambda e: e.reciprocal(out, in_), reads, writes)

    def cp(out, in_, reads, writes, eng="dve"):
        P.add(eng, lambda e: e.tensor_copy(out, in_), reads, writes)

    def memset(ap, v, writes, eng="dve"):
        P.add(eng, lambda e: e.memset(ap, v), (), writes)

    def dma(eng, out, in_, writes, slot, reads=(), war=()):
        P.add(eng, lambda e: e.dma_start(out=out, in_=in_), reads, writes, slot=slot, war=war)

    def rsqrt_small(ap, n, k):
        act(ap, ap, AF.Ln, [k], [k], scale=1.0 / n, bias=EPS)
        act(ap, ap, AF.Exp, [k], [k], scale=-0.5)

    def bc_last(ap, n):
        a = ap.shape[1]
        return ap.unsqueeze(2).broadcast_to([ap.shape[0], a, n])

    def bc_mid(ap, a):
        return ap.unsqueeze(1).broadcast_to([ap.shape[0], a, ap.shape[1]])

    def v3(ap, d):
        return ap.rearrange("p (h d) -> p h d", d=d)

    XK = lambda c, tb: ("xT", c, tb)
    AK = lambda c, tb: ("act", c, tb)

    for c in range(8):
        dma("sp", xT[:, c, :], xT_d[c * 128:(c + 1) * 128, :], [XK(c, tb) for tb in range(NB)], "x%d" % c)
    dma("pool", ident[:, :], ident_d[:, :], ["ident"], "ident")
    dma("sp", maskT[:, :], mask_d[:, :], ["maskT"], "maskT")
    dma("sp", posi[:, :], pos_d[:, :], ["posi"], "pos")
    dma("sp", t2[:, 0:16], invf_d[:, :], ["t2"], "invf")
    dma("sp", sm2[:, 0:2 * L], hglb_d[:, :], ["sm2"], "hglb")
    memset(biasc[:, 0:1], EPS, ["biasc"])
    memset(biasc[:, 1:2], -0.5 * math.log(96.0), ["biasc"])
    memset(biasc[:, 2:3], 1.0, ["biasc"])
    memset(ones_bf[:, :], 1.0, ["ones_bf"])
    memset(zeros_bf[:, :], 0.0, ["zeros_bf"])
    memset(ones_f[:, :], 1.0, ["ones_f"])
    cp(mask_bf[:, :], maskT[:, :], ["maskT"], ["mask_bf"])
    cp(uvg[:, 0:16], posi[:, :], ["posi"], ["uvg"])
    ang = tmpA[:, 0:256]
    tt(v3(ang, 16), bc_last(uvg[:, 0:16], 16), bc_mid(t2[:, 0:16], NT), ALU.mult, ["uvg", "t2"], ["tmpA"])
    TWO_PI = 2.0 * math.pi

    def sin_table(dst, shift, dkey):
        a = tmpB[:, 0:256]
        ts(a, ang, 1.0, shift, ALU.mult, ALU.add, ["tmpA"], ["tmpB"])
        ts(t1[:, 0:256], a, 1.0 / TWO_PI, None, ALU.mult, None, ["tmpB"], ["t1"])
        cp(iscr[:, :], t1[:, 0:256], ["t1"], ["iscr"])
        cp(t1[:, 0:256], iscr[:, :], ["iscr"], ["t1"])
        stt(a, t1[:, 0:256], -TWO_PI, a, ALU.mult, ALU.add, ["t1", "tmpB"], ["tmpB"])
        ts(t1[:, 0:256], a, math.pi, None, ALU.is_gt, None, ["tmpB"], ["t1"])
        stt(a, t1[:, 0:256], -TWO_PI, a, ALU.mult, ALU.add, ["t1", "tmpB"], ["tmpB"])
        ts(t1[:, 0:256], a, -math.pi, None, ALU.is_lt, None, ["tmpB"], ["t1"])
        stt(a, t1[:, 0:256], TWO_PI, a, ALU.mult, ALU.add, ["t1", "tmpB"], ["tmpB"])
        ts(a, a, 3.141592, -3.141592, ALU.min, ALU.max, ["tmpB"], ["tmpB"])
        act(dst[:, :, :].rearrange("p t i -> p (t i)"), a, AF.Sin, ["tmpB"], [dkey])

    sin_table(sinT, 0.0, "sinT")
    sin_table(cosT, math.pi / 2.0, "cosT")
    lb3 = sm2[:, 0:2 * L].rearrange("p (c l) -> p c l", l=L)
    red(sm[:, 0:2], lb3, ["sm2"], ["sm"], op=ALU.max)
    tt(lb3, lb3, bc_last(sm[:, 0:2], L), ALU.subtract, ["sm2", "sm"], ["sm2"])
    act(sm2[:, 0:2 * L], sm2[:, 0:2 * L], AF.Exp, ["sm2"], ["sm2"])
    red(sm[:, 0:2], lb3, ["sm2"], ["sm"])
    recip(sm[:, 0:2], sm[:, 0:2], ["sm"], ["sm"])
    tt(lb3, lb3, bc_last(sm[:, 0:2], L), ALU.mult, ["sm2", "sm"], ["sm2"])
    lbv3 = lbv[:, :].rearrange("p (c l) -> p c l", l=L)
    memset(lbv[:, :], 0.0, ["lbv"])
    for l in range(1, L):
        tt(lbv3[:, :, l:l + 1], lbv3[:, :, l - 1:l], lb3[:, :, l:l + 1], ALU.add, ["lbv", "sm2"], ["lbv"])
    ts(oml[:, :], lbv[:, :], -1.0, 1.0, ALU.mult, ALU.add, ["lbv"], ["oml"])

    class _Stop(Exception):
        pass

    def stage(name):
        if name == stop_at:
            raise _Stop()

    def run_lockstep(gens):
        alive = [True] * len(gens)
        while any(alive):
            for gi, g in enumerate(gens):
                if alive[gi]:
                    try:
                        next(g)
                    except StopIteration:
                        alive[gi] = False

    def rms_to_act(gT, gkey, l):
        def stage_a(tb):
            sl = slice(tb * 512, (tb + 1) * 512)
            b = nb()
            for hf in range(2):
                act(sq4[:, :, :], xT[:, hf * 4:(hf + 1) * 4, sl], AF.Square,
                    [XK(c, tb) for c in range(hf * 4, hf * 4 + 4)], ["sq4"])
                for j in range(4):
                    mm(ps[b][:, :], ones_bf[:, :], sq4[:, j, :], hf == 0 and j == 0, hf == 1 and j == 3,
                       ["sq4", "ones_bf"], [("ps", b)])
            return b

        def stage_b(tb, b):
            sl = slice(tb * 512, (tb + 1) * 512)
            tA, kA = (tmpA, "tmpA") if tb % 2 == 0 else (t1, "t1")
            tB, kB = (tmpB, "tmpB") if tb % 2 == 0 else (t2, "t2")
            act(tA[:, :], ps[b][:, :], AF.Ln, [("ps", b)], [kA], scale=1.0 / D, bias=EPS)
            act(tB[:, :], tA[:, :], AF.Exp, [kA], [kB], scale=-0.5)
            for c in range(8):
                stt(actT[:, c, sl], xT[:, c, sl], gT[:, c:c + 1], tB[:, :], ALU.mult, ALU.mult,
                    [XK(c, tb), kB, gkey], [AK(c, tb)])

        banks = {0: stage_a(0)}
        for tb in range(NB):
            if tb + 1 < NB:
                banks[tb + 1] = stage_a(tb + 1)
            stage_b(tb, banks[tb])

    def load_wg(l, buf, blocks):
        src = win_d[l]
        for bi, (c0, n, d0) in enumerate(blocks):
            dma("pool", wg[buf][:, :, d0:d0 + n], src[:, c0:c0 + n].rearrange("(c p) n -> p c n", p=128),
                [("wg", buf, bi)], "wg%d_%d" % (buf, bi), war=[("wg", buf, k) for k in range(4)])

    def load_params(l):
        for dst, src, nm in ((g1T, g1T_d, "g1T"), (g2T, g2T_d, "g2T"), (gmvg, gmvg_d, "gmvg"), (gmog, gmog_d, "gmog"),
                             (hgog, hgog_d, "hgog"), (mlaog, mlaog_d, "mlaog"), (gmbT, gmbT_d, "gmbT"),
                             (qagT, qagT_d, "qagT"), (kvagT, kvagT_d, "kvagT"), (qg, qg_d, "qg"), (kgn, kgn_d, "kgn"),
                             (kgpe, kgpe_d, "kgpe")):
            dma("sp", dst[:, :], src[l], [nm], "p_" + nm)
        dma("sp", uvg[:, :], gmwT_d[l], ["uvg"], "p_gmw")
        tt(wTm[:, :, :], v3(uvg[:, :], 128), bc_mid(maskT[:, :], 4), ALU.mult, ["uvg", "maskT"], ["wTm"])

    def gelu_psum(b, key):
        pk = ("ps", b)
        act(t1[:, :], ps[b][:, :], AF.Square, [pk], ["t1"])
        ts(t1[:, :], t1[:, :], 0.044715, 1.0, ALU.mult, ALU.add, ["t1"], ["t1"])
        tt(t1[:, :], t1[:, :], ps[b][:, :], ALU.mult, ["t1", pk], ["t1"])
        act(t2[:, :], t1[:, :], AF.Exp, ["t1"], ["t2"], scale=-1.5957691216057308)
        ts(t2[:, :], t2[:, :], 1.0, None, ALU.add, None, ["t2"], ["t2"])
        recip(t2[:, :], t2[:, :], ["t2"], ["t2"])
        tt(uvg[:, :], t2[:, :], ps[b][:, :], ALU.mult, ["t2", pk], ["uvg"])

    def head_norm(dst3, src3, nh, hd, srckeys, dstkeys, gain3, gkey):
        n = nh * hd
        act(v3(t1[:, 0:n], hd), src3, AF.Square, srckeys, ["t1"])
        red(sm[:, 0:nh], v3(t1[:, 0:n], hd), ["t1"], ["sm"])
        rsqrt_small(sm[:, 0:nh], hd, "sm")
        tt(v3(t1[:, 0:n], hd), src3, bc_last(sm[:, 0:nh], hd), ALU.mult, srckeys + ["sm"], ["t1"])
        tt(dst3, v3(t1[:, 0:n], hd), gain3, ALU.mult, ["t1", gkey], dstkeys)

    def layer(l, first_wg_loaded):
        load_params(l)
        stage("params")
        rms_to_act(g1T, "g1T", l)
        stage("norm1")
        if not first_wg_loaded:
            load_wg(l, 0, [(0, 512, 0)])
        def ct_blocks(ct):
            return [(1024 + ct * 128, 128, 0), (1280 + ct * 128, 128, 128), (512 + ct * 128, 128, 256), (768 + ct * 128, 128, 384)]
        load_wg(l, 1, ct_blocks(0))
        stage("wgload")
        def g0_gen(tb, hf_):
            cs_ = slice(hf_ * 1024, (hf_ + 1) * 1024)
            hs_ = slice(hf_ * 512, (hf_ + 1) * 512)
            psq = psbig[:, hf_ * 1024:(hf_ + 1) * 1024]
            pk2 = [("ps", 2 * hf_), ("ps", 2 * hf_ + 1)]
            by, bt = 4 + hf_, 6 + hf_
            T1, T2, T3 = G_t1[:, cs_], G_t2[:, cs_], G_t3[:, hs_]
            K1, K2, K3 = ("G_t1", hf_), ("G_t2", hf_), ("G_t3", hf_)
            U4, V4, KU, KV = u4[:, hs_], v4[:, hs_], ("u4", hf_), ("v4", hf_)
            VN, YB, KVN, KYB = vn4[:, 2 * hf_:2 * hf_ + 2, :], yb4[:, 2 * hf_:2 * hf_ + 2, :], ("vn4", hf_), ("yb4", hf_)
            SM, KSM = sm[:, hf_ * 8:hf_ * 8 + 8], ("smg", hf_)
            for jj in range(2):
                j = 2 * hf_ + jj
                tsl = slice((tb * 4 + j) * 128, (tb * 4 + j + 1) * 128)
                for c in range(8):
                    mm(ps[j][:, :], actT[:, c, tsl], wg[0][:, c, :], c == 0, c == 7, [AK(c, tb), ("wg", 0, 0)], [("ps", j)])
            yield
            act(T1, psq, AF.Square, pk2, [K1], scale=math.sqrt(0.044715))
            yield
            stt(T1, T1, 1.0, psq, ALU.add, ALU.mult, [K1] + pk2, [K1])
            yield
            act(T2, T1, AF.Exp, [K1], [K2], scale=-1.5957691216057308)
            yield
            act(T2, T2, AF.Ln, [K2], [K2], bias=1.0)
            yield
            act(T2, T2, AF.Exp, [K2], [K2], scale=-1.0)
            yield
            q3_ = psq.rearrange("p (j n) -> p j n", n=512)
            g3_ = T2.rearrange("p (j n) -> p j n", n=512)
            tt(v3(V4, 256), g3_[:, :, 256:512], q3_[:, :, 256:512], ALU.mult, [K2] + pk2, [KV])
            yield
            tt(v3(U4, 256), g3_[:, :, 0:256], q3_[:, :, 0:256], ALU.mult, [K2] + pk2, [KU])
            act(T3, V4, AF.Square, [KV], [K3])
            yield
            red(SM, v3(T3, 64), [K3], [KSM])
            yield
            act(SM, SM, AF.Ln, [KSM], [KSM], scale=1.0 / 64, bias=EPS)
            yield
            act(SM, SM, AF.Exp, [KSM], [KSM], scale=-0.5)
            yield
            tt(v3(T3, 64), v3(V4, 64), bc_last(SM, 64), ALU.mult, [KV, KSM], [K3])
            yield
            tt(VN, v3(T3, 256), bc_mid(gmvg[:, :], 2), ALU.mult, [K3, "gmvg"], [KVN])
            yield
            for jj in range(2):
                for h in range(4):
                    co = jj * 256 + h * 64
                    mm(ps[by][:, co:co + 64], wTm[:, h, :], VN[:, jj, h * 64:(h + 1) * 64], True, True, ["wTm", KVN], [("ps", by)])
            yield
            for jj in range(2):
                tt(v3(T3[:, jj * 256:(jj + 1) * 256], 64), v3(ps[by][:, jj * 256:(jj + 1) * 256], 64), bc_last(gmbT[:, 0:4], 64), ALU.add,
                   [("ps", by), "gmbT"], [K3])
            yield
            tt(T3, T3, U4, ALU.mult, [K3, KU], [K3])
            yield
            act(T2[:, 0:512], T3, AF.Square, [K3], [K2])
            yield
            red(SM, v3(T2[:, 0:512], 64), [K2], [KSM])
            yield
            act(SM, SM, AF.Ln, [KSM], [KSM], scale=1.0 / 64, bias=EPS)
            yield
            act(SM, SM, AF.Exp, [KSM], [KSM], scale=-0.5)
            yield
            tt(v3(T3, 64), v3(T3, 64), bc_last(SM, 64), ALU.mult, [K3, KSM], [K3])
            yield
            tt(YB, v3(T3, 256), bc_mid(gmog[:, :], 2), ALU.mult, [K3, "gmog"], [KYB])
            yield
            for jj in range(2):
                for c in range(2):
                    co = jj * 256 + c * 128
                    mm(ps[bt][:, co:co + 128], YB[:, jj, c * 128:(c + 1) * 128], ident[:, :], True, True, [KYB, "ident"], [("ps", bt)])
            yield
            pt4 = ps[bt].rearrange("p (j c t) -> p j c t", c=2, t=128)
            t0_ = tb * 512 + hf_ * 256
            for c in range(2):
                act(gmT[:, c, t0_:t0_ + 256].rearrange("p (j t) -> p j t", t=128), pt4[:, :, c, :], AF.Copy,
                    [("ps", bt)], [("gmT", tb * 4 + 2 * hf_ + jj) for jj in range(2)])
            yield

        memset(biasc[:, 7:8], 0.0, ["sm", ("smg", 0), ("smg", 1)])
        for tb in range(NB):
            run_lockstep([g0_gen(tb, 0), g0_gen(tb, 1)])
        memset(biasc[:, 7:8], 0.0, ["sm", ("smg", 0), ("smg", 1)])
        stage("G0")
        for ct in range(2):
            wb = 1 - ct
            if ct == 0:
                load_wg(l, 0, ct_blocks(1))
            else:
                load_wg(l, 1, [(1536, 384, 0), (1920, 32, 384)])
            for t in range(NT):
                tb = t // 4
                tsl = slice(t * 128, (t + 1) * 128)
                b = nb()
                for c in range(8):
                    mm(ps[b][:, 0:256], actT[:, c, tsl], wg[wb][:, c, 0:256], c == 0, c == 7,
                       [AK(c, tb), ("wg", wb, 0), ("wg", wb, 1)], [("ps", b)])
                pk = ("ps", b)
                act(hgi[:, t, :], ps[b][:, 0:128], AF.Copy, [pk], [("hgi", t)])
                act(t2[:, 0:128], ps[b][:, 128:256], AF.Exp, [pk], ["t2"], scale=-1.0)
                act(t2[:, 0:128], t2[:, 0:128], AF.Ln, ["t2"], ["t2"], bias=1.0)
                act(t2[:, 0:128], t2[:, 0:128], AF.Exp, ["t2"], ["t2"], scale=-1.0)
                tt(sg[:, t, :], t2[:, 0:128], ps[b][:, 128:256], ALU.mult, ["t2", pk], [("sg", t)])
            for tb in range(NB):
                sl = slice(tb * 512, (tb + 1) * 512)
                bq, bf_ = nb(), nb()
                for c in range(8):
                    mm(ps[bq][:, :], wg[wb][:, c, 256:384], actT[:, c, sl], c == 0, c == 7, [AK(c, tb), ("wg", wb, 2)], [("ps", bq)])
                for c in range(8):
                    mm(ps[bf_][:, :], wg[wb][:, c, 384:512], actT[:, c, sl], c == 0, c == 7, [AK(c, tb), ("wg", wb, 3)], [("ps", bf_)])
                act(t2[:, :], ps[bq][:, :], AF.Exp, [("ps", bq)], ["t2"], scale=-1.0)
                act(t2[:, :], t2[:, :], AF.Ln, ["t2"], ["t2"], bias=1.0)
                act(t2[:, :], t2[:, :], AF.Exp, ["t2"], ["t2"], scale=-1.0)
                tt(qT[:, sl], t2[:, :], ps[bq][:, :], ALU.mult, ["t2", ("ps", bq)], [("qT", tb)])
                act(t1[:, :], ps[bf_][:, :], AF.Exp, [("ps", bf_)], ["t1"], scale=-1.0)
                act(t1[:, :], t1[:, :], AF.Ln, ["t1"], ["t1"], bias=1.0)
                act(t1[:, :], t1[:, :], AF.Exp, ["t1"], ["t1"], scale=-1.0)
                li = ct * L + l
                ts(t1[:, :], t1[:, :], oml[:, li:li + 1], lbv[:, li:li + 1], ALU.mult, ALU.add, ["t1", "oml", "lbv"], ["t1"])
                act(uvg[:, :], t1[:, :], AF.Ln, ["t1"], ["uvg"])
                ts(kT[:, sl], t1[:, :], -1.0, 1.0, ALU.mult, ALU.add, ["t1"], [("kT", tb)])
                for j in range(4):
                    ch = tb * 4 + j
                    P.add("dve", (lambda o_, d1: (lambda e: e.tensor_tensor_scan(out=o_, data0=ones_f[:, :], data1=d1, initial=0.0,
                                                                                  op0=ALU.mult, op1=ALU.add)))(bT[:, ch, :], uvg[:, j * 128:(j + 1) * 128]),
                          ["uvg", "ones_f"], [("bT", ch)])
            stage("proj%d" % ct)
            hgrn(l, ct)
            stage("ct%d" % ct)
        g3(l)
        stage("g3")
        mla(l)
        stage("mla")
        if debug and l == 0:
            for c in range(8):
                if c < 2:
                    src, rk = gmT[:, c, :], [("gmT", t) for t in range(NT)]
                elif c < 4:
                    src, rk = hgT[:, c - 2, :], [("hgT", c - 2, tb) for tb in range(NB)]
                else:
                    src, rk = actT[:, c, :], [AK(c, tb) for tb in range(NB)]
                dma("sp", dbg_d[c * 128:(c + 1) * 128, :], src, [("dbg", c)], "dbg%d" % c, reads=rk)
        wout_phase(l)
        stage("wout")
        rms_to_act(g2T, "g2T", l)
        stage("norm2")
        ffn(l)

    def hgrn(l, ct):
        allb = [("bT", ch) for ch in range(NT)]
        for hf in range(2):
            cs = slice(hf * 8, hf * 8 + 8)
            fs = slice(hf * 1024, hf * 1024 + 1024)
            bks = allb[hf * 8:hf * 8 + 8]
            tt(dtmp[:, :, :], bT[:, cs, :], bc_last(bT[:, cs, 127], 128), ALU.subtract, bks, ["dtmp"])
            act(Etmp[:, :, :], dtmp[:, :, :], AF.Exp, ["dtmp"], ["Etmp"], scale=-1.0)
            tt(KendT[:, fs], kT[:, fs], Etmp[:, :, :].rearrange("p c t -> p (c t)"), ALU.mult,
               ["Etmp", ("kT", 2 * hf), ("kT", 2 * hf + 1)], [("KendT", hf)])
        stage("hk")
        act(eb127[:, :], bT[:, :, 127], AF.Exp, allb, ["eb127"])
        for c4 in range(4):
            b = nb()
            for j in range(4):
                ch = c4 * 4 + j
                mm(ps[b][:, j * 128:(j + 1) * 128], KendT[:, ch * 128:(ch + 1) * 128], ident[:, :], True, True,
                   [("KendT", ch // 8), "ident"], [("ps", b)])
            act(Kend_tok[:, c4 * 4:(c4 + 1) * 4, :], v3(ps[b][:, :], 128), AF.Copy, [("ps", b)], [("Ktok", c4)])
        stage("htp")
        ub = [nbl(), nbl()]
        for ch in range(NT):
            b = ub[ch // 8]
            j = ch % 8
            for hh in range(2):
                r = slice(hh * 64, hh * 64 + 64)
                mm(ps[b][r, j * 64:(j + 1) * 64], Kend_tok[:, ch, r], hgi[:, ch, r], True, True,
                   [("Ktok", ch // 4), ("hgi", ch)], [("ps", b)])
        stage("hU")
        memset(Sall[:, 0, :], 0.0, ["Sall"])
        for ch in range(NT):
            b = ub[ch // 8]
            j = ch % 8
            stt(Sall[:, ch + 1, :], Sall[:, ch, :], eb127[:, ch:ch + 1], ps[b][:, j * 64:(j + 1) * 64], ALU.mult, ALU.add,
                ["Sall", "eb127", ("ps", b)], ["Sall"])
        cp(Sbf[:, :, :], Sall[:, 0:NT, :], ["Sall"], ["Sbf"])
        stage("hchain")
        memset(KA[:, :, 64:128], 0.0, ["KA"])
        for hf in range(2):
            cs = slice(hf * 8, hf * 8 + 8)
            fs = slice(hf * 1024, hf * 1024 + 1024)
            bks = allb[hf * 8:hf * 8 + 8]
            kks = [("kT", 2 * hf), ("kT", 2 * hf + 1)]
            qks = [("qT", 2 * hf), ("qT", 2 * hf + 1)]
            q3 = qT[:, fs].rearrange("p (c t) -> p c t", t=128)
            k3 = kT[:, fs].rearrange("p (c t) -> p c t", t=128)
            act(Etmp[:, :, :], bT[:, cs, :], AF.Exp, bks, ["Etmp"])
            tt(QA[:, :, :], q3, Etmp[:, :, :], ALU.mult, ["Etmp"] + qks, ["QA"])
            tt(dtmp[:, :, :], bT[:, cs, :], bc_last(bT[:, cs, 63], 128), ALU.subtract, bks, ["dtmp"])
            act(Etmp[:, :, 64:128], dtmp[:, :, 64:128], AF.Exp, ["dtmp"], ["Etmp"])
            tt(QB[:, :, :], q3[:, :, 64:128], Etmp[:, :, 64:128], ALU.mult, ["Etmp"] + qks, ["QB"])
            act(Etmp[:, :, :], dtmp[:, :, :], AF.Exp, ["dtmp"], ["Etmp"], scale=-1.0)
            tt(KB[:, :, :], k3, Etmp[:, :, :], ALU.mult, ["Etmp"] + kks, ["KB"])
            act(Etmp[:, :, 0:64], bT[:, cs, 0:64], AF.Exp, bks, ["Etmp"], scale=-1.0)
            tt(KA[:, :, 0:64], k3[:, :, 0:64], Etmp[:, :, 0:64], ALU.mult, ["Etmp"] + kks, ["KA"])
            stage("hprep")

            def hbatch_gen(g4, p):
                ST, KST = (sT4, "sT4") if p == 0 else (sT4_b, "sT4_b")
                T1, K1 = (t1, "t1") if p == 0 else (uvg, "uvg")
                T2, K2 = (t2, "t2") if p == 0 else (tmpA, "tmpA")
                SM, KSM = (sm, "sm") if p == 0 else (sm2, "sm2")
                YB, KYB = (ybh, "ybh") if p == 0 else (ybh_b, "ybh_b")
                ch0 = hf * 8 + g4 * 4
                tb = ch0 // 4
                bsb = [nb(), nb()]
                for j in range(4):
                    cc = g4 * 4 + j
                    for hh in range(2):
                        r = slice(hh * 64, hh * 64 + 64)
                        bs_ = bsb[hh]
                        mm(ps[bs_][:, j * 128:j * 128 + 64], KA[r, cc, :], QA[r, cc, 0:64], True, True, ["KA", "QA"], [("ps", bs_)])
                        mm(ps[bs_][:, j * 128 + 64:j * 128 + 128], KB[r, cc, :], QB[r, cc, :], True, True, ["KB", "QB"], [("ps", bs_)])
                yield
                for hh in range(2):
                    bs_ = bsb[hh]
                    tt(ST[:, hh, :, :], v3(ps[bs_][:, :], 128), bc_mid(maskT[:, :], 4), ALU.mult, [("ps", bs_), "maskT"], [KST])
                    yield
                bo = nb()
                for j in range(4):
                    cc = g4 * 4 + j
                    ch = ch0 + j
                    for hh in range(2):
                        r = slice(hh * 64, hh * 64 + 64)
                        co = j * 128 + hh * 64
                        mm(ps[bo][:, co:co + 64], ST[:, hh, j, :], hgi[:, ch, r], True, False, [KST, ("hgi", ch)], [("ps", bo)])
                        mm(ps[bo][:, co:co + 64], QA[r, cc, :], Sbf[r, ch, :], False, True, ["QA", "Sbf"], [("ps", bo)])
                yield
                pk = ("ps", bo)
                act(T1[:, :], ps[bo][:, :], AF.Square, [pk], [K1])
                yield
                red(SM[:, 0:8], v3(T1[:, :], 64), [K1], [KSM])
                yield
                act(SM[:, 0:8], SM[:, 0:8], AF.Ln, [KSM], [KSM], scale=1.0 / 64, bias=EPS)
                yield
                act(SM[:, 0:8], SM[:, 0:8], AF.Exp, [KSM], [KSM], scale=-0.5)
                yield
                tt(v3(T1[:, :], 64), v3(ps[bo][:, :], 64), bc_last(SM[:, 0:8], 64), ALU.mult, [pk, KSM], [K1])
                yield
                tt(v3(T2[:, :], 128), v3(T1[:, :], 128), bc_mid(hgog[:, ct * 128:(ct + 1) * 128], 4), ALU.mult, [K1, "hgog"], [K2])
                yield
                tt(YB[:, :, :], v3(T2[:, :], 128), sg[:, ch0:ch0 + 4, :], ALU.mult, [K2] + [("sg", ch0 + j) for j in range(4)], [KYB])
                yield
                btp = nbl()
                for j in range(4):
                    mm(ps[btp][:, j * 128:(j + 1) * 128], YB[:, j, :], ident[:, :], True, True, [KYB, "ident"], [("ps", btp)])
                yield
                act(hgT[:, ct, tb * 512:(tb + 1) * 512], ps[btp][:, :], AF.Copy, [("ps", btp)], [("hgT", ct, tb)])
                yield

            run_lockstep([hbatch_gen(0, 0), hbatch_gen(1, 1)])

    def g3(l):
        wb = 1
        dma("pool", wuq[:, :, :], wuq_d[l].rearrange("(c p) n -> p c n", p=128), ["wuq"], "wuq")
        dma("pool", wukv[:, :], wukv_d[l], ["wukv"], "wukv")
        for tb in range(NB):
            sl = slice(tb * 512, (tb + 1) * 512)
            bb = []
            for j in range(3):
                b = nb()
                bb.append(b)
                for c in range(8):
                    mm(ps[b][:, :], wg[wb][:, c, j * 128:(j + 1) * 128], actT[:, c, sl], c == 0, c == 7,
                       [AK(c, tb), ("wg", wb, 0)], [("ps", b)])
                act(cqraw[:, j, :], ps[b][:, :], AF.Copy, [("ps", b)], ["cqraw"])
                act(sq4[:, j, :], ps[b][:, :], AF.Square, [("ps", b)], ["sq4"])
            b1, b2 = nb(), nb()
            mm(ps[b1][:, :], ones_bf[:, :], sq4[:, 0, :], True, False, ["sq4", "ones_bf"], [("ps", b1)])
            mm(ps[b1][:, :], ones_bf[:, :], sq4[:, 1, :], False, True, ["sq4", "ones_bf"], [("ps", b1)])
            mm(ps[b2][:, :], ones_bf[:, :], sq4[:, 2, :], True, True, ["sq4", "ones_bf"], [("ps", b2)])
            act(tmpA[:, :], ps[b1][:, :], AF.Ln, [("ps", b1)], ["tmpA"], scale=1.0 / 256, bias=EPS)
            act(tmpB[:, :], tmpA[:, :], AF.Exp, ["tmpA"], ["tmpB"], scale=-0.5)
            for j in range(2):
                stt(cqnT[:, j, sl], cqraw[:, j, :], qagT[:, j:j + 1], tmpB[:, :], ALU.mult, ALU.mult,
                    ["cqraw", "tmpB", "qagT"], [("cqnT", tb)])
            act(tmpA[:, :], ps[b2][:, :], AF.Ln, [("ps", b2)], ["tmpA"], scale=1.0 / 128, bias=EPS)
            act(tmpB[:, :], tmpA[:, :], AF.Exp, ["tmpA"], ["tmpB"], scale=-0.5)
            stt(ckvnT[:, sl], cqraw[:, 2, :], kvagT[:, 0:1], tmpB[:, :], ALU.mult, ALU.mult,
                ["cqraw", "tmpB", "kvagT"], [("ckvnT", tb)])
        for t in range(NT):
            tb = t // 4
            tsl = slice(t * 128, (t + 1) * 128)
            b = nb()
            for c in range(8):
                mm(ps[b][:, 0:32], actT[:, c, tsl], wg[wb][:, c, 384:416], c == 0, c == 7, [AK(c, tb), ("wg", wb, 1)], [("ps", b)])
            act(t1[:, 0:32], ps[b][:, 0:32], AF.Square, [("ps", b)], ["t1"])
            red(sspe[:, t:t + 1], t1[:, 0:32], ["t1"], ["sspe"])
            tt(rkt[:, :], ps[b][:, 0:32], kgpe[:, :], ALU.mult, [("ps", b), "kgpe"], ["rkt"])
            rope(rkb[:, :, :], rkt[:, :].unsqueeze(1), 1, t, "rkt", "rkb")
            b2 = nb()
            mm(ps[b2][64:96, 0:128], rkb[:, 0, :], ident[:, :], True, True, ["rkb", "ident"], [("ps", b2)])
            act(krT[64:96, tsl], ps[b2][64:96, 0:128], AF.Copy, [("ps", b2)], [("krT", t)])

    def rope(dst3, src3, nh, t, srck, dstk):
        cs = bc_mid(cosT[:, t, :], nh)
        sn = bc_mid(sinT[:, t, :], nh)
        tt(ra[:, 0:nh, :], src3[:, :, 0:16], cs, ALU.mult, [srck], ["ra"])
        tt(rb[:, 0:nh, :], src3[:, :, 16:32], sn, ALU.mult, [srck], ["rb"])
        tt(dst3[:, 0:nh, 0:16], ra[:, 0:nh, :], rb[:, 0:nh, :], ALU.subtract, ["ra", "rb"], [dstk])
        tt(ra[:, 0:nh, :], src3[:, :, 16:32], cs, ALU.mult, [srck], ["ra"])
        tt(rb[:, 0:nh, :], src3[:, :, 0:16], sn, ALU.mult, [srck], ["rb"])
        tt(dst3[:, 0:nh, 16:32], ra[:, 0:nh, :], rb[:, 0:nh, :], ALU.add, ["ra", "rb", dstk], [dstk])

    def mla(l):
        allkr = [("krT", t) for t in range(NT)]
        for hf in range(2):
            def kside_gen():
                for hh in range(4):
                    h = hf * 4 + hh
                    for tb in range(NB):
                        sl = slice(tb * 512, (tb + 1) * 512)
                        b = nb()
                        mm(ps[b][0:64, :], wukv[:, h * 64:(h + 1) * 64], ckvnT[:, sl], True, True, ["wukv", ("ckvnT", tb)], [("ps", b)])
                        yield
                        ts(KT[0:64, hh, sl], ps[b][0:64, :], kgn[0:64, 0:1], None, ALU.mult, None, [("ps", b), "kgn"], [("KT", hh, tb)])
                        yield
                        yield
                        yield

            kgen = kside_gen()
            if hf == 0:
                for tb in range(NB):
                    sl = slice(tb * 512, (tb + 1) * 512)
                    act(KT[64:96, :, sl], bc_mid(krT[64:96, sl], 4), AF.Copy, allkr, [("KT", hh, tb) for hh in range(4)])
            memset(Vx[:, :, :, 64:65], 1.0, ["Vx1"])
            def tile_gen(t, p):
                T1, K1 = (t1, "t1") if p == 0 else (t2, "t2")
                SM, SMK = (sm, "sm") if p == 0 else (smb, "smb")
                SM2, SM2K = (sm2, "sm2") if p == 0 else (sm2b, "sm2b")
                QTMP, QTK = (qtmp, "qtmp") if p == 0 else (qtmp_b, "qtmp_b")
                XR, XRK = (xr, "xr") if p == 0 else (xr_b, "xr_b")
                QF, QFK = (qf, "qf") if p == 0 else (qf_b, "qf_b")
                RA, RAK = (ra, "ra") if p == 0 else (ra_b, "ra_b")
                RB, RBK = (rb, "rb") if p == 0 else (rb_b, "rb_b")
                tb = t // 4
                tsl = slice(t * 128, (t + 1) * 128)
                b = nb()
                mm(ps[b][:, 0:256], ckvnT[:, tsl], wukv[:, hf * 256:(hf + 1) * 256], True, True, [("ckvnT", tb), "wukv"], [("ps", b)])
                mm(ps[b][:, 256:512], ckvnT[:, tsl], wukv[:, 512 + hf * 256:512 + (hf + 1) * 256], True, True,
                   [("ckvnT", tb), "wukv"], [("ps", b)])
                bq = nb()
                for j in range(2):
                    mm(ps[bq][:, 0:384], cqnT[:, j, tsl], wuq[:, j, hf * 384:(hf + 1) * 384], j == 0, j == 1,
                       [("cqnT", tb), "wuq"], [("ps", bq)])
                yield
                pk = ("ps", b)
                act(T1[:, 0:256], ps[b][:, 0:256], AF.Square, [pk], [K1])
                yield
                red(SM2[:, 0:4], v3(T1[:, 0:256], 64), [K1], [SM2K])
                yield
                ts(SM2[:, 0:4], SM2[:, 0:4], sspe[:, t:t + 1], None, ALU.add, None, [SM2K, "sspe"], [SM2K])
                yield
                act(SM2[:, 0:4], SM2[:, 0:4], AF.Ln, [SM2K], [SM2K], scale=1.0 / 96, bias=EPS)
                yield
                act(rks[:, t, hf * 4:(hf + 1) * 4], SM2[:, 0:4], AF.Exp, [SM2K], [("rks", t)], scale=-0.5, bias=-0.5 * math.log(96.0))
                act(Vx[:, t, :, 0:64], v3(ps[b][:, 256:512], 64), AF.Copy, [pk], [("Vx", t)])
                yield
                qk = ("ps", bq)
                q3 = v3(ps[bq][:, 0:384], 96)
                act(T1[:, 0:384], ps[bq][:, 0:384], AF.Square, [qk], [K1])
                yield
                red(SM[:, 0:4], v3(T1[:, 0:384], 96), [K1], [SMK])
                yield
                act(SM[:, 0:4], SM[:, 0:4], AF.Ln, [SMK], [SMK], scale=1.0 / 96, bias=EPS)
                yield
                act(SM[:, 0:4], SM[:, 0:4], AF.Exp, [SMK], [SMK], scale=-0.5)
                yield
                tt(QTMP[:, :, :], q3, bc_last(SM[:, 0:4], 96), ALU.mult, [qk, SMK], [QTK])
                yield
                tt(QF[:, :, 0:64], QTMP[:, :, 0:64], bc_mid(qg[:, 0:64], 4), ALU.mult, [QTK, "qg"], [QFK])
                yield
                tt(XR[:, :, :], QTMP[:, :, 64:96], bc_mid(qg[:, 64:96], 4), ALU.mult, [QTK, "qg"], [XRK])
                yield
                cs = bc_mid(cosT[:, t, :], 4)
                sn = bc_mid(sinT[:, t, :], 4)
                tt(RA[:, :, :], XR[:, :, 0:16], cs, ALU.mult, [XRK], [RAK])
                yield
                tt(RB[:, :, :], XR[:, :, 16:32], sn, ALU.mult, [XRK], [RBK])
                yield
                tt(QF[:, :, 64:80], RA[:, :, :], RB[:, :, :], ALU.subtract, [RAK, RBK], [QFK])
                yield
                tt(RA[:, :, :], XR[:, :, 16:32], cs, ALU.mult, [XRK], [RAK])
                yield
                tt(RB[:, :, :], XR[:, :, 0:16], sn, ALU.mult, [XRK], [RBK])
                yield
                tt(QF[:, :, 80:96], RA[:, :, :], RB[:, :, :], ALU.add, [RAK, RBK, QFK], [QFK])
                yield
                bt_ = nb()
                for hh in range(4):
                    mm(ps[bt_][0:96, hh * 128:(hh + 1) * 128], QF[:, hh, :], ident[:, :], True, True, [QFK, "ident"], [("ps", bt_)])
                yield
                act(QT[0:96, :, tsl], v3(ps[bt_][0:96, :], 128), AF.Copy, [("ps", bt_)], [("QT", t)])
                yield

            kalive = True
            for pair in range(NT // 2):
                gens = [tile_gen(2 * pair, 0), tile_gen(2 * pair + 1, 1)]
                alive = [True, True]
                while any(alive):
                    for gi in range(2):
                        if alive[gi]:
                            try:
                                next(gens[gi])
                            except StopIteration:
                                alive[gi] = False
                    if kalive:
                        try:
                            next(kgen)
                        except StopIteration:
                            kalive = False
            for _ in kgen:
                pass
            units = [(hh, qb, kt) for hh in range(4) for qb in range(NB) for kt in range(4 * qb + 4)]
            sbank = {}
            bo_of = {}

            def emit_S(u):
                hh, qb, kt = u
                j0 = max(0, kt - 4 * qb)
                c0 = j0 * 128
                bs_ = nb()
                sbank[u] = bs_
                mm(ps[bs_][:, c0:512], KT[0:96, hh, kt * 128:(kt + 1) * 128], QT[0:96, hh, qb * 512 + c0:(qb + 1) * 512],
                   True, True, [("KT", hh, kt // 4)] + [("QT", qb * 4 + j) for j in range(j0, 4)], [("ps", bs_)])

            def emit_PV(u, pb):
                hh, qb, kt = u
                h = hf * 4 + hh
                j0 = max(0, kt - 4 * qb)
                c0 = j0 * 128
                bs_ = sbank.pop(u)
                if (hh, qb) not in bo_of:
                    bo_of[(hh, qb)] = nbl()
                    mm(ps[bo_of[(hh, qb)]][:, 0:260], zeros_bf[:, 0:128], zeros_bf[:, 0:260], True, False,
                       ["zeros_bf"], [("ps", bo_of[(hh, qb)])])
                bo = bo_of[(hh, qb)]
                act(pT[pb][:, c0:512], ps[bs_][:, c0:512], AF.Exp, [("ps", bs_), ("rks", kt)], [("pT", pb)],
                    scale=rks[:, kt, h:h + 1])
                if kt >= 4 * qb:
                    tt(pT[pb][:, c0:c0 + 128], pT[pb][:, c0:c0 + 128], mask_bf[:, :], ALU.mult,
                       [("pT", pb), "mask_bf"], [("pT", pb)])
                for j in range(j0, 4):
                    mm(ps[bo][:, j * 65:(j + 1) * 65], pT[pb][:, j * 128:(j + 1) * 128], Vx[:, kt, hh, :],
                       False, kt == 4 * qb + 3 and j == 3, [("pT", pb), ("Vx", kt), "Vx1"], [("ps", bo)])

            def epilogue(hh, qb):
                h = hf * 4 + hh
                bo = bo_of[(hh, qb)]
                o3 = ps[bo][:, 0:260].rearrange("p (j d) -> p j d", d=65)
                recip(sm2[:, 0:4].unsqueeze(2), o3[:, :, 64:65], [("ps", bo)], ["sm2"])
                tt(v3(t2[:, 0:256], 64), o3[:, :, 0:64], bc_last(sm2[:, 0:4], 64), ALU.mult, [("ps", bo), "sm2"], ["t2"])
                tt(t1[:, 0:256], t2[:, 0:256], t2[:, 0:256], ALU.mult, ["t2"], ["t1"])
                red(sm[:, 0:4], v3(t1[:, 0:256], 64), ["t1"], ["sm"])
                rsqrt_small(sm[:, 0:4], 64, "sm")
                tt(v3(t1[:, 0:256], 64), v3(t2[:, 0:256], 64), bc_last(sm[:, 0:4], 64), ALU.mult, ["t2", "sm"], ["t1"])
                tt(v3(yb[:, :], 64), v3(t1[:, 0:256], 64), bc_mid(mlaog[:, h * 64:(h + 1) * 64], 4), ALU.mult, ["t1", "mlaog"], ["yb"])

                def transposes():
                    bt_ = nb()
                    r = slice((h % 2) * 64, (h % 2) * 64 + 64)
                    for j in range(4):
                        mm(ps[bt_][r, j * 128:(j + 1) * 128], yb[:, j * 64:(j + 1) * 64], ident[:, :], True, True,
                           ["yb", "ident"], [("ps", bt_)])
                    cp(actT[r, 4 + h // 2, qb * 512:(qb + 1) * 512], ps[bt_][r, :], [("ps", bt_)], [AK(4 + h // 2, qb)])
                return transposes

            pend_el, pend_tp = None, None
            emit_S(units[0])
            emit_S(units[1])
            for i, u in enumerate(units):
                if i + 2 < len(units):
                    emit_S(units[i + 2])
                emit_PV(u, i % 4)
                hh, qb, kt = u
                if kt == 4 * qb + 3:
                    if pend_tp is not None:
                        pend_tp()
                        pend_tp = None
                    if pend_el is not None:
                        pend_tp = epilogue(*pend_el)
                    pend_el = (hh, qb)
            if pend_tp is not None:
                pend_tp()
            epilogue(*pend_el)()

    def wout_phase(l):
        for hlf in range(2):
            dma("pool", Wout[:, hlf * 4:(hlf + 1) * 4, :],
                wout_d[l][hlf * 512:(hlf + 1) * 512, :].rearrange("(c p) n -> p c n", p=128), [("Wout", hlf)], "Wout%d" % hlf)
        for oc in range(8):
            for tb in range(NB):
                sl = slice(tb * 512, (tb + 1) * 512)
                b = nb()
                for c in range(8):
                    if c < 2:
                        src, rk = gmT[:, c, sl], [("gmT", tb * 4 + j) for j in range(4)]
                    elif c < 4:
                        src, rk = hgT[:, c - 2, sl], [("hgT", c - 2, tb)]
                    else:
                        src, rk = actT[:, c, sl], [AK(c, tb)]
                    mm(ps[b][:, :], Wout[:, c, oc * 128:(oc + 1) * 128], src, c == 0, c == 7, rk + [("Wout", c // 4)], [("ps", b)])
                tt(xT[:, oc, sl], xT[:, oc, sl], ps[b][:, :], ALU.add, [XK(oc, tb), ("ps", b)], [XK(oc, tb)])

    def ffn(l):
        NG = 8

        def load(g):
            bf = g % 2
            dma("pool", W1g[bf][:, :, :], wff1_d[l][:, g * 512:(g + 1) * 512].rearrange("(c p) n -> p c n", p=128),
                [("W1g", bf)], "W1g%d" % bf)
            dma("pool", W2g[bf][:, :, :], wff2_d[l][g * 512:(g + 1) * 512, :].rearrange("(j p) n -> p j n", p=128),
                [("W2g", bf)], "W2g%d" % bf)

        def up(g):
            bf = g % 2
            for j in range(4):
                for tb in range(NB):
                    sl = slice(tb * 512, (tb + 1) * 512)
                    b = nb()
                    for c in range(8):
                        mm(ps[b][:, :], W1g[bf][:, c, j * 128:(j + 1) * 128], actT[:, c, sl], c == 0, c == 7,
                           [AK(c, tb), ("W1g", bf)], [("ps", b)])
                    act(relu_a[:, :], ps[b][:, :], AF.Relu, [("ps", b)], ["t1"])
                    act(hT[bf][:, j, sl], relu_a[:, :], AF.Square, ["t1"], [("hT", bf, j, tb)])

        def down(g):
            bf = g % 2
            for oc in range(8):
                for tb in range(NB):
                    sl = slice(tb * 512, (tb + 1) * 512)
                    b = nb()
                    for j in range(4):
                        mm(ps[b][:, :], W2g[bf][:, j, oc * 128:(oc + 1) * 128], hT[bf][:, j, sl], j == 0, j == 3,
                           [("hT", bf, j, tb), ("W2g", bf)], [("ps", b)])
                    tt(xT[:, oc, sl], xT[:, oc, sl], ps[b][:, :], ALU.add, [XK(oc, tb), ("ps", b)], [XK(oc, tb)])

        load(0)
        load(1)
        up(0)
        for g in range(NG):
            if g + 1 < NG:
                up(g + 1)
            down(g)
            if g + 2 < NG:
                load(g + 2)

    try:
        stage("setup")
        for l in range(n_layers):
            layer(l, False)
    except _Stop:
        pass

    for c in range(8):
        dma("sp", outT_d[c * 128:(c + 1) * 128, :], xT[:, c, :], [("out", c)], "o%d" % c,
            reads=[XK(c, tb) for tb in range(NB)])
    P.add("sp", None, reads=[("out", c) for c in range(8)] + ([("dbg", c) for c in range(8)] if debug else []))
    P.emit(nc)
    for al in nc.allocations:
        for ml in (getattr(al, "memorylocations", None) or []):
            if str(ml.type).endswith("SB") and ml.addr >= SB0 and not ml.name.startswith("const-") and ml.name not in _mine and ml.name.rsplit("_", 1)[0] not in _mine:
                raise RuntimeError("unexpected SBUF allocation %s @%d" % (ml.name, ml.addr))
            if str(ml.type).endswith("SB") and ml.name.startswith("const-") and ml.addr >= SB0:
                raise RuntimeError("late const allocation %s @%d overlaps manual map" % (ml.name, ml.addr))
    return nc


def _host_inputs(inp):
    f = lambda a: np.ascontiguousarray(np.asarray(a, dtype=np.float32))
    rep = lambda v: np.ascontiguousarray(np.broadcast_to(np.asarray(v, np.float32)[:, None, :], (L, 128, v.shape[-1])))
    half = 16
    invf = (10000.0 ** (-np.arange(half, dtype=np.float32) / half)).astype(np.float32)
    wukv = np.asarray(inp["mla_w_ukv"], np.float32).reshape(L, 128, 8, 2, 64)
    wukv_p = np.concatenate([wukv[:, :, :, 0, :].reshape(L, 128, 512), wukv[:, :, :, 1, :].reshape(L, 128, 512)], axis=-1)
    kg = np.asarray(inp["mla_k_gain"], np.float32)
    kgn = np.zeros((L, 128, 1), np.float32)
    kgn[:, 0:64, 0] = kg[:, 0:64]
    kgn[:, 64:128, 0] = kg[:, 0:64]
    shared = {
        "invf": np.ascontiguousarray(np.broadcast_to(invf[None, :], (128, 16))),
        "ident": np.eye(128, dtype=np.float32),
        "maskT": np.triu(np.ones((128, 128), np.float32)),
        "g1T": f(np.asarray(inp["norm1_gain"]).reshape(L, 8, 128).transpose(0, 2, 1)),
        "g2T": f(np.asarray(inp["norm2_gain"]).reshape(L, 8, 128).transpose(0, 2, 1)),
        "w_in": f(inp["w_in"]),
        "gmvg": rep(inp["gm_v_gain"]),
        "gmog": rep(inp["gm_out_gain"]),
        "hgog": rep(inp["hg_out_gain"]),
        "mlaog": rep(inp["mla_out_gain"]),
        "gmwT": f(np.asarray(inp["gm_w_s"]).transpose(0, 3, 1, 2).reshape(L, 128, 512)),
        "gmbT": f(np.asarray(inp["gm_b_s"]).transpose(0, 2, 1)),
        "hglbT": f(np.asarray(inp["hg_lower_bound"]).reshape(L, 2, 128).transpose(2, 1, 0).reshape(128, 2 * L)),
        "qagT": f(np.asarray(inp["mla_q_a_gain"]).reshape(L, 2, 128).transpose(0, 2, 1)),
        "kvagT": f(np.asarray(inp["mla_kv_a_gain"]).reshape(L, 1, 128).transpose(0, 2, 1)),
        "wuq": f(inp["mla_w_uq"]),
        "wukv": f(wukv_p),
        "qg": rep(inp["mla_q_gain"]),
        "kgn": kgn,
        "kgpe": rep(kg[:, 64:96]),
        "w_out": f(inp["w_out"]),
        "w_ff1": f(inp["w_ff1"]),
        "w_ff2": f(inp["w_ff2"]),
    }
    return shared


def kernel(**inputs):
    x = np.asarray(inputs["x"], np.float32)
    pos = np.asarray(inputs["positions"], np.int32)
    B = x.shape[0]
    shared = _host_inputs(inputs)
    in_maps = []
    for b in range(B):
        m = dict(shared)
        m["xT"] = np.ascontiguousarray(x[b].T)
        m["pos"] = np.ascontiguousarray(pos[b].reshape(NT, 128).T)
        in_maps.append(m)
    nc = build()
    res = run_bass_kernel_spmd(nc, in_maps, core_ids=list(range(B)))
    out = np.stack([np.asarray(res.results[b]["outT"], np.float32).T for b in range(B)], axis=0)
    return np.ascontiguousarray(out)
```

```python
from contextlib import ExitStack
import math
import numpy as np
import concourse.bass as bass
import concourse.mybir as mybir
from concourse.bass_utils import run_bass_kernel_spmd

F32 = mybir.dt.float32
BF16 = mybir.dt.bfloat16
I32 = mybir.dt.int32
ALU = mybir.AluOpType
AF = mybir.ActivationFunctionType
AX = mybir.AxisListType
ENGS = ("pe", "act", "dve", "pool", "sp")

L = 4
S = 2048
D = 1024
NT = 16
NB = 4
EPS = 1e-6
DIN = 1952
DFF = 4096


class Op:
    __slots__ = ("eng", "fn", "deps", "signal", "semval", "slot", "dval")


class Prog:
    def __init__(self):
        self.ops = {e: [] for e in ENGS}
        self.slot_cnt = {}
        self.lastw = {}
        self.readers = {}
        self.ranges = {}
        self.touch = {}
        self._ov = {}

    @staticmethod
    def kbuf(k):
        if isinstance(k, tuple):
            n = k[0]
            if n in ("wg", "pT", "hT", "W1g", "W2g"):
                return n + str(k[1])
            if n == "Ktok":
                return "Kend_tok"
            return n
        if k == "Vx1":
            return "Vx"
        return k

    def overlaps(self, b):
        r = self._ov.get(b)
        if r is None:
            s0, e0 = self.ranges[b]
            r = [y for y, (s1, e1) in self.ranges.items() if y != b and s1 < e0 and s0 < e1]
            self._ov[b] = r
        return r

    def add(self, eng, fn, reads=(), writes=(), slot=None, war=()):
        o = Op()
        o.eng, o.fn, o.signal, o.semval, o.slot, o.dval = eng, fn, False, 0, slot, 0
        d = []
        wb = {self.kbuf(k) for k in writes}
        for b in wb:
            if b in self.ranges:
                for y in self.overlaps(b):
                    t = self.touch.get(y)
                    if t:
                        d.extend(t["last"].values())
                        d.extend(t["dma"])
        for k in list(reads) + list(writes):
            b = self.kbuf(k)
            if b in self.ranges:
                t = self.touch.setdefault(b, {"last": {}, "dma": []})
                if slot is None:
                    t["last"][eng] = o
                else:
                    t["dma"] = (t["dma"] + [o])[-16:]
        for k in war:
            w = self.lastw.get(k)
            if w is not None:
                d.append(w)
            d.extend(self.readers.get(k, ()))
        for k in reads:
            w = self.lastw.get(k)
            if w is not None:
                d.append(w)
        for k in writes:
            w = self.lastw.get(k)
            if w is not None:
                d.append(w)
            d.extend(self.readers.get(k, ()))
        for k in reads:
            self.readers.setdefault(k, []).append(o)
        for k in writes:
            self.lastw[k] = o
            self.readers[k] = []
        o.deps = d
        if slot is not None:
            self.slot_cnt[slot] = self.slot_cnt.get(slot, 0) + 1
            o.dval = 16 * self.slot_cnt[slot]
        self.ops[eng].append(o)
        return o

    def emit(self, nc):
        for e in ENGS:
            for o in self.ops[e]:
                for d in o.deps:
                    if d.slot is None and not (d.eng == "pe" and e == "pe"):
                        d.signal = True
        for e in ENGS:
            c = 0
            for o in self.ops[e]:
                if o.slot is None and o.signal:
                    c += 1
                    o.semval = c
        with ExitStack() as st:
            sems = {e: st.enter_context(nc.semaphore("s_" + e)) for e in ENGS}
            ssem = {s: st.enter_context(nc.semaphore("d_" + str(s))) for s in self.slot_cnt}
            block = st.enter_context(nc.Block())

            def run(e, eng):
                known = {}
                for o in self.ops[e]:
                    for d in o.deps:
                        if d.slot is not None:
                            key, sem, val = ("d", d.slot), ssem[d.slot], d.dval
                        else:
                            if d.eng == "pe" and e == "pe":
                                continue
                            key, sem, val = ("e", d.eng), sems[d.eng], d.semval
                        if known.get(key, 0) >= val:
                            continue
                        eng.wait_ge(sem, val)
                        known[key] = val
                    if o.fn is None:
                        continue
                    ins = o.fn(eng)
                    if o.slot is not None:
                        ins.then_inc(ssem[o.slot], 16)
                    elif o.signal:
                        ins.then_inc(sems[e], 1)

            block.tensor(lambda eng: run("pe", eng))
            block.scalar(lambda eng: run("act", eng))
            block.vector(lambda eng: run("dve", eng))
            block.gpsimd(lambda eng: run("pool", eng))
            block.sync(lambda eng: run("sp", eng))


def build(n_layers=L, debug=False, stop_at=None):
    nc = bass.Bass("TRN2", target_bir_lowering=False)
    P = Prog()

    def din(name, shape, dt=F32):
        return nc.dram_tensor(name, list(shape), dt, kind="ExternalInput").ap()

    xT_d = din("xT", [D, S])
    pos_d = din("pos", [128, NT], I32)
    invf_d = din("invf", [128, 16])
    ident_d = din("ident", [128, 128])
    mask_d = din("maskT", [128, 128])
    g1T_d = din("g1T", [L, 128, 8])
    g2T_d = din("g2T", [L, 128, 8])
    win_d = din("w_in", [L, D, DIN])
    gmvg_d = din("gmvg", [L, 128, 256])
    gmog_d = din("gmog", [L, 128, 256])
    hgog_d = din("hgog", [L, 128, 256])
    mlaog_d = din("mlaog", [L, 128, 512])
    gmwT_d = din("gmwT", [L, 128, 512])
    gmbT_d = din("gmbT", [L, 128, 4])
    hglb_d = din("hglbT", [128, 2 * L])
    qagT_d = din("qagT", [L, 128, 2])
    kvagT_d = din("kvagT", [L, 128, 1])
    wuq_d = din("wuq", [L, 256, 768])
    wukv_d = din("wukv", [L, 128, 1024])
    qg_d = din("qg", [L, 128, 96])
    kgn_d = din("kgn", [L, 128, 1])
    kgpe_d = din("kgpe", [L, 128, 32])
    wout_d = din("w_out", [L, D, D])
    wff1_d = din("w_ff1", [L, D, DFF])
    wff2_d = din("w_ff2", [L, DFF, D])
    outT_d = nc.dram_tensor("outT", [D, S], F32, kind="ExternalOutput").ap()
    if debug:
        dbg_d = nc.dram_tensor("dbg", [D, S], BF16, kind="ExternalOutput").ap()

    SB0 = 16512
    cur = [SB0]

    _mine = set()

    def sb_at(name, shape, dt, off):
        _mine.add(name)
        return nc.alloc_sbuf_tensor_at(name, list(shape), dt, offset=off)

    def nbytes(shape, dt):
        n = 1
        for s_ in shape[1:]:
            n *= s_
        return n * (4 if dt in (F32, I32) else 2)

    def bump(name, shape, dt):
        off = (cur[0] + 31) // 32 * 32
        t = sb_at(name, shape, dt, off)
        cur[0] = off + nbytes(shape, dt)
        return t

    xT = bump("xT", [128, 8, S], F32)
    actT = bump("actT", [128, 8, S], BF16)
    ident = bump("ident", [128, 128], BF16)
    ones_bf = bump("ones_bf", [128, 128], BF16)
    ones_f = bump("ones_f", [128, 128], F32)
    maskT = bump("maskT", [128, 128], F32)
    mask_bf = bump("mask_bf", [128, 128], BF16)
    cosT = bump("cosT", [128, NT, 16], F32)
    sinT = bump("sinT", [128, NT, 16], F32)
    lbv = bump("lbv", [128, 2 * L], F32)
    oml = bump("oml", [128, 2 * L], F32)
    g1T = bump("g1T", [128, 8], F32)
    g2T = bump("g2T", [128, 8], F32)
    gmvg = bump("gmvg", [128, 256], F32)
    gmog = bump("gmog", [128, 256], F32)
    hgog = bump("hgog", [128, 256], F32)
    mlaog = bump("mlaog", [128, 512], F32)
    wTm = bump("wTm", [128, 4, 128], BF16)
    gmbT = bump("gmbT", [128, 4], F32)
    qagT = bump("qagT", [128, 2], F32)
    kvagT = bump("kvagT", [128, 1], F32)
    qg = bump("qg", [128, 96], F32)
    kgn = bump("kgn", [128, 1], F32)
    kgpe = bump("kgpe", [128, 32], F32)
    t1 = bump("t1", [128, 512], F32)
    t2 = bump("t2", [128, 512], F32)
    uvg = bump("uvg", [128, 512], F32)
    sq4 = bump("sq4", [128, 4, 512], BF16)
    tmpA = bump("tmpA", [128, 512], F32)
    tmpB = bump("tmpB", [128, 512], F32)
    sm = bump("sm", [128, 16], F32)
    sm2 = bump("sm2", [128, 16], F32)
    vn = bump("vn", [128, 256], BF16)
    yb = bump("yb", [128, 256], BF16)
    sspe = bump("sspe", [128, NT], F32)
    ra = bump("ra", [128, 4, 16], F32)
    rb = bump("rb", [128, 4, 16], F32)
    rkt = bump("rkt", [128, 32], F32)
    rkb = bump("rkb", [128, 4, 32], BF16)
    krT_off = (cur[0] + 31) // 32 * 32
    krT = bump("krT", [128, S], BF16)
    ybh = sb_at("ybh", [128, 4, 128], BF16, krT_off)
    P.ranges["krT"] = (10 ** 7, 10 ** 7 + 4096)
    P.ranges["ybh"] = (10 ** 7, 10 ** 7 + 1024)
    ybh_b = sb_at("ybh_b", [128, 4, 128], BF16, krT_off + 1024)
    P.ranges["ybh_b"] = (10 ** 7 + 1024, 10 ** 7 + 2048)
    biasc = bump("biasc", [128, 8], F32)
    zeros_bf = bump("zeros_bf", [128, 272], BF16)
    AR = (cur[0] + 31) // 32 * 32
    AEND = 229376 - 256
    ASZ = (AEND - AR) // 32 * 32
    assert ASZ >= 82304 and ASZ - 16384 >= 65856, (AR, ASZ)

    def ar(name, shape, dt, off):
        assert off % 32 == 0 and off + nbytes(shape, dt) <= ASZ, (name, off, ASZ)
        P.ranges[name] = (off, off + nbytes(shape, dt))
        return sb_at(name, shape, dt, AR + off)

    gmT = ar("gmT", [128, 2, S], BF16, 0)
    posi = ar("posi", [128, NT], I32, 0)
    iscr = ar("iscr", [128, 256], I32, 1024)
    hgT = ar("hgT", [128, 2, S], BF16, 8192)
    wg = [ar("wg0", [128, 8, 512], BF16, ASZ - 16384), ar("wg1", [128, 8, 512], BF16, ASZ - 8192)]
    G_t1 = ar("G_t1", [128, 2048], F32, 16384)
    G_t2 = ar("G_t2", [128, 2048], F32, 24576)
    u4 = ar("u4", [128, 1024], F32, 32768)
    v4 = ar("v4", [128, 1024], F32, 36864)
    vn4 = ar("vn4", [128, 4, 256], BF16, 40960)
    yb4 = ar("yb4", [128, 4, 256], BF16, 43008)
    G_t3 = ar("G_t3", [128, 1024], F32, 45056)
    hgi = ar("hgi", [128, NT, 128], BF16, 16384)
    sg = ar("sg", [128, NT, 128], BF16, 20480)
    qT = ar("qT", [128, S], BF16, 24576)
    kT = ar("kT", [128, S], BF16, 28672)
    bT = ar("bT", [128, NT, 128], F32, 32768)
    dtmp = ar("dtmp", [128, 8, 128], F32, 40960)
    Etmp = ar("Etmp", [128, 8, 128], F32, 45056)
    KendT = ar("KendT", [128, S], BF16, 49152)
    Kend_tok = ar("Kend_tok", [128, NT, 128], BF16, 53248)
    QA = ar("QA", [128, 8, 128], BF16, 49152)
    QB = ar("QB", [128, 8, 64], BF16, 51200)
    KA = ar("KA", [128, 8, 128], BF16, 53248)
    KB = ar("KB", [128, 8, 128], BF16, 55296)
    Sall = ar("Sall", [128, NT + 1, 64], F32, 57344)
    Sbf = ar("Sbf", [128, NT, 64], BF16, 61696)
    eb127 = ar("eb127", [128, NT], F32, 63744)
    sT4 = ar("sT4", [128, 2, 4, 128], BF16, 63808)
    sT4_b = ar("sT4_b", [128, 2, 4, 128], BF16, 40960)
    cqnT = ar("cqnT", [128, 2, S], BF16, 16384)
    ckvnT = ar("ckvnT", [128, S], BF16, 24576)
    pT23 = [ar("pT2", [128, 512], BF16, 28672), ar("pT3", [128, 512], BF16, 29696)]
    wuq = ar("wuq", [128, 2, 768], BF16, 30720)
    wukv = ar("wukv", [128, 1024], BF16, 33792)
    rks = ar("rks", [128, NT, 8], F32, 35840)
    Vx = ar("Vx", [128, NT, 4, 65], BF16, 36352)
    QT = ar("QT", [128, 4, S], BF16, 44672)
    KT = ar("KT", [128, 4, S], BF16, 61056)
    pT = [ar("pT0", [128, 512], BF16, 77440), ar("pT1", [128, 512], BF16, 78464)] + pT23
    qtmp = ar("qtmp", [128, 4, 96], F32, 79488)
    xr = ar("xr", [128, 4, 32], F32, 81024)
    qf = ar("qf", [128, 4, 96], BF16, 81536)
    qtmp_b = ar("qtmp_b", [128, 4, 96], F32, 28672)
    xr_b = ar("xr_b", [128, 4, 32], F32, 30208)
    qf_b = ar("qf_b", [128, 4, 96], BF16, 77440)
    ra_b = ar("ra_b", [128, 4, 16], F32, 78464)
    rb_b = ar("rb_b", [128, 4, 16], F32, 78720)
    smb = ar("smb", [128, 16], F32, 78976)
    sm2b = ar("sm2b", [128, 16], F32, 79040)
    cqraw = ar("cqraw", [128, 3, 512], F32, 44672)
    Wout = ar("Wout", [128, 8, D], BF16, 16384)
    hT = [ar("hT0", [128, 4, S], BF16, 0), ar("hT1", [128, 4, S], BF16, 16384)]
    W1g = [ar("W1g0", [128, 8, 512], BF16, 32768), ar("W1g1", [128, 8, 512], BF16, 40960)]
    W2g = [ar("W2g0", [128, 4, D], BF16, 49152), ar("W2g1", [128, 4, D], BF16, 57344)]
    relu_a = t1

    psbig = nc.alloc_psum_tensor("psbig", [128, 4096], F32)
    ps = [psbig[:, i * 512:(i + 1) * 512] for i in range(8)]
    pctr = [0]

    def nb():
        i = pctr[0] % 6
        pctr[0] += 1
        return i

    lctr = [0]

    def nbl():
        i = 6 + lctr[0] % 2
        lctr[0] += 1
        return i

    def mm(out, lhsT, rhs, start, stop, reads, writes):
        P.add("pe", lambda e: e.matmul(out, lhsT=lhsT, rhs=rhs, start=start, stop=stop), reads, writes)

    def act(out, in_, func, reads, writes, scale=1.0, bias=0.0):
        if bias == EPS:
            bias, reads = biasc[0:out.shape[0], 0:1], list(reads) + ["biasc"]
        elif bias == 1.0:
            bias, reads = biasc[0:out.shape[0], 2:3], list(reads) + ["biasc"]
        elif bias != 0.0:
            assert abs(bias + 0.5 * math.log(96.0)) < 1e-9
            bias, reads = biasc[0:out.shape[0], 1:2], list(reads) + ["biasc"]
        P.add("act", lambda e: e.activation(out, in_, func, bias=bias, scale=scale), reads, writes)

    def tt(out, in0, in1, op, reads, writes, eng="dve"):
        P.add(eng, lambda e: e.tensor_tensor(out=out, in0=in0, in1=in1, op=op), reads, writes)

    def ts(out, in0, s1, s2, op0, op1, reads, writes, eng="dve"):
        if s2 is None:
            P.add(eng, lambda e: e.tensor_scalar(out=out, in0=in0, scalar1=s1, scalar2=None, op0=op0), reads, writes)
        else:
            P.add(eng, lambda e: e.tensor_scalar(out=out, in0=in0, scalar1=s1, scalar2=s2, op0=op0, op1=op1), reads, writes)

    def stt(out, in0, scalar, in1, op0, op1, reads, writes):
        P.add("dve", lambda e: e.scalar_tensor_tensor(out=out, in0=in0, scalar=scalar, in1=in1, op0=op0, op1=op1), reads, writes)

    def red(out, in_, reads, writes, op=ALU.add):
        P.add("dve", lambda e: e.tensor_reduce(out=out, in_=in_, axis=AX.X, op=op), reads, writes)

    def recip(out, in_, reads, writes):
        P.add("dve", lambda e: e.reciprocal(out, in_), reads, writes)

    def cp(out, in_, reads, writes, eng="dve"):
        P.add(eng, lambda e: e.tensor_copy(out, in_), reads, writes)

    def memset(ap, v, writes, eng="dve"):
        P.add(eng, lambda e: e.memset(ap, v), (), writes)

    def dma(eng, out, in_, writes, slot, reads=(), war=()):
        P.add(eng, lambda e: e.dma_start(out=out, in_=in_), reads, writes, slot=slot, war=war)

    def rsqrt_small(ap, n, k):
        act(ap, ap, AF.Ln, [k], [k], scale=1.0 / n, bias=EPS)
        act(ap, ap, AF.Exp, [k], [k], scale=-0.5)

    def bc_last(ap, n):
        a = ap.shape[1]
        return ap.unsqueeze(2).broadcast_to([ap.shape[0], a, n])

    def bc_mid(ap, a):
        return ap.unsqueeze(1).broadcast_to([ap.shape[0], a, ap.shape[1]])

    def v3(ap, d):
        return ap.rearrange("p (h d) -> p h d", d=d)

    XK = lambda c, tb: ("xT", c, tb)
    AK = lambda c, tb: ("act", c, tb)

    for c in range(8):
        dma("sp", xT[:, c, :], xT_d[c * 128:(c + 1) * 128, :], [XK(c, tb) for tb in range(NB)], "x%d" % c)
    dma("pool", ident[:, :], ident_d[:, :], ["ident"], "ident")
    dma("sp", maskT[:, :], mask_d[:, :], ["maskT"], "maskT")
    dma("sp", posi[:, :], pos_d[:, :], ["posi"], "pos")
    dma("sp", t2[:, 0:16], invf_d[:, :], ["t2"], "invf")
    dma("sp", sm2[:, 0:2 * L], hglb_d[:, :], ["sm2"], "hglb")
    memset(biasc[:, 0:1], EPS, ["biasc"])
    memset(biasc[:, 1:2], -0.5 * math.log(96.0), ["biasc"])
    memset(biasc[:, 2:3], 1.0, ["biasc"])
    memset(ones_bf[:, :], 1.0, ["ones_bf"])
    memset(zeros_bf[:, :], 0.0, ["zeros_bf"])
    memset(ones_f[:, :], 1.0, ["ones_f"])
    cp(mask_bf[:, :], maskT[:, :], ["maskT"], ["mask_bf"])
    cp(uvg[:, 0:16], posi[:, :], ["posi"], ["uvg"])
    ang = tmpA[:, 0:256]
    tt(v3(ang, 16), bc_last(uvg[:, 0:16], 16), bc_mid(t2[:, 0:16], NT), ALU.mult, ["uvg", "t2"], ["tmpA"])
    TWO_PI = 2.0 * math.pi

    def sin_table(dst, shift, dkey):
        a = tmpB[:, 0:256]
        ts(a, ang, 1.0, shift, ALU.mult, ALU.add, ["tmpA"], ["tmpB"])
        ts(t1[:, 0:256], a, 1.0 / TWO_PI, None, ALU.mult, None, ["tmpB"], ["t1"])
        cp(iscr[:, :], t1[:, 0:256], ["t1"], ["iscr"])
        cp(t1[:, 0:256], iscr[:, :], ["iscr"], ["t1"])
        stt(a, t1[:, 0:256], -TWO_PI, a, ALU.mult, ALU.add, ["t1", "tmpB"], ["tmpB"])
        ts(t1[:, 0:256], a, math.pi, None, ALU.is_gt, None, ["tmpB"], ["t1"])
        stt(a, t1[:, 0:256], -TWO_PI, a, ALU.mult, ALU.add, ["t1", "tmpB"], ["tmpB"])
        ts(t1[:, 0:256], a, -math.pi, None, ALU.is_lt, None, ["tmpB"], ["t1"])
        stt(a, t1[:, 0:256], TWO_PI, a, ALU.mult, ALU.add, ["t1", "tmpB"], ["tmpB"])
        ts(a, a, 3.141592, -3.141592, ALU.min, ALU.max, ["tmpB"], ["tmpB"])
        act(dst[:, :, :].rearrange("p t i -> p (t i)"), a, AF.Sin, ["tmpB"], [dkey])

    sin_table(sinT, 0.0, "sinT")
    sin_table(cosT, math.pi / 2.0, "cosT")
    lb3 = sm2[:, 0:2 * L].rearrange("p (c l) -> p c l", l=L)
    red(sm[:, 0:2], lb3, ["sm2"], ["sm"], op=ALU.max)
    tt(lb3, lb3, bc_last(sm[:, 0:2], L), ALU.subtract, ["sm2", "sm"], ["sm2"])
    act(sm2[:, 0:2 * L], sm2[:, 0:2 * L], AF.Exp, ["sm2"], ["sm2"])
    red(sm[:, 0:2], lb3, ["sm2"], ["sm"])
    recip(sm[:, 0:2], sm[:, 0:2], ["sm"], ["sm"])
    tt(lb3, lb3, bc_last(sm[:, 0:2], L), ALU.mult, ["sm2", "sm"], ["sm2"])
    lbv3 = lbv[:, :].rearrange("p (c l) -> p c l", l=L)
    memset(lbv[:, :], 0.0, ["lbv"])
    for l in range(1, L):
        tt(lbv3[:, :, l:l + 1], lbv3[:, :, l - 1:l], lb3[:, :, l:l + 1], ALU.add, ["lbv", "sm2"], ["lbv"])
    ts(oml[:, :], lbv[:, :], -1.0, 1.0, ALU.mult, ALU.add, ["lbv"], ["oml"])

    class _Stop(Exception):
        pass

    def stage(name):
        if name == stop_at:
            raise _Stop()

    def run_lockstep(gens):
        alive = [True] * len(gens)
        while any(alive):
            for gi, g in enumerate(gens):
                if alive[gi]:
                    try:
                        next(g)
                    except StopIteration:
                        alive[gi] = False

    def rms_to_act(gT, gkey, l):
        def stage_a(tb):
            sl = slice(tb * 512, (tb + 1) * 512)
            b = nb()
            for hf in range(2):
                act(sq4[:, :, :], xT[:, hf * 4:(hf + 1) * 4, sl], AF.Square,
                    [XK(c, tb) for c in range(hf * 4, hf * 4 + 4)], ["sq4"])
                for j in range(4):
                    mm(ps[b][:, :], ones_bf[:, :], sq4[:, j, :], hf == 0 and j == 0, hf == 1 and j == 3,
                       ["sq4", "ones_bf"], [("ps", b)])
            return b

        def stage_b(tb, b):
            sl = slice(tb * 512, (tb + 1) * 512)
            tA, kA = (tmpA, "tmpA") if tb % 2 == 0 else (t1, "t1")
            tB, kB = (tmpB, "tmpB") if tb % 2 == 0 else (t2, "t2")
            act(tA[:, :], ps[b][:, :], AF.Ln, [("ps", b)], [kA], scale=1.0 / D, bias=EPS)
            act(tB[:, :], tA[:, :], AF.Exp, [kA], [kB], scale=-0.5)
            for c in range(8):
                stt(actT[:, c, sl], xT[:, c, sl], gT[:, c:c + 1], tB[:, :], ALU.mult, ALU.mult,
                    [XK(c, tb), kB, gkey], [AK(c, tb)])

        banks = {0: stage_a(0)}
        for tb in range(NB):
            if tb + 1 < NB:
                banks[tb + 1] = stage_a(tb + 1)
            stage_b(tb, banks[tb])

    def load_wg(l, buf, blocks):
        src = win_d[l]
        for bi, (c0, n, d0) in enumerate(blocks):
            dma("pool", wg[buf][:, :, d0:d0 + n], src[:, c0:c0 + n].rearrange("(c p) n -> p c n", p=128),
                [("wg", buf, bi)], "wg%d_%d" % (buf, bi), war=[("wg", buf, k) for k in range(4)])

    def load_params(l):
        for dst, src, nm in ((g1T, g1T_d, "g1T"), (g2T, g2T_d, "g2T"), (gmvg, gmvg_d, "gmvg"), (gmog, gmog_d, "gmog"),
                             (hgog, hgog_d, "hgog"), (mlaog, mlaog_d, "mlaog"), (gmbT, gmbT_d, "gmbT"),
                             (qagT, qagT_d, "qagT"), (kvagT, kvagT_d, "kvagT"), (qg, qg_d, "qg"), (kgn, kgn_d, "kgn"),
                             (kgpe, kgpe_d, "kgpe")):
            dma("sp", dst[:, :], src[l], [nm], "p_" + nm)
        dma("sp", uvg[:, :], gmwT_d[l], ["uvg"], "p_gmw")
        tt(wTm[:, :, :], v3(uvg[:, :], 128), bc_mid(maskT[:, :], 4), ALU.mult, ["uvg", "maskT"], ["wTm"])

    def gelu_psum(b, key):
        pk = ("ps", b)
        act(t1[:, :], ps[b][:, :], AF.Square, [pk], ["t1"])
        ts(t1[:, :], t1[:, :], 0.044715, 1.0, ALU.mult, ALU.add, ["t1"], ["t1"])
        tt(t1[:, :], t1[:, :], ps[b][:, :], ALU.mult, ["t1", pk], ["t1"])
        act(t2[:, :], t1[:, :], AF.Exp, ["t1"], ["t2"], scale=-1.5957691216057308)
        ts(t2[:, :], t2[:, :], 1.0, None, ALU.add, None, ["t2"], ["t2"])
        recip(t2[:, :], t2[:, :], ["t2"], ["t2"])
        tt(uvg[:, :], t2[:, :], ps[b][:, :], ALU.mult, ["t2", pk], ["uvg"])

    def head_norm(dst3, src3, nh, hd, srckeys, dstkeys, gain3, gkey):
        n = nh * hd
        act(v3(t1[:, 0:n], hd), src3, AF.Square, srckeys, ["t1"])
        red(sm[:, 0:nh], v3(t1[:, 0:n], hd), ["t1"], ["sm"])
        rsqrt_small(sm[:, 0:nh], hd, "sm")
        tt(v3(t1[:, 0:n], hd), src3, bc_last(sm[:, 0:nh], hd), ALU.mult, srckeys + ["sm"], ["t1"])
        tt(dst3, v3(t1[:, 0:n], hd), gain3, ALU.mult, ["t1", gkey], dstkeys)

    def layer(l, first_wg_loaded):
        load_params(l)
        stage("params")
        rms_to_act(g1T, "g1T", l)
        stage("norm1")
        if not first_wg_loaded:
            load_wg(l, 0, [(0, 512, 0)])
        def ct_blocks(ct):
            return [(1024 + ct * 128, 128, 0), (1280 + ct * 128, 128, 128), (512 + ct * 128, 128, 256), (768 + ct * 128, 128, 384)]
        load_wg(l, 1, ct_blocks(0))
        stage("wgload")
        def g0_gen(tb, hf_):
            cs_ = slice(hf_ * 1024, (hf_ + 1) * 1024)
            hs_ = slice(hf_ * 512, (hf_ + 1) * 512)
            psq = psbig[:, hf_ * 1024:(hf_ + 1) * 1024]
            pk2 = [("ps", 2 * hf_), ("ps", 2 * hf_ + 1)]
            by, bt = 4 + hf_, 6 + hf_
            T1, T2, T3 = G_t1[:, cs_], G_t2[:, cs_], G_t3[:, hs_]
            K1, K2, K3 = ("G_t1", hf_), ("G_t2", hf_), ("G_t3", hf_)
            U4, V4, KU, KV = u4[:, hs_], v4[:, hs_], ("u4", hf_), ("v4", hf_)
            VN, YB, KVN, KYB = vn4[:, 2 * hf_:2 * hf_ + 2, :], yb4[:, 2 * hf_:2 * hf_ + 2, :], ("vn4", hf_), ("yb4", hf_)
            SM, KSM = sm[:, hf_ * 8:hf_ * 8 + 8], ("smg", hf_)
            for jj in range(2):
                j = 2 * hf_ + jj
                tsl = slice((tb * 4 + j) * 128, (tb * 4 + j + 1) * 128)
                for c in range(8):
                    mm(ps[j][:, :], actT[:, c, tsl], wg[0][:, c, :], c == 0, c == 7, [AK(c, tb), ("wg", 0, 0)], [("ps", j)])
            yield
            act(T1, psq, AF.Square, pk2, [K1], scale=math.sqrt(0.044715))
            yield
            stt(T1, T1, 1.0, psq, ALU.add, ALU.mult, [K1] + pk2, [K1])
            yield
            act(T2, T1, AF.Exp, [K1], [K2], scale=-1.5957691216057308)
            yield
            act(T2, T2, AF.Ln, [K2], [K2], bias=1.0)
            yield
            act(T2, T2, AF.Exp, [K2], [K2], scale=-1.0)
            yield
            q3_ = psq.rearrange("p (j n) -> p j n", n=512)
            g3_ = T2.rearrange("p (j n) -> p j n", n=512)
            tt(v3(V4, 256), g3_[:, :, 256:512], q3_[:, :, 256:512], ALU.mult, [K2] + pk2, [KV])
            yield
            tt(v3(U4, 256), g3_[:, :, 0:256], q3_[:, :, 0:256], ALU.mult, [K2] + pk2, [KU])
            act(T3, V4, AF.Square, [KV], [K3])
            yield
            red(SM, v3(T3, 64), [K3], [KSM])
            yield
            act(SM, SM, AF.Ln, [KSM], [KSM], scale=1.0 / 64, bias=EPS)
            yield
            act(SM, SM, AF.Exp, [KSM], [KSM], scale=-0.5)
            yield
            tt(v3(T3, 64), v3(V4, 64), bc_last(SM, 64), ALU.mult, [KV, KSM], [K3])
            yield
            tt(VN, v3(T3, 256), bc_mid(gmvg[:, :], 2), ALU.mult, [K3, "gmvg"], [KVN])
            yield
            for jj in range(2):
                for h in range(4):
                    co = jj * 256 + h * 64
                    mm(ps[by][:, co:co + 64], wTm[:, h, :], VN[:, jj, h * 64:(h + 1) * 64], True, True, ["wTm", KVN], [("ps", by)])
            yield
            for jj in range(2):
                tt(v3(T3[:, jj * 256:(jj + 1) * 256], 64), v3(ps[by][:, jj * 256:(jj + 1) * 256], 64), bc_last(gmbT[:, 0:4], 64), ALU.add,
                   [("ps", by), "gmbT"], [K3])
            yield
            tt(T3, T3, U4, ALU.mult, [K3, KU], [K3])
            yield
            act(T2[:, 0:512], T3, AF.Square, [K3], [K2])
            yield
            red(SM, v3(T2[:, 0:512], 64), [K2], [KSM])
            yield
            act(SM, SM, AF.Ln, [KSM], [KSM], scale=1.0 / 64, bias=EPS)
            yield
            act(SM, SM, AF.Exp, [KSM], [KSM], scale=-0.5)
            yield
            tt(v3(T3, 64), v3(T3, 64), bc_last(SM, 64), ALU.mult, [K3, KSM], [K3])
            yield
            tt(YB, v3(T3, 256), bc_mid(gmog[:, :], 2), ALU.mult, [K3, "gmog"], [KYB])
            yield
            for jj in range(2):
                for c in range(2):
                    co = jj * 256 + c * 128
                    mm(ps[bt][:, co:co + 128], YB[:, jj, c * 128:(c + 1) * 128], ident[:, :], True, True, [KYB, "ident"], [("ps", bt)])
            yield
            pt4 = ps[bt].rearrange("p (j c t) -> p j c t", c=2, t=128)
            t0_ = tb * 512 + hf_ * 256
            for c in range(2):
                act(gmT[:, c, t0_:t0_ + 256].rearrange("p (j t) -> p j t", t=128), pt4[:, :, c, :], AF.Copy,
                    [("ps", bt)], [("gmT", tb * 4 + 2 * hf_ + jj) for jj in range(2)])
            yield

        memset(biasc[:, 7:8], 0.0, ["sm", ("smg", 0), ("smg", 1)])
        for tb in range(NB):
            run_lockstep([g0_gen(tb, 0), g0_gen(tb, 1)])
        memset(biasc[:, 7:8], 0.0, ["sm", ("smg", 0), ("smg", 1)])
        stage("G0")
        for ct in range(2):
            wb = 1 - ct
            if ct == 0:
                load_wg(l, 0, ct_blocks(1))
            else:
                load_wg(l, 1, [(1536, 384, 0), (1920, 32, 384)])
            for t in range(NT):
                tb = t // 4
                tsl = slice(t * 128, (t + 1) * 128)
                b = nb()
                for c in range(8):
                    mm(ps[b][:, 0:256], actT[:, c, tsl], wg[wb][:, c, 0:256], c == 0, c == 7,
                       [AK(c, tb), ("wg", wb, 0), ("wg", wb, 1)], [("ps", b)])
                pk = ("ps", b)
                act(hgi[:, t, :], ps[b][:, 0:128], AF.Copy, [pk], [("hgi", t)])
                TS, KS = (t2, "t2") if t % 2 == 0 else (tmpB, "tmpB")
                act(TS[:, 0:128], ps[b][:, 128:256], AF.Exp, [pk], [KS], scale=-1.0)
                act(TS[:, 0:128], TS[:, 0:128], AF.Ln, [KS], [KS], bias=1.0)
                act(TS[:, 0:128], TS[:, 0:128], AF.Exp, [KS], [KS], scale=-1.0)
                tt(sg[:, t, :], TS[:, 0:128], ps[b][:, 128:256], ALU.mult, [KS, pk], [("sg", t)])
            for tb in range(NB):
                sl = slice(tb * 512, (tb + 1) * 512)
                bq, bf_ = nb(), nb()
                for c in range(8):
                    mm(ps[bq][:, :], wg[wb][:, c, 256:384], actT[:, c, sl], c == 0, c == 7, [AK(c, tb), ("wg", wb, 2)], [("ps", bq)])
                for c in range(8):
                    mm(ps[bf_][:, :], wg[wb][:, c, 384:512], actT[:, c, sl], c == 0, c == 7, [AK(c, tb), ("wg", wb, 3)], [("ps", bf_)])
                act(t2[:, :], ps[bq][:, :], AF.Exp, [("ps", bq)], ["t2"], scale=-1.0)
                act(t2[:, :], t2[:, :], AF.Ln, ["t2"], ["t2"], bias=1.0)
                act(t2[:, :], t2[:, :], AF.Exp, ["t2"], ["t2"], scale=-1.0)
                tt(qT[:, sl], t2[:, :], ps[bq][:, :], ALU.mult, ["t2", ("ps", bq)], [("qT", tb)])
                act(t1[:, :], ps[bf_][:, :], AF.Exp, [("ps", bf_)], ["t1"], scale=-1.0)
                act(t1[:, :], t1[:, :], AF.Ln, ["t1"], ["t1"], bias=1.0)
                act(t1[:, :], t1[:, :], AF.Exp, ["t1"], ["t1"], scale=-1.0)
                li = ct * L + l
                ts(t1[:, :], t1[:, :], oml[:, li:li + 1], lbv[:, li:li + 1], ALU.mult, ALU.add, ["t1", "oml", "lbv"], ["t1"])
                act(uvg[:, :], t1[:, :], AF.Ln, ["t1"], ["uvg"])
                ts(kT[:, sl], t1[:, :], -1.0, 1.0, ALU.mult, ALU.add, ["t1"], [("kT", tb)])
                for j in range(4):
                    ch = tb * 4 + j
                    P.add("dve", (lambda o_, d1: (lambda e: e.tensor_tensor_scan(out=o_, data0=ones_f[:, :], data1=d1, initial=0.0,
                                                                                  op0=ALU.mult, op1=ALU.add)))(bT[:, ch, :], uvg[:, j * 128:(j + 1) * 128]),
                          ["uvg", "ones_f"], [("bT", ch)])
            stage("proj%d" % ct)
            hgrn(l, ct)
            stage("ct%d" % ct)
        g3(l)
        stage("g3")
        mla(l)
        stage("mla")
        if debug and l == 0:
            for c in range(8):
                if c < 2:
                    src, rk = gmT[:, c, :], [("gmT", t) for t in range(NT)]
                elif c < 4:
                    src, rk = hgT[:, c - 2, :], [("hgT", c - 2, tb) for tb in range(NB)]
                else:
                    src, rk = actT[:, c, :], [AK(c, tb) for tb in range(NB)]
                dma("sp", dbg_d[c * 128:(c + 1) * 128, :], src, [("dbg", c)], "dbg%d" % c, reads=rk)
        wout_phase(l)
        stage("wout")
        rms_to_act(g2T, "g2T", l)
        stage("norm2")
        ffn(l)

    def hgrn(l, ct):
        allb = [("bT", ch) for ch in range(NT)]
        for hf in range(2):
            cs = slice(hf * 8, hf * 8 + 8)
            fs = slice(hf * 1024, hf * 1024 + 1024)
            bks = allb[hf * 8:hf * 8 + 8]
            tt(dtmp[:, :, :], bT[:, cs, :], bc_last(bT[:, cs, 127], 128), ALU.subtract, bks, ["dtmp"])
            act(Etmp[:, :, :], dtmp[:, :, :], AF.Exp, ["dtmp"], ["Etmp"], scale=-1.0)
            tt(KendT[:, fs], kT[:, fs], Etmp[:, :, :].rearrange("p c t -> p (c t)"), ALU.mult,
               ["Etmp", ("kT", 2 * hf), ("kT", 2 * hf + 1)], [("KendT", hf)])
        stage("hk")
        act(eb127[:, :], bT[:, :, 127], AF.Exp, allb, ["eb127"])
        for c4 in range(4):
            b = nb()
            for j in range(4):
                ch = c4 * 4 + j
                mm(ps[b][:, j * 128:(j + 1) * 128], KendT[:, ch * 128:(ch + 1) * 128], ident[:, :], True, True,
                   [("KendT", ch // 8), "ident"], [("ps", b)])
            act(Kend_tok[:, c4 * 4:(c4 + 1) * 4, :], v3(ps[b][:, :], 128), AF.Copy, [("ps", b)], [("Ktok", c4)])
        stage("htp")
        ub = [nbl(), nbl()]
        for ch in range(NT):
            b = ub[ch // 8]
            j = ch % 8
            for hh in range(2):
                r = slice(hh * 64, hh * 64 + 64)
                mm(ps[b][r, j * 64:(j + 1) * 64], Kend_tok[:, ch, r], hgi[:, ch, r], True, True,
                   [("Ktok", ch // 4), ("hgi", ch)], [("ps", b)])
        stage("hU")
        memset(Sall[:, 0, :], 0.0, ["Sall"])
        for ch in range(NT):
            b = ub[ch // 8]
            j = ch % 8
            stt(Sall[:, ch + 1, :], Sall[:, ch, :], eb127[:, ch:ch + 1], ps[b][:, j * 64:(j + 1) * 64], ALU.mult, ALU.add,
                ["Sall", "eb127", ("ps", b)], ["Sall"])
        cp(Sbf[:, :, :], Sall[:, 0:NT, :], ["Sall"], ["Sbf"])
        stage("hchain")
        memset(KA[:, :, 64:128], 0.0, ["KA"])
        for hf in range(2):
            cs = slice(hf * 8, hf * 8 + 8)
            fs = slice(hf * 1024, hf * 1024 + 1024)
            bks = allb[hf * 8:hf * 8 + 8]
            kks = [("kT", 2 * hf), ("kT", 2 * hf + 1)]
            qks = [("qT", 2 * hf), ("qT", 2 * hf + 1)]
            q3 = qT[:, fs].rearrange("p (c t) -> p c t", t=128)
            k3 = kT[:, fs].rearrange("p (c t) -> p c t", t=128)
            act(Etmp[:, :, :], bT[:, cs, :], AF.Exp, bks, ["Etmp"])
            tt(QA[:, :, :], q3, Etmp[:, :, :], ALU.mult, ["Etmp"] + qks, ["QA"])
            tt(dtmp[:, :, :], bT[:, cs, :], bc_last(bT[:, cs, 63], 128), ALU.subtract, bks, ["dtmp"])
            act(Etmp[:, :, 64:128], dtmp[:, :, 64:128], AF.Exp, ["dtmp"], ["Etmp"])
            tt(QB[:, :, :], q3[:, :, 64:128], Etmp[:, :, 64:128], ALU.mult, ["Etmp"] + qks, ["QB"])
            act(Etmp[:, :, :], dtmp[:, :, :], AF.Exp, ["dtmp"], ["Etmp"], scale=-1.0)
            tt(KB[:, :, :], k3, Etmp[:, :, :], ALU.mult, ["Etmp"] + kks, ["KB"])
            act(Etmp[:, :, 0:64], bT[:, cs, 0:64], AF.Exp, bks, ["Etmp"], scale=-1.0)
            tt(KA[:, :, 0:64], k3[:, :, 0:64], Etmp[:, :, 0:64], ALU.mult, ["Etmp"] + kks, ["KA"])
            stage("hprep")

            def hbatch_gen(g4, p):
                ST, KST = (sT4, "sT4") if p == 0 else (sT4_b, "sT4_b")
                T1, K1 = (t1, "t1") if p == 0 else (uvg, "uvg")
                T2, K2 = (t2, "t2") if p == 0 else (tmpA, "tmpA")
                SM, KSM = (sm, "sm") if p == 0 else (sm2, "sm2")
                YB, KYB = (ybh, "ybh") if p == 0 else (ybh_b, "ybh_b")
                ch0 = hf * 8 + g4 * 4
                tb = ch0 // 4
                bsb = [nb(), nb()]
                for j in range(4):
                    cc = g4 * 4 + j
                    for hh in range(2):
                        r = slice(hh * 64, hh * 64 + 64)
                        bs_ = bsb[hh]
                        mm(ps[bs_][:, j * 128:j * 128 + 64], KA[r, cc, :], QA[r, cc, 0:64], True, True, ["KA", "QA"], [("ps", bs_)])
                        mm(ps[bs_][:, j * 128 + 64:j * 128 + 128], KB[r, cc, :], QB[r, cc, :], True, True, ["KB", "QB"], [("ps", bs_)])
                yield
                for hh in range(2):
                    bs_ = bsb[hh]
                    tt(ST[:, hh, :, :], v3(ps[bs_][:, :], 128), bc_mid(maskT[:, :], 4), ALU.mult, [("ps", bs_), "maskT"], [KST])
                    yield
                bo = nb()
                for j in range(4):
                    cc = g4 * 4 + j
                    ch = ch0 + j
                    for hh in range(2):
                        r = slice(hh * 64, hh * 64 + 64)
                        co = j * 128 + hh * 64
                        mm(ps[bo][:, co:co + 64], ST[:, hh, j, :], hgi[:, ch, r], True, False, [KST, ("hgi", ch)], [("ps", bo)])
                        mm(ps[bo][:, co:co + 64], QA[r, cc, :], Sbf[r, ch, :], False, True, ["QA", "Sbf"], [("ps", bo)])
                yield
                pk = ("ps", bo)
                act(T1[:, :], ps[bo][:, :], AF.Square, [pk], [K1])
                yield
                red(SM[:, 0:8], v3(T1[:, :], 64), [K1], [KSM])
                yield
                act(SM[:, 0:8], SM[:, 0:8], AF.Ln, [KSM], [KSM], scale=1.0 / 64, bias=EPS)
                yield
                act(SM[:, 0:8], SM[:, 0:8], AF.Exp, [KSM], [KSM], scale=-0.5)
                yield
                tt(v3(T1[:, :], 64), v3(ps[bo][:, :], 64), bc_last(SM[:, 0:8], 64), ALU.mult, [pk, KSM], [K1])
                yield
                tt(v3(T2[:, :], 128), v3(T1[:, :], 128), bc_mid(hgog[:, ct * 128:(ct + 1) * 128], 4), ALU.mult, [K1, "hgog"], [K2])
                yield
                tt(YB[:, :, :], v3(T2[:, :], 128), sg[:, ch0:ch0 + 4, :], ALU.mult, [K2] + [("sg", ch0 + j) for j in range(4)], [KYB])
                yield
                btp = nbl()
                for j in range(4):
                    mm(ps[btp][:, j * 128:(j + 1) * 128], YB[:, j, :], ident[:, :], True, True, [KYB, "ident"], [("ps", btp)])
                yield
                act(hgT[:, ct, tb * 512:(tb + 1) * 512], ps[btp][:, :], AF.Copy, [("ps", btp)], [("hgT", ct, tb)])
                yield

            run_lockstep([hbatch_gen(0, 0), hbatch_gen(1, 1)])

    def g3(l):
        wb = 1
        dma("pool", wuq[:, :, :], wuq_d[l].rearrange("(c p) n -> p c n", p=128), ["wuq"], "wuq")
        dma("pool", wukv[:, :], wukv_d[l], ["wukv"], "wukv")
        for tb in range(NB):
            sl = slice(tb * 512, (tb + 1) * 512)
            bb = []
            for j in range(3):
                b = nb()
                bb.append(b)
                for c in range(8):
                    mm(ps[b][:, :], wg[wb][:, c, j * 128:(j + 1) * 128], actT[:, c, sl], c == 0, c == 7,
                       [AK(c, tb), ("wg", wb, 0)], [("ps", b)])
                act(cqraw[:, j, :], ps[b][:, :], AF.Copy, [("ps", b)], ["cqraw"])
                act(sq4[:, j, :], ps[b][:, :], AF.Square, [("ps", b)], ["sq4"])
            b1, b2 = nb(), nb()
            mm(ps[b1][:, :], ones_bf[:, :], sq4[:, 0, :], True, False, ["sq4", "ones_bf"], [("ps", b1)])
            mm(ps[b1][:, :], ones_bf[:, :], sq4[:, 1, :], False, True, ["sq4", "ones_bf"], [("ps", b1)])
            mm(ps[b2][:, :], ones_bf[:, :], sq4[:, 2, :], True, True, ["sq4", "ones_bf"], [("ps", b2)])
            act(tmpA[:, :], ps[b1][:, :], AF.Ln, [("ps", b1)], ["tmpA"], scale=1.0 / 256, bias=EPS)
            act(tmpB[:, :], tmpA[:, :], AF.Exp, ["tmpA"], ["tmpB"], scale=-0.5)
            for j in range(2):
                stt(cqnT[:, j, sl], cqraw[:, j, :], qagT[:, j:j + 1], tmpB[:, :], ALU.mult, ALU.mult,
                    ["cqraw", "tmpB", "qagT"], [("cqnT", tb)])
            act(tmpA[:, :], ps[b2][:, :], AF.Ln, [("ps", b2)], ["tmpA"], scale=1.0 / 128, bias=EPS)
            act(tmpB[:, :], tmpA[:, :], AF.Exp, ["tmpA"], ["tmpB"], scale=-0.5)
            stt(ckvnT[:, sl], cqraw[:, 2, :], kvagT[:, 0:1], tmpB[:, :], ALU.mult, ALU.mult,
                ["cqraw", "tmpB", "kvagT"], [("ckvnT", tb)])
        for t in range(NT):
            tb = t // 4
            tsl = slice(t * 128, (t + 1) * 128)
            b = nb()
            for c in range(8):
                mm(ps[b][:, 0:32], actT[:, c, tsl], wg[wb][:, c, 384:416], c == 0, c == 7, [AK(c, tb), ("wg", wb, 1)], [("ps", b)])
            act(t1[:, 0:32], ps[b][:, 0:32], AF.Square, [("ps", b)], ["t1"])
            red(sspe[:, t:t + 1], t1[:, 0:32], ["t1"], ["sspe"])
            tt(rkt[:, :], ps[b][:, 0:32], kgpe[:, :], ALU.mult, [("ps", b), "kgpe"], ["rkt"])
            rope(rkb[:, :, :], rkt[:, :].unsqueeze(1), 1, t, "rkt", "rkb")
            b2 = nb()
            mm(ps[b2][64:96, 0:128], rkb[:, 0, :], ident[:, :], True, True, ["rkb", "ident"], [("ps", b2)])
            act(krT[64:96, tsl], ps[b2][64:96, 0:128], AF.Copy, [("ps", b2)], [("krT", t)])

    def rope(dst3, src3, nh, t, srck, dstk):
        cs = bc_mid(cosT[:, t, :], nh)
        sn = bc_mid(sinT[:, t, :], nh)
        tt(ra[:, 0:nh, :], src3[:, :, 0:16], cs, ALU.mult, [srck], ["ra"])
        tt(rb[:, 0:nh, :], src3[:, :, 16:32], sn, ALU.mult, [srck], ["rb"])
        tt(dst3[:, 0:nh, 0:16], ra[:, 0:nh, :], rb[:, 0:nh, :], ALU.subtract, ["ra", "rb"], [dstk])
        tt(ra[:, 0:nh, :], src3[:, :, 16:32], cs, ALU.mult, [srck], ["ra"])
        tt(rb[:, 0:nh, :], src3[:, :, 0:16], sn, ALU.mult, [srck], ["rb"])
        tt(dst3[:, 0:nh, 16:32], ra[:, 0:nh, :], rb[:, 0:nh, :], ALU.add, ["ra", "rb", dstk], [dstk])

    def mla(l):
        allkr = [("krT", t) for t in range(NT)]
        for hf in range(2):
            def kside_gen():
                for hh in range(4):
                    h = hf * 4 + hh
                    for tb in range(NB):
                        sl = slice(tb * 512, (tb + 1) * 512)
                        b = nb()
                        mm(ps[b][0:64, :], wukv[:, h * 64:(h + 1) * 64], ckvnT[:, sl], True, True, ["wukv", ("ckvnT", tb)], [("ps", b)])
                        yield
                        ts(KT[0:64, hh, sl], ps[b][0:64, :], kgn[0:64, 0:1], None, ALU.mult, None, [("ps", b), "kgn"], [("KT", hh, tb)])
                        yield
                        yield
                        yield

            kgen = kside_gen()
            if hf == 0:
                for tb in range(NB):
                    sl = slice(tb * 512, (tb + 1) * 512)
                    act(KT[64:96, :, sl], bc_mid(krT[64:96, sl], 4), AF.Copy, allkr, [("KT", hh, tb) for hh in range(4)])
            memset(Vx[:, :, :, 64:65], 1.0, ["Vx1"])
            def tile_gen(t, p):
                T1, K1 = (t1, "t1") if p == 0 else (t2, "t2")
                SM, SMK = (sm, "sm") if p == 0 else (smb, "smb")
                SM2, SM2K = (sm2, "sm2") if p == 0 else (sm2b, "sm2b")
                QTMP, QTK = (qtmp, "qtmp") if p == 0 else (qtmp_b, "qtmp_b")
                XR, XRK = (xr, "xr") if p == 0 else (xr_b, "xr_b")
                QF, QFK = (qf, "qf") if p == 0 else (qf_b, "qf_b")
                RA, RAK = (ra, "ra") if p == 0 else (ra_b, "ra_b")
                RB, RBK = (rb, "rb") if p == 0 else (rb_b, "rb_b")
                tb = t // 4
                tsl = slice(t * 128, (t + 1) * 128)
                b = nb()
                mm(ps[b][:, 0:256], ckvnT[:, tsl], wukv[:, hf * 256:(hf + 1) * 256], True, True, [("ckvnT", tb), "wukv"], [("ps", b)])
                mm(ps[b][:, 256:512], ckvnT[:, tsl], wukv[:, 512 + hf * 256:512 + (hf + 1) * 256], True, True,
                   [("ckvnT", tb), "wukv"], [("ps", b)])
                bq = nb()
                for j in range(2):
                    mm(ps[bq][:, 0:384], cqnT[:, j, tsl], wuq[:, j, hf * 384:(hf + 1) * 384], j == 0, j == 1,
                       [("cqnT", tb), "wuq"], [("ps", bq)])
                yield
                pk = ("ps", b)
                act(T1[:, 0:256], ps[b][:, 0:256], AF.Square, [pk], [K1])
                yield
                red(SM2[:, 0:4], v3(T1[:, 0:256], 64), [K1], [SM2K])
                yield
                ts(SM2[:, 0:4], SM2[:, 0:4], sspe[:, t:t + 1], None, ALU.add, None, [SM2K, "sspe"], [SM2K])
                yield
                act(SM2[:, 0:4], SM2[:, 0:4], AF.Ln, [SM2K], [SM2K], scale=1.0 / 96, bias=EPS)
                yield
                act(rks[:, t, hf * 4:(hf + 1) * 4], SM2[:, 0:4], AF.Exp, [SM2K], [("rks", t)], scale=-0.5, bias=-0.5 * math.log(96.0))
                act(Vx[:, t, :, 0:64], v3(ps[b][:, 256:512], 64), AF.Copy, [pk], [("Vx", t)])
                yield
                qk = ("ps", bq)
                q3 = v3(ps[bq][:, 0:384], 96)
                act(T1[:, 0:384], ps[bq][:, 0:384], AF.Square, [qk], [K1])
                yield
                red(SM[:, 0:4], v3(T1[:, 0:384], 96), [K1], [SMK])
                yield
                act(SM[:, 0:4], SM[:, 0:4], AF.Ln, [SMK], [SMK], scale=1.0 / 96, bias=EPS)
                yield
                act(SM[:, 0:4], SM[:, 0:4], AF.Exp, [SMK], [SMK], scale=-0.5)
                yield
                tt(QTMP[:, :, :], q3, bc_last(SM[:, 0:4], 96), ALU.mult, [qk, SMK], [QTK])
                yield
                tt(QF[:, :, 0:64], QTMP[:, :, 0:64], bc_mid(qg[:, 0:64], 4), ALU.mult, [QTK, "qg"], [QFK])
                yield
                tt(XR[:, :, :], QTMP[:, :, 64:96], bc_mid(qg[:, 64:96], 4), ALU.mult, [QTK, "qg"], [XRK])
                yield
                cs = bc_mid(cosT[:, t, :], 4)
                sn = bc_mid(sinT[:, t, :], 4)
                tt(RA[:, :, :], XR[:, :, 0:16], cs, ALU.mult, [XRK], [RAK])
                yield
                tt(RB[:, :, :], XR[:, :, 16:32], sn, ALU.mult, [XRK], [RBK])
                yield
                tt(QF[:, :, 64:80], RA[:, :, :], RB[:, :, :], ALU.subtract, [RAK, RBK], [QFK])
                yield
                tt(RA[:, :, :], XR[:, :, 16:32], cs, ALU.mult, [XRK], [RAK])
                yield
                tt(RB[:, :, :], XR[:, :, 0:16], sn, ALU.mult, [XRK], [RBK])
                yield
                tt(QF[:, :, 80:96], RA[:, :, :], RB[:, :, :], ALU.add, [RAK, RBK, QFK], [QFK])
                yield
                bt_ = nb()
                for hh in range(4):
                    mm(ps[bt_][0:96, hh * 128:(hh + 1) * 128], QF[:, hh, :], ident[:, :], True, True, [QFK, "ident"], [("ps", bt_)])
                yield
                act(QT[0:96, :, tsl], v3(ps[bt_][0:96, :], 128), AF.Copy, [("ps", bt_)], [("QT", t)])
                yield

            kalive = True
            for pair in range(NT // 2):
                gens = [tile_gen(2 * pair, 0), tile_gen(2 * pair + 1, 1)]
                alive = [True, True]
                while any(alive):
                    for gi in range(2):
                        if alive[gi]:
                            try:
                                next(gens[gi])
                            except StopIteration:
                                alive[gi] = False
                    if kalive:
                        try:
                            next(kgen)
                        except StopIteration:
                            kalive = False
            for _ in kgen:
                pass
            units = [(hh, qb, kt) for hh in range(4) for qb in range(NB) for kt in range(4 * qb + 4)]
            sbank = {}
            bo_of = {}

            def emit_S(u):
                hh, qb, kt = u
                j0 = max(0, kt - 4 * qb)
                c0 = j0 * 128
                bs_ = nb()
                sbank[u] = bs_
                mm(ps[bs_][:, c0:512], KT[0:96, hh, kt * 128:(kt + 1) * 128], QT[0:96, hh, qb * 512 + c0:(qb + 1) * 512],
                   True, True, [("KT", hh, kt // 4)] + [("QT", qb * 4 + j) for j in range(j0, 4)], [("ps", bs_)])

            def emit_PV(u, pb):
                hh, qb, kt = u
                h = hf * 4 + hh
                j0 = max(0, kt - 4 * qb)
                c0 = j0 * 128
                bs_ = sbank.pop(u)
                if (hh, qb) not in bo_of:
                    bo_of[(hh, qb)] = nbl()
                    mm(ps[bo_of[(hh, qb)]][:, 0:260], zeros_bf[:, 0:128], zeros_bf[:, 0:260], True, False,
                       ["zeros_bf"], [("ps", bo_of[(hh, qb)])])
                bo = bo_of[(hh, qb)]
                act(pT[pb][:, c0:512], ps[bs_][:, c0:512], AF.Exp, [("ps", bs_), ("rks", kt)], [("pT", pb)],
                    scale=rks[:, kt, h:h + 1])
                if kt >= 4 * qb:
                    tt(pT[pb][:, c0:c0 + 128], pT[pb][:, c0:c0 + 128], mask_bf[:, :], ALU.mult,
                       [("pT", pb), "mask_bf"], [("pT", pb)])
                for j in range(j0, 4):
                    mm(ps[bo][:, j * 65:(j + 1) * 65], pT[pb][:, j * 128:(j + 1) * 128], Vx[:, kt, hh, :],
                       False, kt == 4 * qb + 3 and j == 3, [("pT", pb), ("Vx", kt), "Vx1"], [("ps", bo)])

            def epilogue(hh, qb):
                h = hf * 4 + hh
                bo = bo_of[(hh, qb)]
                o3 = ps[bo][:, 0:260].rearrange("p (j d) -> p j d", d=65)
                recip(sm2[:, 0:4].unsqueeze(2), o3[:, :, 64:65], [("ps", bo)], ["sm2"])
                tt(v3(t2[:, 0:256], 64), o3[:, :, 0:64], bc_last(sm2[:, 0:4], 64), ALU.mult, [("ps", bo), "sm2"], ["t2"])
                tt(t1[:, 0:256], t2[:, 0:256], t2[:, 0:256], ALU.mult, ["t2"], ["t1"])
                red(sm[:, 0:4], v3(t1[:, 0:256], 64), ["t1"], ["sm"])
                rsqrt_small(sm[:, 0:4], 64, "sm")
                tt(v3(t1[:, 0:256], 64), v3(t2[:, 0:256], 64), bc_last(sm[:, 0:4], 64), ALU.mult, ["t2", "sm"], ["t1"])
                tt(v3(yb[:, :], 64), v3(t1[:, 0:256], 64), bc_mid(mlaog[:, h * 64:(h + 1) * 64], 4), ALU.mult, ["t1", "mlaog"], ["yb"])

                def transposes():
                    bt_ = nb()
                    r = slice((h % 2) * 64, (h % 2) * 64 + 64)
                    for j in range(4):
                        mm(ps[bt_][r, j * 128:(j + 1) * 128], yb[:, j * 64:(j + 1) * 64], ident[:, :], True, True,
                           ["yb", "ident"], [("ps", bt_)])
                    cp(actT[r, 4 + h // 2, qb * 512:(qb + 1) * 512], ps[bt_][r, :], [("ps", bt_)], [AK(4 + h // 2, qb)])
                return transposes

            pend_el, pend_tp = None, None
            emit_S(units[0])
            emit_S(units[1])
            for i, u in enumerate(units):
                if i + 2 < len(units):
                    emit_S(units[i + 2])
                emit_PV(u, i % 4)
                hh, qb, kt = u
                if kt == 4 * qb + 3:
                    if pend_tp is not None:
                        pend_tp()
                        pend_tp = None
                    if pend_el is not None:
                        pend_tp = epilogue(*pend_el)
                    pend_el = (hh, qb)
            if pend_tp is not None:
                pend_tp()
            epilogue(*pend_el)()

    def wout_phase(l):
        for hlf in range(2):
            dma("pool", Wout[:, hlf * 4:(hlf + 1) * 4, :],
                wout_d[l][hlf * 512:(hlf + 1) * 512, :].rearrange("(c p) n -> p c n", p=128), [("Wout", hlf)], "Wout%d" % hlf)
        for oc in range(8):
            for tb in range(NB):
                sl = slice(tb * 512, (tb + 1) * 512)
                b = nb()
                for c in range(8):
                    if c < 2:
                        src, rk = gmT[:, c, sl], [("gmT", tb * 4 + j) for j in range(4)]
                    elif c < 4:
                        src, rk = hgT[:, c - 2, sl], [("hgT", c - 2, tb)]
                    else:
                        src, rk = actT[:, c, sl], [AK(c, tb)]
                    mm(ps[b][:, :], Wout[:, c, oc * 128:(oc + 1) * 128], src, c == 0, c == 7, rk + [("Wout", c // 4)], [("ps", b)])
                tt(xT[:, oc, sl], xT[:, oc, sl], ps[b][:, :], ALU.add, [XK(oc, tb), ("ps", b)], [XK(oc, tb)])

    def ffn(l):
        NG = 8

        def load(g):
            bf = g % 2
            dma("pool", W1g[bf][:, :, :], wff1_d[l][:, g * 512:(g + 1) * 512].rearrange("(c p) n -> p c n", p=128),
                [("W1g", bf)], "W1g%d" % bf)
            dma("pool", W2g[bf][:, :, :], wff2_d[l][g * 512:(g + 1) * 512, :].rearrange("(j p) n -> p j n", p=128),
                [("W2g", bf)], "W2g%d" % bf)

        def up(g):
            bf = g % 2
            for j in range(4):
                for tb in range(NB):
                    sl = slice(tb * 512, (tb + 1) * 512)
                    b = nb()
                    for c in range(8):
                        mm(ps[b][:, :], W1g[bf][:, c, j * 128:(j + 1) * 128], actT[:, c, sl], c == 0, c == 7,
                           [AK(c, tb), ("W1g", bf)], [("ps", b)])
                    act(relu_a[:, :], ps[b][:, :], AF.Relu, [("ps", b)], ["t1"])
                    act(hT[bf][:, j, sl], relu_a[:, :], AF.Square, ["t1"], [("hT", bf, j, tb)])

        def down(g):
            bf = g % 2
            for oc in range(8):
                for tb in range(NB):
                    sl = slice(tb * 512, (tb + 1) * 512)
                    b = nb()
                    for j in range(4):
                        mm(ps[b][:, :], W2g[bf][:, j, oc * 128:(oc + 1) * 128], hT[bf][:, j, sl], j == 0, j == 3,
                           [("hT", bf, j, tb), ("W2g", bf)], [("ps", b)])
                    tt(xT[:, oc, sl], xT[:, oc, sl], ps[b][:, :], ALU.add, [XK(oc, tb), ("ps", b)], [XK(oc, tb)])

        load(0)
        load(1)
        up(0)
        for g in range(NG):
            if g + 1 < NG:
                up(g + 1)
            down(g)
            if g + 2 < NG:
                load(g + 2)

    try:
        stage("setup")
        for l in range(n_layers):
            layer(l, False)
    except _Stop:
        pass

    for c in range(8):
        dma("sp", outT_d[c * 128:(c + 1) * 128, :], xT[:, c, :], [("out", c)], "o%d" % c,
            reads=[XK(c, tb) for tb in range(NB)])
    P.add("sp", None, reads=[("out", c) for c in range(8)] + ([("dbg", c) for c in range(8)] if debug else []))
    P.emit(nc)
    for al in nc.allocations:
        for ml in (getattr(al, "memorylocations", None) or []):
            if str(ml.type).endswith("SB") and ml.addr >= SB0 and not ml.name.startswith("const-") and ml.name not in _mine and ml.name.rsplit("_", 1)[0] not in _mine:
                raise RuntimeError("unexpected SBUF allocation %s @%d" % (ml.name, ml.addr))
            if str(ml.type).endswith("SB") and ml.name.startswith("const-") and ml.addr >= SB0:
                raise RuntimeError("late const allocation %s @%d overlaps manual map" % (ml.name, ml.addr))
    return nc


def _host_inputs(inp):
    f = lambda a: np.ascontiguousarray(np.asarray(a, dtype=np.float32))
    rep = lambda v: np.ascontiguousarray(np.broadcast_to(np.asarray(v, np.float32)[:, None, :], (L, 128, v.shape[-1])))
    half = 16
    invf = (10000.0 ** (-np.arange(half, dtype=np.float32) / half)).astype(np.float32)
    wukv = np.asarray(inp["mla_w_ukv"], np.float32).reshape(L, 128, 8, 2, 64)
    wukv_p = np.concatenate([wukv[:, :, :, 0, :].reshape(L, 128, 512), wukv[:, :, :, 1, :].reshape(L, 128, 512)], axis=-1)
    kg = np.asarray(inp["mla_k_gain"], np.float32)
    kgn = np.zeros((L, 128, 1), np.float32)
    kgn[:, 0:64, 0] = kg[:, 0:64]
    kgn[:, 64:128, 0] = kg[:, 0:64]
    shared = {
        "invf": np.ascontiguousarray(np.broadcast_to(invf[None, :], (128, 16))),
        "ident": np.eye(128, dtype=np.float32),
        "maskT": np.triu(np.ones((128, 128), np.float32)),
        "g1T": f(np.asarray(inp["norm1_gain"]).reshape(L, 8, 128).transpose(0, 2, 1)),
        "g2T": f(np.asarray(inp["norm2_gain"]).reshape(L, 8, 128).transpose(0, 2, 1)),
        "w_in": f(inp["w_in"]),
        "gmvg": rep(inp["gm_v_gain"]),
        "gmog": rep(inp["gm_out_gain"]),
        "hgog": rep(inp["hg_out_gain"]),
        "mlaog": rep(inp["mla_out_gain"]),
        "gmwT": f(np.asarray(inp["gm_w_s"]).transpose(0, 3, 1, 2).reshape(L, 128, 512)),
        "gmbT": f(np.asarray(inp["gm_b_s"]).transpose(0, 2, 1)),
        "hglbT": f(np.asarray(inp["hg_lower_bound"]).reshape(L, 2, 128).transpose(2, 1, 0).reshape(128, 2 * L)),
        "qagT": f(np.asarray(inp["mla_q_a_gain"]).reshape(L, 2, 128).transpose(0, 2, 1)),
        "kvagT": f(np.asarray(inp["mla_kv_a_gain"]).reshape(L, 1, 128).transpose(0, 2, 1)),
        "wuq": f(inp["mla_w_uq"]),
        "wukv": f(wukv_p),
        "qg": rep(inp["mla_q_gain"]),
        "kgn": kgn,
        "kgpe": rep(kg[:, 64:96]),
        "w_out": f(inp["w_out"]),
        "w_ff1": f(inp["w_ff1"]),
        "w_ff2": f(inp["w_ff2"]),
    }
    return shared


def kernel(**inputs):
    x = np.asarray(inputs["x"], np.float32)
    pos = np.asarray(inputs["positions"], np.int32)
    B = x.shape[0]
    shared = _host_inputs(inputs)
    in_maps = []
    for b in range(B):
        m = dict(shared)
        m["xT"] = np.ascontiguousarray(x[b].T)
        m["pos"] = np.ascontiguousarray(pos[b].reshape(NT, 128).T)
        in_maps.append(m)
    nc = build()
    res = run_bass_kernel_spmd(nc, in_maps, core_ids=list(range(B)))
    out = np.stack([np.asarray(res.results[b]["outT"], np.float32).T for b in range(B)], axis=0)
    return np.ascontiguousarray(out)
```

```python
from contextlib import ExitStack
import math
import numpy as np
import concourse.bass as bass
import concourse.mybir as mybir
from concourse.bass_utils import run_bass_kernel_spmd

F32 = mybir.dt.float32
BF16 = mybir.dt.bfloat16
I32 = mybir.dt.int32
ALU = mybir.AluOpType
AF = mybir.ActivationFunctionType
AX = mybir.AxisListType
ENGS = ("pe", "act", "dve", "pool", "sp")

L = 4
S = 2048
D = 1024
NT = 16
NB = 4
EPS = 1e-6
DIN = 1952
DFF = 4096


class Op:
    __slots__ = ("eng", "fn", "deps", "signal", "semval", "slot", "dval")


class Prog:
    def __init__(self):
        self.ops = {e: [] for e in ENGS}
        self.slot_cnt = {}
        self.lastw = {}
        self.readers = {}
        self.ranges = {}
        self.touch = {}
        self._ov = {}

    @staticmethod
    def kbuf(k):
        if isinstance(k, tuple):
            n = k[0]
            if n in ("wg", "pT", "hT", "W1g", "W2g"):
                return n + str(k[1])
            if n == "Ktok":
                return "Kend_tok"
            return n
        if k == "Vx1":
            return "Vx"
        return k

    def overlaps(self, b):
        r = self._ov.get(b)
        if r is None:
            s0, e0 = self.ranges[b]
            r = [y for y, (s1, e1) in self.ranges.items() if y != b and s1 < e0 and s0 < e1]
            self._ov[b] = r
        return r

    def add(self, eng, fn, reads=(), writes=(), slot=None, war=()):
        o = Op()
        o.eng, o.fn, o.signal, o.semval, o.slot, o.dval = eng, fn, False, 0, slot, 0
        d = []
        wb = {self.kbuf(k) for k in writes}
        for b in wb:
            if b in self.ranges:
                for y in self.overlaps(b):
                    t = self.touch.get(y)
                    if t:
                        d.extend(t["last"].values())
                        d.extend(t["dma"])
        for k in list(reads) + list(writes):
            b = self.kbuf(k)
            if b in self.ranges:
                t = self.touch.setdefault(b, {"last": {}, "dma": []})
                if slot is None:
                    t["last"][eng] = o
                else:
                    t["dma"] = (t["dma"] + [o])[-16:]
        for k in war:
            w = self.lastw.get(k)
            if w is not None:
                d.append(w)
            d.extend(self.readers.get(k, ()))
        for k in reads:
            w = self.lastw.get(k)
            if w is not None:
                d.append(w)
        for k in writes:
            w = self.lastw.get(k)
            if w is not None:
                d.append(w)
            d.extend(self.readers.get(k, ()))
        for k in reads:
            self.readers.setdefault(k, []).append(o)
        for k in writes:
            self.lastw[k] = o
            self.readers[k] = []
        o.deps = d
        if slot is not None:
            self.slot_cnt[slot] = self.slot_cnt.get(slot, 0) + 1
            o.dval = 16 * self.slot_cnt[slot]
        self.ops[eng].append(o)
        return o

    def emit(self, nc):
        for e in ENGS:
            for o in self.ops[e]:
                for d in o.deps:
                    if d.slot is None and not (d.eng == "pe" and e == "pe"):
                        d.signal = True
        for e in ENGS:
            c = 0
            for o in self.ops[e]:
                if o.slot is None and o.signal:
                    c += 1
                    o.semval = c
        with ExitStack() as st:
            sems = {e: st.enter_context(nc.semaphore("s_" + e)) for e in ENGS}
            ssem = {s: st.enter_context(nc.semaphore("d_" + str(s))) for s in self.slot_cnt}
            block = st.enter_context(nc.Block())

            def run(e, eng):
                known = {}
                for o in self.ops[e]:
                    for d in o.deps:
                        if d.slot is not None:
                            key, sem, val = ("d", d.slot), ssem[d.slot], d.dval
                        else:
                            if d.eng == "pe" and e == "pe":
                                continue
                            key, sem, val = ("e", d.eng), sems[d.eng], d.semval
                        if known.get(key, 0) >= val:
                            continue
                        eng.wait_ge(sem, val)
                        known[key] = val
                    if o.fn is None:
                        continue
                    ins = o.fn(eng)
                    if o.slot is not None:
                        ins.then_inc(ssem[o.slot], 16)
                    elif o.signal:
                        ins.then_inc(sems[e], 1)

            block.tensor(lambda eng: run("pe", eng))
            block.scalar(lambda eng: run("act", eng))
            block.vector(lambda eng: run("dve", eng))
            block.gpsimd(lambda eng: run("pool", eng))
            block.sync(lambda eng: run("sp", eng))


def build(n_layers=L, debug=False, stop_at=None):
    nc = bass.Bass("TRN2", target_bir_lowering=False)
    P = Prog()

    def din(name, shape, dt=F32):
        return nc.dram_tensor(name, list(shape), dt, kind="ExternalInput").ap()

    xT_d = din("xT", [D, S])
    pos_d = din("pos", [128, NT], I32)
    invf_d = din("invf", [128, 16])
    ident_d = din("ident", [128, 128])
    mask_d = din("maskT", [128, 128])
    g1T_d = din("g1T", [L, 128, 8])
    g2T_d = din("g2T", [L, 128, 8])
    win_d = din("w_in", [L, D, DIN])
    gmvg_d = din("gmvg", [L, 128, 256])
    gmog_d = din("gmog", [L, 128, 256])
    hgog_d = din("hgog", [L, 128, 256])
    mlaog_d = din("mlaog", [L, 128, 512])
    gmwT_d = din("gmwT", [L, 128, 512])
    gmbT_d = din("gmbT", [L, 128, 4])
    hglb_d = din("hglbT", [128, 2 * L])
    qagT_d = din("qagT", [L, 128, 2])
    kvagT_d = din("kvagT", [L, 128, 1])
    wuq_d = din("wuq", [L, 256, 768])
    wukv_d = din("wukv", [L, 128, 1024])
    qg_d = din("qg", [L, 128, 96])
    kgn_d = din("kgn", [L, 128, 1])
    kgpe_d = din("kgpe", [L, 128, 32])
    wout_d = din("w_out", [L, D, D])
    wff1_d = din("w_ff1", [L, D, DFF])
    wff2_d = din("w_ff2", [L, DFF, D])
    outT_d = nc.dram_tensor("outT", [D, S], F32, kind="ExternalOutput").ap()
    if debug:
        dbg_d = nc.dram_tensor("dbg", [D, S], BF16, kind="ExternalOutput").ap()

    SB0 = 16512
    cur = [SB0]

    _mine = set()

    def sb_at(name, shape, dt, off):
        _mine.add(name)
        return nc.alloc_sbuf_tensor_at(name, list(shape), dt, offset=off)

    def nbytes(shape, dt):
        n = 1
        for s_ in shape[1:]:
            n *= s_
        return n * (4 if dt in (F32, I32) else 2)

    def bump(name, shape, dt):
        off = (cur[0] + 31) // 32 * 32
        t = sb_at(name, shape, dt, off)
        cur[0] = off + nbytes(shape, dt)
        return t

    xT = bump("xT", [128, 8, S], F32)
    actT = bump("actT", [128, 8, S], BF16)
    ident = bump("ident", [128, 128], BF16)
    ones_bf = bump("ones_bf", [128, 128], BF16)
    ones_f = bump("ones_f", [128, 128], F32)
    maskT = bump("maskT", [128, 128], F32)
    mask_bf = bump("mask_bf", [128, 128], BF16)
    cosT = bump("cosT", [128, NT, 16], F32)
    sinT = bump("sinT", [128, NT, 16], F32)
    lbv = bump("lbv", [128, 2 * L], F32)
    oml = bump("oml", [128, 2 * L], F32)
    g1T = bump("g1T", [128, 8], F32)
    g2T = bump("g2T", [128, 8], F32)
    gmvg = bump("gmvg", [128, 256], F32)
    gmog = bump("gmog", [128, 256], F32)
    hgog = bump("hgog", [128, 256], F32)
    mlaog = bump("mlaog", [128, 512], F32)
    wTm = bump("wTm", [128, 4, 128], BF16)
    gmbT = bump("gmbT", [128, 4], F32)
    qagT = bump("qagT", [128, 2], F32)
    kvagT = bump("kvagT", [128, 1], F32)
    qg = bump("qg", [128, 96], F32)
    kgn = bump("kgn", [128, 1], F32)
    kgpe = bump("kgpe", [128, 32], F32)
    t1 = bump("t1", [128, 512], F32)
    t2 = bump("t2", [128, 512], F32)
    uvg = bump("uvg", [128, 512], F32)
    sq4_off = (cur[0] + 31) // 32 * 32
    sq4 = bump("sq4", [128, 4, 512], BF16)
    sq4f = sb_at("sq4f", [128, 512], F32, sq4_off)
    tmpA = bump("tmpA", [128, 512], F32)
    tmpB = bump("tmpB", [128, 512], F32)
    sm = bump("sm", [128, 16], F32)
    sm2 = bump("sm2", [128, 16], F32)
    vn = bump("vn", [128, 256], BF16)
    yb = bump("yb", [128, 256], BF16)
    sspe = bump("sspe", [128, NT], F32)
    ra = bump("ra", [128, 4, 16], F32)
    rb = bump("rb", [128, 4, 16], F32)
    rkt = bump("rkt", [128, 32], F32)
    rkb = bump("rkb", [128, 4, 32], BF16)
    krT_off = (cur[0] + 31) // 32 * 32
    krT = bump("krT", [128, S], BF16)
    ybh = sb_at("ybh", [128, 4, 128], BF16, krT_off)
    P.ranges["krT"] = (10 ** 7, 10 ** 7 + 4096)
    P.ranges["ybh"] = (10 ** 7, 10 ** 7 + 1024)
    ybh_b = sb_at("ybh_b", [128, 4, 128], BF16, krT_off + 1024)
    P.ranges["ybh_b"] = (10 ** 7 + 1024, 10 ** 7 + 2048)
    biasc = bump("biasc", [128, 8], F32)
    zeros_bf = bump("zeros_bf", [128, 272], BF16)
    AR = (cur[0] + 31) // 32 * 32
    AEND = 229376 - 256
    ASZ = (AEND - AR) // 32 * 32
    assert ASZ >= 82304 and ASZ - 16384 >= 65856, (AR, ASZ)

    def ar(name, shape, dt, off):
        assert off % 32 == 0 and off + nbytes(shape, dt) <= ASZ, (name, off, ASZ)
        P.ranges[name] = (off, off + nbytes(shape, dt))
        return sb_at(name, shape, dt, AR + off)

    gmT = ar("gmT", [128, 2, S], BF16, 0)
    posi = ar("posi", [128, NT], I32, 0)
    iscr = ar("iscr", [128, 256], I32, 1024)
    hgT = ar("hgT", [128, 2, S], BF16, 8192)
    wg = [ar("wg0", [128, 8, 512], BF16, ASZ - 16384), ar("wg1", [128, 8, 512], BF16, ASZ - 8192)]
    G_t1 = ar("G_t1", [128, 2048], F32, 16384)
    G_t2 = ar("G_t2", [128, 2048], F32, 24576)
    u4 = ar("u4", [128, 1024], F32, 32768)
    v4 = ar("v4", [128, 1024], F32, 36864)
    vn4 = ar("vn4", [128, 4, 256], BF16, 40960)
    yb4 = ar("yb4", [128, 4, 256], BF16, 43008)
    G_t3 = ar("G_t3", [128, 1024], F32, 45056)
    hgi = ar("hgi", [128, NT, 128], BF16, 16384)
    sg = ar("sg", [128, NT, 128], BF16, 20480)
    qT = ar("qT", [128, S], BF16, 24576)
    kT = ar("kT", [128, S], BF16, 28672)
    bT = ar("bT", [128, NT, 128], F32, 32768)
    dtmp = ar("dtmp", [128, 8, 128], F32, 40960)
    Etmp = ar("Etmp", [128, 8, 128], F32, 45056)
    KendT = ar("KendT", [128, S], BF16, 49152)
    Kend_tok = ar("Kend_tok", [128, NT, 128], BF16, 53248)
    QA = ar("QA", [128, 8, 128], BF16, 49152)
    QB = ar("QB", [128, 8, 64], BF16, 51200)
    KA = ar("KA", [128, 8, 128], BF16, 53248)
    KB = ar("KB", [128, 8, 128], BF16, 55296)
    Sall = ar("Sall", [128, NT + 1, 64], F32, 57344)
    Sbf = ar("Sbf", [128, NT, 64], BF16, 61696)
    eb127 = ar("eb127", [128, NT], F32, 63744)
    sT4 = ar("sT4", [128, 2, 4, 128], BF16, 63808)
    sT4_b = ar("sT4_b", [128, 2, 4, 128], BF16, 40960)
    cqnT = ar("cqnT", [128, 2, S], BF16, 16384)
    ckvnT = ar("ckvnT", [128, S], BF16, 24576)
    pT23 = [ar("pT2", [128, 512], BF16, 28672), ar("pT3", [128, 512], BF16, 29696)]
    wuq = ar("wuq", [128, 2, 768], BF16, 30720)
    wukv = ar("wukv", [128, 1024], BF16, 33792)
    rks = ar("rks", [128, NT, 8], F32, 35840)
    Vx = ar("Vx", [128, NT, 4, 65], BF16, 36352)
    QT = ar("QT", [128, 4, S], BF16, 44672)
    KT = ar("KT", [128, 4, S], BF16, 61056)
    pT = [ar("pT0", [128, 512], BF16, 77440), ar("pT1", [128, 512], BF16, 78464)] + pT23
    qtmp = ar("qtmp", [128, 4, 96], F32, 79488)
    xr = ar("xr", [128, 4, 32], F32, 81024)
    qf = ar("qf", [128, 4, 96], BF16, 81536)
    qtmp_b = ar("qtmp_b", [128, 4, 96], F32, 28672)
    xr_b = ar("xr_b", [128, 4, 32], F32, 30208)
    qf_b = ar("qf_b", [128, 4, 96], BF16, 77440)
    ra_b = ar("ra_b", [128, 4, 16], F32, 78464)
    rb_b = ar("rb_b", [128, 4, 16], F32, 78720)
    smb = ar("smb", [128, 16], F32, 78976)
    sm2b = ar("sm2b", [128, 16], F32, 79040)
    cqraw = ar("cqraw", [128, 3, 512], F32, 44672)
    Wout = ar("Wout", [128, 8, D], BF16, 16384)
    hT = [ar("hT0", [128, 4, S], BF16, 0), ar("hT1", [128, 4, S], BF16, 16384)]
    W1g = [ar("W1g0", [128, 8, 512], BF16, 32768), ar("W1g1", [128, 8, 512], BF16, 40960)]
    W2g = [ar("W2g0", [128, 4, D], BF16, 49152), ar("W2g1", [128, 4, D], BF16, 57344)]
    relu_a = t1

    psbig = nc.alloc_psum_tensor("psbig", [128, 4096], F32)
    ps = [psbig[:, i * 512:(i + 1) * 512] for i in range(8)]
    pctr = [0]

    def nb():
        i = pctr[0] % 6
        pctr[0] += 1
        return i

    lctr = [0]

    def nbl():
        i = 6 + lctr[0] % 2
        lctr[0] += 1
        return i

    def mm(out, lhsT, rhs, start, stop, reads, writes):
        P.add("pe", lambda e: e.matmul(out, lhsT=lhsT, rhs=rhs, start=start, stop=stop), reads, writes)

    def act(out, in_, func, reads, writes, scale=1.0, bias=0.0):
        if bias == EPS:
            bias, reads = biasc[0:out.shape[0], 0:1], list(reads) + ["biasc"]
        elif bias == 1.0:
            bias, reads = biasc[0:out.shape[0], 2:3], list(reads) + ["biasc"]
        elif bias != 0.0:
            assert abs(bias + 0.5 * math.log(96.0)) < 1e-9
            bias, reads = biasc[0:out.shape[0], 1:2], list(reads) + ["biasc"]
        P.add("act", lambda e: e.activation(out, in_, func, bias=bias, scale=scale), reads, writes)

    def tt(out, in0, in1, op, reads, writes, eng="dve"):
        P.add(eng, lambda e: e.tensor_tensor(out=out, in0=in0, in1=in1, op=op), reads, writes)

    def ts(out, in0, s1, s2, op0, op1, reads, writes, eng="dve"):
        if s2 is None:
            P.add(eng, lambda e: e.tensor_scalar(out=out, in0=in0, scalar1=s1, scalar2=None, op0=op0), reads, writes)
        else:
            P.add(eng, lambda e: e.tensor_scalar(out=out, in0=in0, scalar1=s1, scalar2=s2, op0=op0, op1=op1), reads, writes)

    def stt(out, in0, scalar, in1, op0, op1, reads, writes):
        P.add("dve", lambda e: e.scalar_tensor_tensor(out=out, in0=in0, scalar=scalar, in1=in1, op0=op0, op1=op1), reads, writes)

    def red(out, in_, reads, writes, op=ALU.add):
        P.add("dve", lambda e: e.tensor_reduce(out=out, in_=in_, axis=AX.X, op=op), reads, writes)

    def recip(out, in_, reads, writes):
        P.add("dve", lambda e: e.reciprocal(out, in_), reads, writes)

    def cp(out, in_, reads, writes, eng="dve"):
        P.add(eng, lambda e: e.tensor_copy(out, in_), reads, writes)

    def memset(ap, v, writes, eng="dve"):
        P.add(eng, lambda e: e.memset(ap, v), (), writes)

    def dma(eng, out, in_, writes, slot, reads=(), war=()):
        P.add(eng, lambda e: e.dma_start(out=out, in_=in_), reads, writes, slot=slot, war=war)

    def rsqrt_small(ap, n, k):
        act(ap, ap, AF.Ln, [k], [k], scale=1.0 / n, bias=EPS)
        act(ap, ap, AF.Exp, [k], [k], scale=-0.5)

    def bc_last(ap, n):
        a = ap.shape[1]
        return ap.unsqueeze(2).broadcast_to([ap.shape[0], a, n])

    def bc_mid(ap, a):
        return ap.unsqueeze(1).broadcast_to([ap.shape[0], a, ap.shape[1]])

    def v3(ap, d):
        return ap.rearrange("p (h d) -> p h d", d=d)

    XK = lambda c, tb: ("xT", c, tb)
    AK = lambda c, tb: ("act", c, tb)

    for c in range(8):
        dma("sp", xT[:, c, :], xT_d[c * 128:(c + 1) * 128, :], [XK(c, tb) for tb in range(NB)], "x%d" % c)
    dma("pool", ident[:, :], ident_d[:, :], ["ident"], "ident")
    dma("sp", maskT[:, :], mask_d[:, :], ["maskT"], "maskT")
    dma("sp", posi[:, :], pos_d[:, :], ["posi"], "pos")
    dma("sp", t2[:, 0:16], invf_d[:, :], ["t2"], "invf")
    dma("sp", sm2[:, 0:2 * L], hglb_d[:, :], ["sm2"], "hglb")
    memset(biasc[:, 0:1], EPS, ["biasc"])
    memset(biasc[:, 1:2], -0.5 * math.log(96.0), ["biasc"])
    memset(biasc[:, 2:3], 1.0, ["biasc"])
    memset(ones_bf[:, :], 1.0, ["ones_bf"])
    memset(zeros_bf[:, :], 0.0, ["zeros_bf"])
    memset(ones_f[:, :], 1.0, ["ones_f"])
    cp(mask_bf[:, :], maskT[:, :], ["maskT"], ["mask_bf"])
    cp(uvg[:, 0:16], posi[:, :], ["posi"], ["uvg"])
    ang = tmpA[:, 0:256]
    tt(v3(ang, 16), bc_last(uvg[:, 0:16], 16), bc_mid(t2[:, 0:16], NT), ALU.mult, ["uvg", "t2"], ["tmpA"])
    TWO_PI = 2.0 * math.pi

    def sin_table(dst, shift, dkey):
        a = tmpB[:, 0:256]
        ts(a, ang, 1.0, shift, ALU.mult, ALU.add, ["tmpA"], ["tmpB"])
        ts(t1[:, 0:256], a, 1.0 / TWO_PI, None, ALU.mult, None, ["tmpB"], ["t1"])
        cp(iscr[:, :], t1[:, 0:256], ["t1"], ["iscr"])
        cp(t1[:, 0:256], iscr[:, :], ["iscr"], ["t1"])
        stt(a, t1[:, 0:256], -TWO_PI, a, ALU.mult, ALU.add, ["t1", "tmpB"], ["tmpB"])
        ts(t1[:, 0:256], a, math.pi, None, ALU.is_gt, None, ["tmpB"], ["t1"])
        stt(a, t1[:, 0:256], -TWO_PI, a, ALU.mult, ALU.add, ["t1", "tmpB"], ["tmpB"])
        ts(t1[:, 0:256], a, -math.pi, None, ALU.is_lt, None, ["tmpB"], ["t1"])
        stt(a, t1[:, 0:256], TWO_PI, a, ALU.mult, ALU.add, ["t1", "tmpB"], ["tmpB"])
        ts(a, a, 3.141592, -3.141592, ALU.min, ALU.max, ["tmpB"], ["tmpB"])
        act(dst[:, :, :].rearrange("p t i -> p (t i)"), a, AF.Sin, ["tmpB"], [dkey])

    sin_table(sinT, 0.0, "sinT")
    sin_table(cosT, math.pi / 2.0, "cosT")
    lb3 = sm2[:, 0:2 * L].rearrange("p (c l) -> p c l", l=L)
    red(sm[:, 0:2], lb3, ["sm2"], ["sm"], op=ALU.max)
    tt(lb3, lb3, bc_last(sm[:, 0:2], L), ALU.subtract, ["sm2", "sm"], ["sm2"])
    act(sm2[:, 0:2 * L], sm2[:, 0:2 * L], AF.Exp, ["sm2"], ["sm2"])
    red(sm[:, 0:2], lb3, ["sm2"], ["sm"])
    recip(sm[:, 0:2], sm[:, 0:2], ["sm"], ["sm"])
    tt(lb3, lb3, bc_last(sm[:, 0:2], L), ALU.mult, ["sm2", "sm"], ["sm2"])
    lbv3 = lbv[:, :].rearrange("p (c l) -> p c l", l=L)
    memset(lbv[:, :], 0.0, ["lbv"])
    for l in range(1, L):
        tt(lbv3[:, :, l:l + 1], lbv3[:, :, l - 1:l], lb3[:, :, l:l + 1], ALU.add, ["lbv", "sm2"], ["lbv"])
    ts(oml[:, :], lbv[:, :], -1.0, 1.0, ALU.mult, ALU.add, ["lbv"], ["oml"])

    class _Stop(Exception):
        pass

    def stage(name):
        if name == stop_at:
            raise _Stop()

    def run_lockstep(gens):
        alive = [True] * len(gens)
        while any(alive):
            for gi, g in enumerate(gens):
                if alive[gi]:
                    try:
                        next(g)
                    except StopIteration:
                        alive[gi] = False

    def rms_to_act(gT, gkey, l):
        def stage_a(tb):
            sl = slice(tb * 512, (tb + 1) * 512)
            b = nb()
            for hf in range(2):
                act(sq4[:, :, :], xT[:, hf * 4:(hf + 1) * 4, sl], AF.Square,
                    [XK(c, tb) for c in range(hf * 4, hf * 4 + 4)], ["sq4"])
                for j in range(4):
                    mm(ps[b][:, :], ones_bf[:, :], sq4[:, j, :], hf == 0 and j == 0, hf == 1 and j == 3,
                       ["sq4", "ones_bf"], [("ps", b)])
            return b

        def stage_b(tb, b):
            sl = slice(tb * 512, (tb + 1) * 512)
            tA, kA = (tmpA, "tmpA") if tb % 2 == 0 else (t1, "t1")
            tB, kB = (tmpB, "tmpB") if tb % 2 == 0 else (t2, "t2")
            act(tA[:, :], ps[b][:, :], AF.Ln, [("ps", b)], [kA], scale=1.0 / D, bias=EPS)
            act(tB[:, :], tA[:, :], AF.Exp, [kA], [kB], scale=-0.5)
            for c in range(8):
                stt(actT[:, c, sl], xT[:, c, sl], gT[:, c:c + 1], tB[:, :], ALU.mult, ALU.mult,
                    [XK(c, tb), kB, gkey], [AK(c, tb)])

        banks = {0: stage_a(0)}
        for tb in range(NB):
            if tb + 1 < NB:
                banks[tb + 1] = stage_a(tb + 1)
            stage_b(tb, banks[tb])

    def load_wg(l, buf, blocks):
        src = win_d[l]
        for bi, (c0, n, d0) in enumerate(blocks):
            dma("pool", wg[buf][:, :, d0:d0 + n], src[:, c0:c0 + n].rearrange("(c p) n -> p c n", p=128),
                [("wg", buf, bi)], "wg%d_%d" % (buf, bi), war=[("wg", buf, k) for k in range(4)])

    def load_params(l):
        for dst, src, nm in ((g1T, g1T_d, "g1T"), (g2T, g2T_d, "g2T"), (gmvg, gmvg_d, "gmvg"), (gmog, gmog_d, "gmog"),
                             (hgog, hgog_d, "hgog"), (mlaog, mlaog_d, "mlaog"), (gmbT, gmbT_d, "gmbT"),
                             (qagT, qagT_d, "qagT"), (kvagT, kvagT_d, "kvagT"), (qg, qg_d, "qg"), (kgn, kgn_d, "kgn"),
                             (kgpe, kgpe_d, "kgpe")):
            dma("sp", dst[:, :], src[l], [nm], "p_" + nm)
        dma("sp", uvg[:, :], gmwT_d[l], ["uvg"], "p_gmw")
        tt(wTm[:, :, :], v3(uvg[:, :], 128), bc_mid(maskT[:, :], 4), ALU.mult, ["uvg", "maskT"], ["wTm"])

    def gelu_psum(b, key):
        pk = ("ps", b)
        act(t1[:, :], ps[b][:, :], AF.Square, [pk], ["t1"])
        ts(t1[:, :], t1[:, :], 0.044715, 1.0, ALU.mult, ALU.add, ["t1"], ["t1"])
        tt(t1[:, :], t1[:, :], ps[b][:, :], ALU.mult, ["t1", pk], ["t1"])
        act(t2[:, :], t1[:, :], AF.Exp, ["t1"], ["t2"], scale=-1.5957691216057308)
        ts(t2[:, :], t2[:, :], 1.0, None, ALU.add, None, ["t2"], ["t2"])
        recip(t2[:, :], t2[:, :], ["t2"], ["t2"])
        tt(uvg[:, :], t2[:, :], ps[b][:, :], ALU.mult, ["t2", pk], ["uvg"])

    def head_norm(dst3, src3, nh, hd, srckeys, dstkeys, gain3, gkey):
        n = nh * hd
        act(v3(t1[:, 0:n], hd), src3, AF.Square, srckeys, ["t1"])
        red(sm[:, 0:nh], v3(t1[:, 0:n], hd), ["t1"], ["sm"])
        rsqrt_small(sm[:, 0:nh], hd, "sm")
        tt(v3(t1[:, 0:n], hd), src3, bc_last(sm[:, 0:nh], hd), ALU.mult, srckeys + ["sm"], ["t1"])
        tt(dst3, v3(t1[:, 0:n], hd), gain3, ALU.mult, ["t1", gkey], dstkeys)

    def layer(l, first_wg_loaded):
        load_params(l)
        stage("params")
        rms_to_act(g1T, "g1T", l)
        stage("norm1")
        if not first_wg_loaded:
            load_wg(l, 0, [(0, 512, 0)])
        def ct_blocks(ct):
            return [(1024 + ct * 128, 128, 0), (1280 + ct * 128, 128, 128), (512 + ct * 128, 128, 256), (768 + ct * 128, 128, 384)]
        load_wg(l, 1, ct_blocks(0))
        stage("wgload")
        def g0_gen(tb, hf_):
            cs_ = slice(hf_ * 1024, (hf_ + 1) * 1024)
            hs_ = slice(hf_ * 512, (hf_ + 1) * 512)
            psq = psbig[:, hf_ * 1024:(hf_ + 1) * 1024]
            pk2 = [("ps", 2 * hf_), ("ps", 2 * hf_ + 1)]
            by, bt = 4 + hf_, 6 + hf_
            T1, T2, T3 = G_t1[:, cs_], G_t2[:, cs_], G_t3[:, hs_]
            K1, K2, K3 = ("G_t1", hf_), ("G_t2", hf_), ("G_t3", hf_)
            U4, V4, KU, KV = u4[:, hs_], v4[:, hs_], ("u4", hf_), ("v4", hf_)
            VN, YB, KVN, KYB = vn4[:, 2 * hf_:2 * hf_ + 2, :], yb4[:, 2 * hf_:2 * hf_ + 2, :], ("vn4", hf_), ("yb4", hf_)
            SM, KSM = sm[:, hf_ * 8:hf_ * 8 + 8], ("smg", hf_)
            for jj in range(2):
                j = 2 * hf_ + jj
                tsl = slice((tb * 4 + j) * 128, (tb * 4 + j + 1) * 128)
                for c in range(8):
                    mm(ps[j][:, :], actT[:, c, tsl], wg[0][:, c, :], c == 0, c == 7, [AK(c, tb), ("wg", 0, 0)], [("ps", j)])
            yield
            act(T1, psq, AF.Square, pk2, [K1], scale=math.sqrt(0.044715))
            yield
            stt(T1, T1, 1.0, psq, ALU.add, ALU.mult, [K1] + pk2, [K1])
            yield
            act(T2, T1, AF.Exp, [K1], [K2], scale=-1.5957691216057308)
            yield
            act(T2, T2, AF.Ln, [K2], [K2], bias=1.0)
            yield
            act(T2, T2, AF.Exp, [K2], [K2], scale=-1.0)
            yield
            q3_ = psq.rearrange("p (j n) -> p j n", n=512)
            g3_ = T2.rearrange("p (j n) -> p j n", n=512)
            tt(v3(V4, 256), g3_[:, :, 256:512], q3_[:, :, 256:512], ALU.mult, [K2] + pk2, [KV])
            yield
            tt(v3(U4, 256), g3_[:, :, 0:256], q3_[:, :, 0:256], ALU.mult, [K2] + pk2, [KU])
            act(T3, V4, AF.Square, [KV], [K3])
            yield
            red(SM, v3(T3, 64), [K3], [KSM])
            yield
            act(SM, SM, AF.Ln, [KSM], [KSM], scale=1.0 / 64, bias=EPS)
            yield
            act(SM, SM, AF.Exp, [KSM], [KSM], scale=-0.5)
            yield
            tt(v3(T3, 64), v3(V4, 64), bc_last(SM, 64), ALU.mult, [KV, KSM], [K3])
            yield
            tt(VN, v3(T3, 256), bc_mid(gmvg[:, :], 2), ALU.mult, [K3, "gmvg"], [KVN])
            yield
            for jj in range(2):
                for h in range(4):
                    co = jj * 256 + h * 64
                    mm(ps[by][:, co:co + 64], wTm[:, h, :], VN[:, jj, h * 64:(h + 1) * 64], True, True, ["wTm", KVN], [("ps", by)])
            yield
            for jj in range(2):
                tt(v3(T3[:, jj * 256:(jj + 1) * 256], 64), v3(ps[by][:, jj * 256:(jj + 1) * 256], 64), bc_last(gmbT[:, 0:4], 64), ALU.add,
                   [("ps", by), "gmbT"], [K3])
            yield
            tt(T3, T3, U4, ALU.mult, [K3, KU], [K3])
            yield
            act(T2[:, 0:512], T3, AF.Square, [K3], [K2])
            yield
            red(SM, v3(T2[:, 0:512], 64), [K2], [KSM])
            yield
            act(SM, SM, AF.Ln, [KSM], [KSM], scale=1.0 / 64, bias=EPS)
            yield
            act(SM, SM, AF.Exp, [KSM], [KSM], scale=-0.5)
            yield
            tt(v3(T3, 64), v3(T3, 64), bc_last(SM, 64), ALU.mult, [K3, KSM], [K3])
            yield
            tt(YB, v3(T3, 256), bc_mid(gmog[:, :], 2), ALU.mult, [K3, "gmog"], [KYB])
            yield
            for jj in range(2):
                for c in range(2):
                    co = jj * 256 + c * 128
                    mm(ps[bt][:, co:co + 128], YB[:, jj, c * 128:(c + 1) * 128], ident[:, :], True, True, [KYB, "ident"], [("ps", bt)])
            yield
            pt4 = ps[bt].rearrange("p (j c t) -> p j c t", c=2, t=128)
            t0_ = tb * 512 + hf_ * 256
            for c in range(2):
                act(gmT[:, c, t0_:t0_ + 256].rearrange("p (j t) -> p j t", t=128), pt4[:, :, c, :], AF.Copy,
                    [("ps", bt)], [("gmT", tb * 4 + 2 * hf_ + jj) for jj in range(2)])
            yield

        memset(biasc[:, 7:8], 0.0, ["sm", ("smg", 0), ("smg", 1)])
        for tb in range(NB):
            run_lockstep([g0_gen(tb, 0), g0_gen(tb, 1)])
        memset(biasc[:, 7:8], 0.0, ["sm", ("smg", 0), ("smg", 1)])
        stage("G0")
        for ct in range(2):
            wb = 1 - ct
            if ct == 0:
                load_wg(l, 0, ct_blocks(1))
            else:
                load_wg(l, 1, [(1536, 384, 0), (1920, 32, 384)])
            for t in range(NT):
                tb = t // 4
                tsl = slice(t * 128, (t + 1) * 128)
                b = nb()
                for c in range(8):
                    mm(ps[b][:, 0:256], actT[:, c, tsl], wg[wb][:, c, 0:256], c == 0, c == 7,
                       [AK(c, tb), ("wg", wb, 0), ("wg", wb, 1)], [("ps", b)])
                pk = ("ps", b)
                act(hgi[:, t, :], ps[b][:, 0:128], AF.Copy, [pk], [("hgi", t)])
                TS, KS = (t2, "t2") if t % 2 == 0 else (tmpB, "tmpB")
                act(TS[:, 0:128], ps[b][:, 128:256], AF.Exp, [pk], [KS], scale=-1.0)
                act(TS[:, 0:128], TS[:, 0:128], AF.Ln, [KS], [KS], bias=1.0)
                act(TS[:, 0:128], TS[:, 0:128], AF.Exp, [KS], [KS], scale=-1.0)
                tt(sg[:, t, :], TS[:, 0:128], ps[b][:, 128:256], ALU.mult, [KS, pk], [("sg", t)])
            for tb in range(NB):
                sl = slice(tb * 512, (tb + 1) * 512)
                bq, bf_ = nb(), nb()
                for c in range(8):
                    mm(ps[bq][:, :], wg[wb][:, c, 256:384], actT[:, c, sl], c == 0, c == 7, [AK(c, tb), ("wg", wb, 2)], [("ps", bq)])
                for c in range(8):
                    mm(ps[bf_][:, :], wg[wb][:, c, 384:512], actT[:, c, sl], c == 0, c == 7, [AK(c, tb), ("wg", wb, 3)], [("ps", bf_)])
                TQ, KQ = (t2, "t2") if tb % 2 == 0 else (tmpA, "tmpA")
                TF, KF = (t1, "t1") if tb % 2 == 0 else (tmpB, "tmpB")
                TL, KL = (uvg, "uvg") if tb % 2 == 0 else (sq4f, "sq4")
                act(TF[:, :], ps[bf_][:, :], AF.Exp, [("ps", bf_)], [KF], scale=-1.0)
                act(TF[:, :], TF[:, :], AF.Ln, [KF], [KF], bias=1.0)
                act(TF[:, :], TF[:, :], AF.Exp, [KF], [KF], scale=-1.0)
                li = ct * L + l
                ts(TF[:, :], TF[:, :], oml[:, li:li + 1], lbv[:, li:li + 1], ALU.mult, ALU.add, [KF, "oml", "lbv"], [KF])
                act(TQ[:, :], ps[bq][:, :], AF.Exp, [("ps", bq)], [KQ], scale=-1.0)
                act(TQ[:, :], TQ[:, :], AF.Ln, [KQ], [KQ], bias=1.0)
                act(TQ[:, :], TQ[:, :], AF.Exp, [KQ], [KQ], scale=-1.0)
                act(TL[:, :], TF[:, :], AF.Ln, [KF], [KL])
                tt(qT[:, sl], TQ[:, :], ps[bq][:, :], ALU.mult, [KQ, ("ps", bq)], [("qT", tb)])
                ts(kT[:, sl], TF[:, :], -1.0, 1.0, ALU.mult, ALU.add, [KF], [("kT", tb)])
                for j in range(4):
                    ch = tb * 4 + j
                    P.add("dve", (lambda o_, d1: (lambda e: e.tensor_tensor_scan(out=o_, data0=ones_f[:, :], data1=d1, initial=0.0,
                                                                                  op0=ALU.mult, op1=ALU.add)))(bT[:, ch, :], TL[:, j * 128:(j + 1) * 128]),
                          [KL, "ones_f"], [("bT", ch)])
            stage("proj%d" % ct)
            hgrn(l, ct)
            stage("ct%d" % ct)
        g3(l)
        stage("g3")
        mla(l)
        stage("mla")
        if debug and l == 0:
            for c in range(8):
                if c < 2:
                    src, rk = gmT[:, c, :], [("gmT", t) for t in range(NT)]
                elif c < 4:
                    src, rk = hgT[:, c - 2, :], [("hgT", c - 2, tb) for tb in range(NB)]
                else:
                    src, rk = actT[:, c, :], [AK(c, tb) for tb in range(NB)]
                dma("sp", dbg_d[c * 128:(c + 1) * 128, :], src, [("dbg", c)], "dbg%d" % c, reads=rk)
        wout_phase(l)
        stage("wout")
        rms_to_act(g2T, "g2T", l)
        stage("norm2")
        ffn(l)

    def hgrn(l, ct):
        allb = [("bT", ch) for ch in range(NT)]
        for hf in range(2):
            cs = slice(hf * 8, hf * 8 + 8)
            fs = slice(hf * 1024, hf * 1024 + 1024)
            bks = allb[hf * 8:hf * 8 + 8]
            tt(dtmp[:, :, :], bT[:, cs, :], bc_last(bT[:, cs, 127], 128), ALU.subtract, bks, ["dtmp"])
            act(Etmp[:, :, :], dtmp[:, :, :], AF.Exp, ["dtmp"], ["Etmp"], scale=-1.0)
            tt(KendT[:, fs], kT[:, fs], Etmp[:, :, :].rearrange("p c t -> p (c t)"), ALU.mult,
               ["Etmp", ("kT", 2 * hf), ("kT", 2 * hf + 1)], [("KendT", hf)])
        stage("hk")
        act(eb127[:, :], bT[:, :, 127], AF.Exp, allb, ["eb127"])
        for c4 in range(4):
            b = nb()
            for j in range(4):
                ch = c4 * 4 + j
                mm(ps[b][:, j * 128:(j + 1) * 128], KendT[:, ch * 128:(ch + 1) * 128], ident[:, :], True, True,
                   [("KendT", ch // 8), "ident"], [("ps", b)])
            act(Kend_tok[:, c4 * 4:(c4 + 1) * 4, :], v3(ps[b][:, :], 128), AF.Copy, [("ps", b)], [("Ktok", c4)])
        stage("htp")
        ub = [nbl(), nbl()]
        for ch in range(NT):
            b = ub[ch // 8]
            j = ch % 8
            for hh in range(2):
                r = slice(hh * 64, hh * 64 + 64)
                mm(ps[b][r, j * 64:(j + 1) * 64], Kend_tok[:, ch, r], hgi[:, ch, r], True, True,
                   [("Ktok", ch // 4), ("hgi", ch)], [("ps", b)])
        stage("hU")
        memset(Sall[:, 0, :], 0.0, ["Sall"])
        for ch in range(NT):
            b = ub[ch // 8]
            j = ch % 8
            stt(Sall[:, ch + 1, :], Sall[:, ch, :], eb127[:, ch:ch + 1], ps[b][:, j * 64:(j + 1) * 64], ALU.mult, ALU.add,
                ["Sall", "eb127", ("ps", b)], ["Sall"])
        cp(Sbf[:, :, :], Sall[:, 0:NT, :], ["Sall"], ["Sbf"])
        stage("hchain")
        memset(KA[:, :, 64:128], 0.0, ["KA"])
        for hf in range(2):
            cs = slice(hf * 8, hf * 8 + 8)
            fs = slice(hf * 1024, hf * 1024 + 1024)
            bks = allb[hf * 8:hf * 8 + 8]
            kks = [("kT", 2 * hf), ("kT", 2 * hf + 1)]
            qks = [("qT", 2 * hf), ("qT", 2 * hf + 1)]
            q3 = qT[:, fs].rearrange("p (c t) -> p c t", t=128)
            k3 = kT[:, fs].rearrange("p (c t) -> p c t", t=128)
            act(Etmp[:, :, :], bT[:, cs, :], AF.Exp, bks, ["Etmp"])
            tt(QA[:, :, :], q3, Etmp[:, :, :], ALU.mult, ["Etmp"] + qks, ["QA"])
            tt(dtmp[:, :, :], bT[:, cs, :], bc_last(bT[:, cs, 63], 128), ALU.subtract, bks, ["dtmp"])
            act(Etmp[:, :, 64:128], dtmp[:, :, 64:128], AF.Exp, ["dtmp"], ["Etmp"])
            tt(QB[:, :, :], q3[:, :, 64:128], Etmp[:, :, 64:128], ALU.mult, ["Etmp"] + qks, ["QB"])
            act(Etmp[:, :, :], dtmp[:, :, :], AF.Exp, ["dtmp"], ["Etmp"], scale=-1.0)
            tt(KB[:, :, :], k3, Etmp[:, :, :], ALU.mult, ["Etmp"] + kks, ["KB"])
            act(Etmp[:, :, 0:64], bT[:, cs, 0:64], AF.Exp, bks, ["Etmp"], scale=-1.0)
            tt(KA[:, :, 0:64], k3[:, :, 0:64], Etmp[:, :, 0:64], ALU.mult, ["Etmp"] + kks, ["KA"])
            stage("hprep")

            def hbatch_gen(g4, p):
                ST, KST = (sT4, "sT4") if p == 0 else (sT4_b, "sT4_b")
                T1, K1 = (t1, "t1") if p == 0 else (uvg, "uvg")
                T2, K2 = (t2, "t2") if p == 0 else (tmpA, "tmpA")
                SM, KSM = (sm, "sm") if p == 0 else (sm2, "sm2")
                YB, KYB = (ybh, "ybh") if p == 0 else (ybh_b, "ybh_b")
                ch0 = hf * 8 + g4 * 4
                tb = ch0 // 4
                bsb = [nb(), nb()]
                for j in range(4):
                    cc = g4 * 4 + j
                    for hh in range(2):
                        r = slice(hh * 64, hh * 64 + 64)
                        bs_ = bsb[hh]
                        mm(ps[bs_][:, j * 128:j * 128 + 64], KA[r, cc, :], QA[r, cc, 0:64], True, True, ["KA", "QA"], [("ps", bs_)])
                        mm(ps[bs_][:, j * 128 + 64:j * 128 + 128], KB[r, cc, :], QB[r, cc, :], True, True, ["KB", "QB"], [("ps", bs_)])
                yield
                for hh in range(2):
                    bs_ = bsb[hh]
                    tt(ST[:, hh, :, :], v3(ps[bs_][:, :], 128), bc_mid(maskT[:, :], 4), ALU.mult, [("ps", bs_), "maskT"], [KST])
                    yield
                bo = nb()
                for j in range(4):
                    cc = g4 * 4 + j
                    ch = ch0 + j
                    for hh in range(2):
                        r = slice(hh * 64, hh * 64 + 64)
                        co = j * 128 + hh * 64
                        mm(ps[bo][:, co:co + 64], ST[:, hh, j, :], hgi[:, ch, r], True, False, [KST, ("hgi", ch)], [("ps", bo)])
                        mm(ps[bo][:, co:co + 64], QA[r, cc, :], Sbf[r, ch, :], False, True, ["QA", "Sbf"], [("ps", bo)])
                yield
                pk = ("ps", bo)
                act(T1[:, :], ps[bo][:, :], AF.Square, [pk], [K1])
                yield
                red(SM[:, 0:8], v3(T1[:, :], 64), [K1], [KSM])
                yield
                act(SM[:, 0:8], SM[:, 0:8], AF.Ln, [KSM], [KSM], scale=1.0 / 64, bias=EPS)
                yield
                act(SM[:, 0:8], SM[:, 0:8], AF.Exp, [KSM], [KSM], scale=-0.5)
                yield
                tt(v3(T1[:, :], 64), v3(ps[bo][:, :], 64), bc_last(SM[:, 0:8], 64), ALU.mult, [pk, KSM], [K1])
                yield
                tt(v3(T2[:, :], 128), v3(T1[:, :], 128), bc_mid(hgog[:, ct * 128:(ct + 1) * 128], 4), ALU.mult, [K1, "hgog"], [K2])
                yield
                tt(YB[:, :, :], v3(T2[:, :], 128), sg[:, ch0:ch0 + 4, :], ALU.mult, [K2] + [("sg", ch0 + j) for j in range(4)], [KYB])
                yield
                btp = nbl()
                for j in range(4):
                    mm(ps[btp][:, j * 128:(j + 1) * 128], YB[:, j, :], ident[:, :], True, True, [KYB, "ident"], [("ps", btp)])
                yield
                act(hgT[:, ct, tb * 512:(tb + 1) * 512], ps[btp][:, :], AF.Copy, [("ps", btp)], [("hgT", ct, tb)])
                yield

            run_lockstep([hbatch_gen(0, 0), hbatch_gen(1, 1)])

    def g3(l):
        wb = 1
        dma("pool", wuq[:, :, :], wuq_d[l].rearrange("(c p) n -> p c n", p=128), ["wuq"], "wuq")
        dma("pool", wukv[:, :], wukv_d[l], ["wukv"], "wukv")
        for tb in range(NB):
            sl = slice(tb * 512, (tb + 1) * 512)
            bb = []
            for j in range(3):
                b = nb()
                bb.append(b)
                for c in range(8):
                    mm(ps[b][:, :], wg[wb][:, c, j * 128:(j + 1) * 128], actT[:, c, sl], c == 0, c == 7,
                       [AK(c, tb), ("wg", wb, 0)], [("ps", b)])
                act(cqraw[:, j, :], ps[b][:, :], AF.Copy, [("ps", b)], ["cqraw"])
                act(sq4[:, j, :], ps[b][:, :], AF.Square, [("ps", b)], ["sq4"])
            b1, b2 = nb(), nb()
            mm(ps[b1][:, :], ones_bf[:, :], sq4[:, 0, :], True, False, ["sq4", "ones_bf"], [("ps", b1)])
            mm(ps[b1][:, :], ones_bf[:, :], sq4[:, 1, :], False, True, ["sq4", "ones_bf"], [("ps", b1)])
            mm(ps[b2][:, :], ones_bf[:, :], sq4[:, 2, :], True, True, ["sq4", "ones_bf"], [("ps", b2)])
            act(tmpA[:, :], ps[b1][:, :], AF.Ln, [("ps", b1)], ["tmpA"], scale=1.0 / 256, bias=EPS)
            act(tmpB[:, :], tmpA[:, :], AF.Exp, ["tmpA"], ["tmpB"], scale=-0.5)
            for j in range(2):
                stt(cqnT[:, j, sl], cqraw[:, j, :], qagT[:, j:j + 1], tmpB[:, :], ALU.mult, ALU.mult,
                    ["cqraw", "tmpB", "qagT"], [("cqnT", tb)])
            act(tmpA[:, :], ps[b2][:, :], AF.Ln, [("ps", b2)], ["tmpA"], scale=1.0 / 128, bias=EPS)
            act(tmpB[:, :], tmpA[:, :], AF.Exp, ["tmpA"], ["tmpB"], scale=-0.5)
            stt(ckvnT[:, sl], cqraw[:, 2, :], kvagT[:, 0:1], tmpB[:, :], ALU.mult, ALU.mult,
                ["cqraw", "tmpB", "kvagT"], [("ckvnT", tb)])
        for t in range(NT):
            tb = t // 4
            tsl = slice(t * 128, (t + 1) * 128)
            b = nb()
            for c in range(8):
                mm(ps[b][:, 0:32], actT[:, c, tsl], wg[wb][:, c, 384:416], c == 0, c == 7, [AK(c, tb), ("wg", wb, 1)], [("ps", b)])
            act(t1[:, 0:32], ps[b][:, 0:32], AF.Square, [("ps", b)], ["t1"])
            red(sspe[:, t:t + 1], t1[:, 0:32], ["t1"], ["sspe"])
            tt(rkt[:, :], ps[b][:, 0:32], kgpe[:, :], ALU.mult, [("ps", b), "kgpe"], ["rkt"])
            rope(rkb[:, :, :], rkt[:, :].unsqueeze(1), 1, t, "rkt", "rkb")
            b2 = nb()
            mm(ps[b2][64:96, 0:128], rkb[:, 0, :], ident[:, :], True, True, ["rkb", "ident"], [("ps", b2)])
            act(krT[64:96, tsl], ps[b2][64:96, 0:128], AF.Copy, [("ps", b2)], [("krT", t)])

    def rope(dst3, src3, nh, t, srck, dstk):
        cs = bc_mid(cosT[:, t, :], nh)
        sn = bc_mid(sinT[:, t, :], nh)
        tt(ra[:, 0:nh, :], src3[:, :, 0:16], cs, ALU.mult, [srck], ["ra"])
        tt(rb[:, 0:nh, :], src3[:, :, 16:32], sn, ALU.mult, [srck], ["rb"])
        tt(dst3[:, 0:nh, 0:16], ra[:, 0:nh, :], rb[:, 0:nh, :], ALU.subtract, ["ra", "rb"], [dstk])
        tt(ra[:, 0:nh, :], src3[:, :, 16:32], cs, ALU.mult, [srck], ["ra"])
        tt(rb[:, 0:nh, :], src3[:, :, 0:16], sn, ALU.mult, [srck], ["rb"])
        tt(dst3[:, 0:nh, 16:32], ra[:, 0:nh, :], rb[:, 0:nh, :], ALU.add, ["ra", "rb", dstk], [dstk])

    def mla(l):
        allkr = [("krT", t) for t in range(NT)]
        for hf in range(2):
            def kside_gen():
                for hh in range(4):
                    h = hf * 4 + hh
                    for tb in range(NB):
                        sl = slice(tb * 512, (tb + 1) * 512)
                        b = nb()
                        mm(ps[b][0:64, :], wukv[:, h * 64:(h + 1) * 64], ckvnT[:, sl], True, True, ["wukv", ("ckvnT", tb)], [("ps", b)])
                        yield
                        ts(KT[0:64, hh, sl], ps[b][0:64, :], kgn[0:64, 0:1], None, ALU.mult, None, [("ps", b), "kgn"], [("KT", hh, tb)])
                        yield
                        yield
                        yield

            kgen = kside_gen()
            if hf == 0:
                for tb in range(NB):
                    sl = slice(tb * 512, (tb + 1) * 512)
                    act(KT[64:96, :, sl], bc_mid(krT[64:96, sl], 4), AF.Copy, allkr, [("KT", hh, tb) for hh in range(4)])
            memset(Vx[:, :, :, 64:65], 1.0, ["Vx1"])
            def tile_gen(t, p):
                T1, K1 = (t1, "t1") if p == 0 else (t2, "t2")
                SM, SMK = (sm, "sm") if p == 0 else (smb, "smb")
                SM2, SM2K = (sm2, "sm2") if p == 0 else (sm2b, "sm2b")
                QTMP, QTK = (qtmp, "qtmp") if p == 0 else (qtmp_b, "qtmp_b")
                XR, XRK = (xr, "xr") if p == 0 else (xr_b, "xr_b")
                QF, QFK = (qf, "qf") if p == 0 else (qf_b, "qf_b")
                RA, RAK = (ra, "ra") if p == 0 else (ra_b, "ra_b")
                RB, RBK = (rb, "rb") if p == 0 else (rb_b, "rb_b")
                tb = t // 4
                tsl = slice(t * 128, (t + 1) * 128)
                b = nb()
                mm(ps[b][:, 0:256], ckvnT[:, tsl], wukv[:, hf * 256:(hf + 1) * 256], True, True, [("ckvnT", tb), "wukv"], [("ps", b)])
                mm(ps[b][:, 256:512], ckvnT[:, tsl], wukv[:, 512 + hf * 256:512 + (hf + 1) * 256], True, True,
                   [("ckvnT", tb), "wukv"], [("ps", b)])
                bq = nb()
                for j in range(2):
                    mm(ps[bq][:, 0:384], cqnT[:, j, tsl], wuq[:, j, hf * 384:(hf + 1) * 384], j == 0, j == 1,
                       [("cqnT", tb), "wuq"], [("ps", bq)])
                yield
                pk = ("ps", b)
                act(T1[:, 0:256], ps[b][:, 0:256], AF.Square, [pk], [K1])
                yield
                red(SM2[:, 0:4], v3(T1[:, 0:256], 64), [K1], [SM2K])
                yield
                ts(SM2[:, 0:4], SM2[:, 0:4], sspe[:, t:t + 1], None, ALU.add, None, [SM2K, "sspe"], [SM2K])
                yield
                act(SM2[:, 0:4], SM2[:, 0:4], AF.Ln, [SM2K], [SM2K], scale=1.0 / 96, bias=EPS)
                yield
                act(rks[:, t, hf * 4:(hf + 1) * 4], SM2[:, 0:4], AF.Exp, [SM2K], [("rks", t)], scale=-0.5, bias=-0.5 * math.log(96.0))
                act(Vx[:, t, :, 0:64], v3(ps[b][:, 256:512], 64), AF.Copy, [pk], [("Vx", t)])
                yield
                qk = ("ps", bq)
                q3 = v3(ps[bq][:, 0:384], 96)
                act(T1[:, 0:384], ps[bq][:, 0:384], AF.Square, [qk], [K1])
                yield
                red(SM[:, 0:4], v3(T1[:, 0:384], 96), [K1], [SMK])
                yield
                act(SM[:, 0:4], SM[:, 0:4], AF.Ln, [SMK], [SMK], scale=1.0 / 96, bias=EPS)
                yield
                act(SM[:, 0:4], SM[:, 0:4], AF.Exp, [SMK], [SMK], scale=-0.5)
                yield
                tt(QTMP[:, :, :], q3, bc_last(SM[:, 0:4], 96), ALU.mult, [qk, SMK], [QTK])
                yield
                tt(QF[:, :, 0:64], QTMP[:, :, 0:64], bc_mid(qg[:, 0:64], 4), ALU.mult, [QTK, "qg"], [QFK])
                yield
                tt(XR[:, :, :], QTMP[:, :, 64:96], bc_mid(qg[:, 64:96], 4), ALU.mult, [QTK, "qg"], [XRK])
                yield
                cs = bc_mid(cosT[:, t, :], 4)
                sn = bc_mid(sinT[:, t, :], 4)
                tt(RA[:, :, :], XR[:, :, 0:16], cs, ALU.mult, [XRK], [RAK])
                yield
                tt(RB[:, :, :], XR[:, :, 16:32], sn, ALU.mult, [XRK], [RBK])
                yield
                tt(QF[:, :, 64:80], RA[:, :, :], RB[:, :, :], ALU.subtract, [RAK, RBK], [QFK])
                yield
                tt(RA[:, :, :], XR[:, :, 16:32], cs, ALU.mult, [XRK], [RAK])
                yield
                tt(RB[:, :, :], XR[:, :, 0:16], sn, ALU.mult, [XRK], [RBK])
                yield
                tt(QF[:, :, 80:96], RA[:, :, :], RB[:, :, :], ALU.add, [RAK, RBK, QFK], [QFK])
                yield
                bt_ = nb()
                for hh in range(4):
                    mm(ps[bt_][0:96, hh * 128:(hh + 1) * 128], QF[:, hh, :], ident[:, :], True, True, [QFK, "ident"], [("ps", bt_)])
                yield
                act(QT[0:96, :, tsl], v3(ps[bt_][0:96, :], 128), AF.Copy, [("ps", bt_)], [("QT", t)])
                yield

            kalive = True
            for pair in range(NT // 2):
                gens = [tile_gen(2 * pair, 0), tile_gen(2 * pair + 1, 1)]
                alive = [True, True]
                while any(alive):
                    for gi in range(2):
                        if alive[gi]:
                            try:
                                next(gens[gi])
                            except StopIteration:
                                alive[gi] = False
                    if kalive:
                        try:
                            next(kgen)
                        except StopIteration:
                            kalive = False
            for _ in kgen:
                pass
            units = [(hh, qb, kt) for hh in range(4) for qb in range(NB) for kt in range(4 * qb + 4)]
            sbank = {}
            bo_of = {}

            def emit_S(u):
                hh, qb, kt = u
                j0 = max(0, kt - 4 * qb)
                c0 = j0 * 128
                bs_ = nb()
                sbank[u] = bs_
                mm(ps[bs_][:, c0:512], KT[0:96, hh, kt * 128:(kt + 1) * 128], QT[0:96, hh, qb * 512 + c0:(qb + 1) * 512],
                   True, True, [("KT", hh, kt // 4)] + [("QT", qb * 4 + j) for j in range(j0, 4)], [("ps", bs_)])

            def emit_PV(u, pb):
                hh, qb, kt = u
                h = hf * 4 + hh
                j0 = max(0, kt - 4 * qb)
                c0 = j0 * 128
                bs_ = sbank.pop(u)
                if (hh, qb) not in bo_of:
                    bo_of[(hh, qb)] = nbl()
                    mm(ps[bo_of[(hh, qb)]][:, 0:260], zeros_bf[:, 0:128], zeros_bf[:, 0:260], True, False,
                       ["zeros_bf"], [("ps", bo_of[(hh, qb)])])
                bo = bo_of[(hh, qb)]
                act(pT[pb][:, c0:512], ps[bs_][:, c0:512], AF.Exp, [("ps", bs_), ("rks", kt)], [("pT", pb)],
                    scale=rks[:, kt, h:h + 1])
                if kt >= 4 * qb:
                    tt(pT[pb][:, c0:c0 + 128], pT[pb][:, c0:c0 + 128], mask_bf[:, :], ALU.mult,
                       [("pT", pb), "mask_bf"], [("pT", pb)])
                for j in range(j0, 4):
                    mm(ps[bo][:, j * 65:(j + 1) * 65], pT[pb][:, j * 128:(j + 1) * 128], Vx[:, kt, hh, :],
                       False, kt == 4 * qb + 3 and j == 3, [("pT", pb), ("Vx", kt), "Vx1"], [("ps", bo)])

            def epilogue(hh, qb):
                h = hf * 4 + hh
                bo = bo_of[(hh, qb)]
                o3 = ps[bo][:, 0:260].rearrange("p (j d) -> p j d", d=65)
                recip(sm2[:, 0:4].unsqueeze(2), o3[:, :, 64:65], [("ps", bo)], ["sm2"])
                tt(v3(t2[:, 0:256], 64), o3[:, :, 0:64], bc_last(sm2[:, 0:4], 64), ALU.mult, [("ps", bo), "sm2"], ["t2"])
                tt(t1[:, 0:256], t2[:, 0:256], t2[:, 0:256], ALU.mult, ["t2"], ["t1"])
                red(sm[:, 0:4], v3(t1[:, 0:256], 64), ["t1"], ["sm"])
                rsqrt_small(sm[:, 0:4], 64, "sm")
                tt(v3(t1[:, 0:256], 64), v3(t2[:, 0:256], 64), bc_last(sm[:, 0:4], 64), ALU.mult, ["t2", "sm"], ["t1"])
                tt(v3(yb[:, :], 64), v3(t1[:, 0:256], 64), bc_mid(mlaog[:, h * 64:(h + 1) * 64], 4), ALU.mult, ["t1", "mlaog"], ["yb"])

                def transposes():
                    bt_ = nb()
                    r = slice((h % 2) * 64, (h % 2) * 64 + 64)
                    for j in range(4):
                        mm(ps[bt_][r, j * 128:(j + 1) * 128], yb[:, j * 64:(j + 1) * 64], ident[:, :], True, True,
                           ["yb", "ident"], [("ps", bt_)])
                    cp(actT[r, 4 + h // 2, qb * 512:(qb + 1) * 512], ps[bt_][r, :], [("ps", bt_)], [AK(4 + h // 2, qb)])
                return transposes

            pend_el, pend_tp = None, None
            emit_S(units[0])
            emit_S(units[1])
            for i, u in enumerate(units):
                if i + 2 < len(units):
                    emit_S(units[i + 2])
                emit_PV(u, i % 4)
                hh, qb, kt = u
                if kt == 4 * qb + 3:
                    if pend_tp is not None:
                        pend_tp()
                        pend_tp = None
                    if pend_el is not None:
                        pend_tp = epilogue(*pend_el)
                    pend_el = (hh, qb)
            if pend_tp is not None:
                pend_tp()
            epilogue(*pend_el)()

    def wout_phase(l):
        for hlf in range(2):
            dma("pool", Wout[:, hlf * 4:(hlf + 1) * 4, :],
                wout_d[l][hlf * 512:(hlf + 1) * 512, :].rearrange("(c p) n -> p c n", p=128), [("Wout", hlf)], "Wout%d" % hlf)
        for oc in range(8):
            for tb in range(NB):
                sl = slice(tb * 512, (tb + 1) * 512)
                b = nb()
                for c in range(8):
                    if c < 2:
                        src, rk = gmT[:, c, sl], [("gmT", tb * 4 + j) for j in range(4)]
                    elif c < 4:
                        src, rk = hgT[:, c - 2, sl], [("hgT", c - 2, tb)]
                    else:
                        src, rk = actT[:, c, sl], [AK(c, tb)]
                    mm(ps[b][:, :], Wout[:, c, oc * 128:(oc + 1) * 128], src, c == 0, c == 7, rk + [("Wout", c // 4)], [("ps", b)])
                tt(xT[:, oc, sl], xT[:, oc, sl], ps[b][:, :], ALU.add, [XK(oc, tb), ("ps", b)], [XK(oc, tb)])

    def ffn(l):
        NG = 8

        def load(g):
            bf = g % 2
            dma("pool", W1g[bf][:, :, :], wff1_d[l][:, g * 512:(g + 1) * 512].rearrange("(c p) n -> p c n", p=128),
                [("W1g", bf)], "W1g%d" % bf)
            dma("pool", W2g[bf][:, :, :], wff2_d[l][g * 512:(g + 1) * 512, :].rearrange("(j p) n -> p j n", p=128),
                [("W2g", bf)], "W2g%d" % bf)

        def up(g):
            bf = g % 2
            for j in range(4):
                for tb in range(NB):
                    sl = slice(tb * 512, (tb + 1) * 512)
                    b = nb()
                    for c in range(8):
                        mm(ps[b][:, :], W1g[bf][:, c, j * 128:(j + 1) * 128], actT[:, c, sl], c == 0, c == 7,
                           [AK(c, tb), ("W1g", bf)], [("ps", b)])
                    act(relu_a[:, :], ps[b][:, :], AF.Relu, [("ps", b)], ["t1"])
                    act(hT[bf][:, j, sl], relu_a[:, :], AF.Square, ["t1"], [("hT", bf, j, tb)])

        def down(g):
            bf = g % 2
            for oc in range(8):
                for tb in range(NB):
                    sl = slice(tb * 512, (tb + 1) * 512)
                    b = nb()
                    for j in range(4):
                        mm(ps[b][:, :], W2g[bf][:, j, oc * 128:(oc + 1) * 128], hT[bf][:, j, sl], j == 0, j == 3,
                           [("hT", bf, j, tb), ("W2g", bf)], [("ps", b)])
                    tt(xT[:, oc, sl], xT[:, oc, sl], ps[b][:, :], ALU.add, [XK(oc, tb), ("ps", b)], [XK(oc, tb)])

        load(0)
        load(1)
        up(0)
        for g in range(NG):
            if g + 1 < NG:
                up(g + 1)
            down(g)
            if g + 2 < NG:
                load(g + 2)

    try:
        stage("setup")
        for l in range(n_layers):
            layer(l, False)
    except _Stop:
        pass

    for c in range(8):
        dma("sp", outT_d[c * 128:(c + 1) * 128, :], xT[:, c, :], [("out", c)], "o%d" % c,
            reads=[XK(c, tb) for tb in range(NB)])
    P.add("sp", None, reads=[("out", c) for c in range(8)] + ([("dbg", c) for c in range(8)] if debug else []))
    P.emit(nc)
    for al in nc.allocations:
        for ml in (getattr(al, "memorylocations", None) or []):
            if str(ml.type).endswith("SB") and ml.addr >= SB0 and not ml.name.startswith("const-") and ml.name not in _mine and ml.name.rsplit("_", 1)[0] not in _mine:
                raise RuntimeError("unexpected SBUF allocation %s @%d" % (ml.name, ml.addr))
            if str(ml.type).endswith("SB") and ml.name.startswith("const-") and ml.addr >= SB0:
                raise RuntimeError("late const allocation %s @%d overlaps manual map" % (ml.name, ml.addr))
    return nc


def _host_inputs(inp):
    f = lambda a: np.ascontiguousarray(np.asarray(a, dtype=np.float32))
    rep = lambda v: np.ascontiguousarray(np.broadcast_to(np.asarray(v, np.float32)[:, None, :], (L, 128, v.shape[-1])))
    half = 16
    invf = (10000.0 ** (-np.arange(half, dtype=np.float32) / half)).astype(np.float32)
    wukv = np.asarray(inp["mla_w_ukv"], np.float32).reshape(L, 128, 8, 2, 64)
    wukv_p = np.concatenate([wukv[:, :, :, 0, :].reshape(L, 128, 512), wukv[:, :, :, 1, :].reshape(L, 128, 512)], axis=-1)
    kg = np.asarray(inp["mla_k_gain"], np.float32)
    kgn = np.zeros((L, 128, 1), np.float32)
    kgn[:, 0:64, 0] = kg[:, 0:64]
    kgn[:, 64:128, 0] = kg[:, 0:64]
    shared = {
        "invf": np.ascontiguousarray(np.broadcast_to(invf[None, :], (128, 16))),
        "ident": np.eye(128, dtype=np.float32),
        "maskT": np.triu(np.ones((128, 128), np.float32)),
        "g1T": f(np.asarray(inp["norm1_gain"]).reshape(L, 8, 128).transpose(0, 2, 1)),
        "g2T": f(np.asarray(inp["norm2_gain"]).reshape(L, 8, 128).transpose(0, 2, 1)),
        "w_in": f(inp["w_in"]),
        "gmvg": rep(inp["gm_v_gain"]),
        "gmog": rep(inp["gm_out_gain"]),
        "hgog": rep(inp["hg_out_gain"]),
        "mlaog": rep(inp["mla_out_gain"]),
        "gmwT": f(np.asarray(inp["gm_w_s"]).transpose(0, 3, 1, 2).reshape(L, 128, 512)),
        "gmbT": f(np.asarray(inp["gm_b_s"]).transpose(0, 2, 1)),
        "hglbT": f(np.asarray(inp["hg_lower_bound"]).reshape(L, 2, 128).transpose(2, 1, 0).reshape(128, 2 * L)),
        "qagT": f(np.asarray(inp["mla_q_a_gain"]).reshape(L, 2, 128).transpose(0, 2, 1)),
        "kvagT": f(np.asarray(inp["mla_kv_a_gain"]).reshape(L, 1, 128).transpose(0, 2, 1)),
        "wuq": f(inp["mla_w_uq"]),
        "wukv": f(wukv_p),
        "qg": rep(inp["mla_q_gain"]),
        "kgn": kgn,
        "kgpe": rep(kg[:, 64:96]),
        "w_out": f(inp["w_out"]),
        "w_ff1": f(inp["w_ff1"]),
        "w_ff2": f(inp["w_ff2"]),
    }
    return shared


def kernel(**inputs):
    x = np.asarray(inputs["x"], np.float32)
    pos = np.asarray(inputs["positions"], np.int32)
    B = x.shape[0]
    shared = _host_inputs(inputs)
    in_maps = []
    for b in range(B):
        m = dict(shared)
        m["xT"] = np.ascontiguousarray(x[b].T)
        m["pos"] = np.ascontiguousarray(pos[b].reshape(NT, 128).T)
        in_maps.append(m)
    nc = build()
    res = run_bass_kernel_spmd(nc, in_maps, core_ids=list(range(B)))
    out = np.stack([np.asarray(res.results[b]["outT"], np.float32).T for b in range(B)], axis=0)
    return np.ascontiguousarray(out)
```
